# Optimizing a Trainium2 kernel written in Bass

```python
import math
import jax, jax.numpy as jnp
from jax import lax
import numpy as np

D_MODEL = 4096
BATCH = 2
SEQ = 8192
DEPTH = 2

CHUNK = 64
N_A = DEPTH // 2
N_B = DEPTH - N_A
D_FF = 11008
CONV_WIDTH = 31
HEAD_DIM = 128
N_HEADS = D_MODEL // HEAD_DIM
Q_BLOCK = 128
NORM_EPS = 1e-6

kernel_name = "yoco_conformer_conv_fox_hybrid"


def rms_norm(x, g):
    xf = x.astype(jnp.float32)
    y = xf * lax.rsqrt(jnp.mean(xf * xf, axis=-1, keepdims=True) + NORM_EPS)
    return (y * g.astype(jnp.float32)).astype(x.dtype)


def layer_norm(x, g, b):
    xf = x.astype(jnp.float32)
    mu = jnp.mean(xf, axis=-1, keepdims=True)
    var = jnp.mean(jnp.square(xf - mu), axis=-1, keepdims=True)
    y = (xf - mu) * lax.rsqrt(var + NORM_EPS)
    return (y * g.astype(jnp.float32) + b.astype(jnp.float32)).astype(x.dtype)


def swiglu(h, w1, w3, w2):
    return (jax.nn.silu(h @ w1) * (h @ w3)) @ w2


def conformer_conv(h, pw1_w, pw1_b, dw_w, dw_b, ln_g, ln_b, pw2_w, pw2_b):
    a, gate = jnp.split(h @ pw1_w + pw1_b, 2, axis=-1)
    u = a * jax.nn.sigmoid(gate)
    u = lax.conv_general_dilated(
        u, dw_w[:, None, :].astype(u.dtype),
        window_strides=(1,), padding=[(CONV_WIDTH - 1, 0)],
        dimension_numbers=("NWC", "WIO", "NWC"),
        feature_group_count=D_MODEL) + dw_b
    u = jax.nn.silu(layer_norm(u, ln_g, ln_b))
    return u @ pw2_w + pw2_b


def shared_kv(y, kv_norm, w_kvf, b_f):
    B, S, _ = y.shape
    proj = rms_norm(y, kv_norm) @ w_kvf
    k = proj[..., :D_MODEL].reshape(B, S, N_HEADS, HEAD_DIM).transpose(0, 2, 1, 3)
    v = proj[..., D_MODEL:2 * D_MODEL].reshape(B, S, N_HEADS, HEAD_DIM).transpose(0, 2, 1, 3)
    f_logit = (proj[..., 2 * D_MODEL:] + b_f).astype(jnp.float32)
    c = jnp.cumsum(jax.nn.log_sigmoid(f_logit), axis=1)
    return k, v, c.transpose(0, 2, 1)


def forgetting_attention(h, w_q, w_o, k, v, c):
    B, S, _ = h.shape
    n_blk = S // Q_BLOCK
    scale = 1.0 / math.sqrt(HEAD_DIM)
    q = (h @ w_q).reshape(B, n_blk, Q_BLOCK, N_HEADS, HEAD_DIM).transpose(1, 0, 3, 2, 4)
    cq = c.reshape(B, N_HEADS, n_blk, Q_BLOCK).transpose(2, 0, 1, 3)
    key_pos = jnp.arange(S)

    def one_block(args):
        qi, cqi, i = args
        s = jnp.einsum("bhqd,bhkd->bhqk", qi, k,
                       preferred_element_type=jnp.float32) * scale
        s = s + cqi[..., None] - c[:, :, None, :]
        q_pos = i * Q_BLOCK + jnp.arange(Q_BLOCK)
        mask = key_pos[None, :] <= q_pos[:, None]
        p = jax.nn.softmax(jnp.where(mask, s, -jnp.inf), axis=-1)
        return jnp.einsum("bhqk,bhkd->bhqd", p.astype(v.dtype), v)

    o = lax.map(one_block, (q, cq, jnp.arange(n_blk)))
    o = o.transpose(1, 0, 3, 2, 4).reshape(B, S, D_MODEL)
    return o @ w_o


def setup_inputs(seed: int = 0) -> dict:
    key = jax.random.key(seed)
    ks = jax.random.split(key, 32)
    D, F, H, W = D_MODEL, D_FF, N_HEADS, CONV_WIDTH
    nrm = lambda k, shape, fan_in: jax.random.normal(k, shape, jnp.float32) * (fan_in ** -0.5)
    gain = lambda k, shape: 1.0 + 0.05 * jax.random.normal(k, shape, jnp.float32)
    small = lambda k, shape: 0.02 * jax.random.normal(k, shape, jnp.float32)
    return {
        "x": jax.random.normal(ks[0], (BATCH, SEQ, D), jnp.float32),
        "norm_ffn1": gain(ks[1], (DEPTH, D)),
        "ffn1_w1": nrm(ks[2], (DEPTH, D, F), D),
        "ffn1_w3": nrm(ks[3], (DEPTH, D, F), D),
        "ffn1_w2": nrm(ks[4], (DEPTH, F, D), F),
        "norm_mix": gain(ks[5], (DEPTH, D)),
        "norm_ffn2": gain(ks[6], (DEPTH, D)),
        "ffn2_w1": nrm(ks[7], (DEPTH, D, F), D),
        "ffn2_w3": nrm(ks[8], (DEPTH, D, F), D),
        "ffn2_w2": nrm(ks[9], (DEPTH, F, D), F),
        "conv_pw1_w": nrm(ks[10], (N_A, D, 2 * D), D),
        "conv_pw1_b": small(ks[11], (N_A, 2 * D)),
        "conv_dw_w": nrm(ks[12], (N_A, W, D), W),
        "conv_dw_b": small(ks[13], (N_A, D)),
        "conv_ln_g": gain(ks[14], (N_A, D)),
        "conv_ln_b": small(ks[15], (N_A, D)),
        "conv_pw2_w": nrm(ks[16], (N_A, D, D), D),
        "conv_pw2_b": small(ks[17], (N_A, D)),
        "kv_norm": gain(ks[18], (D,)),
        "w_kvf": nrm(ks[19], (D, 2 * D + H), D),
        "b_f": 1.0 + 0.5 * jax.random.normal(ks[20], (H,), jnp.float32),
        "attn_wq": nrm(ks[21], (N_B, D, D), D),
        "attn_wo": nrm(ks[22], (N_B, D, D), D),
        "final_norm": gain(ks[23], (D,)),
    }


def reference(x, norm_ffn1, ffn1_w1, ffn1_w3, ffn1_w2, norm_mix, norm_ffn2,
              ffn2_w1, ffn2_w3, ffn2_w2, conv_pw1_w, conv_pw1_b, conv_dw_w,
              conv_dw_b, conv_ln_g, conv_ln_b, conv_pw2_w, conv_pw2_b,
              kv_norm, w_kvf, b_f, attn_wq, attn_wo, final_norm):
    h = x
    kv = None
    for layer in range(DEPTH):
        h = h + 0.5 * swiglu(rms_norm(h, norm_ffn1[layer]),
                             ffn1_w1[layer], ffn1_w3[layer], ffn1_w2[layer])
        hn = rms_norm(h, norm_mix[layer])
        if layer < N_A:
            i = layer
            h = h + conformer_conv(hn, conv_pw1_w[i], conv_pw1_b[i], conv_dw_w[i],
                                   conv_dw_b[i], conv_ln_g[i], conv_ln_b[i],
                                   conv_pw2_w[i], conv_pw2_b[i])
        else:
            j = layer - N_A
            k, v, c = kv
            h = h + forgetting_attention(hn, attn_wq[j], attn_wo[j], k, v, c)
        h = h + 0.5 * swiglu(rms_norm(h, norm_ffn2[layer]),
                             ffn2_w1[layer], ffn2_w3[layer], ffn2_w2[layer])
        if layer == N_A - 1:
            kv = shared_kv(h, kv_norm, w_kvf, b_f)
    return rms_norm(h, final_norm)
```

```python
import types
import numpy as np
import ml_dtypes
import concourse.bass as bass
import concourse.mybir as mybir
from concourse.bass_utils import run_bass_kernel_spmd

F32 = mybir.dt.float32
BF16 = mybir.dt.bfloat16
ALU = mybir.AluOpType
AF = mybir.ActivationFunctionType

D = FF = NCH = NF = NFP = H = T = NT = TOK = TOKH = NJMAX = GV = IG = KBC = NP1 = NCP = NVG = NVC = KR = 0
PARTS = []
O_NF1 = O_NMIX = O_NF2 = O_KVN = O_FIN = O_PW1B = O_DWB = O_LNG = O_LNB = O_PW2B = O_DWW = O_BF = O_META = NS = 0
CW = 31
HALO = 32
EPS = 1e-6
NEG = -30000.0


def _ceil8(n):
    return (n + 7) // 8 * 8


def set_cfg(d=4096, ff=11008, tok=2048, t=416, njmax=18):
    g = globals()
    nch = d // 128
    nf = ff // 128
    nloc = _ceil8(nf) // 8
    parts = [(r * nloc, min((r + 1) * nloc, nf)) for r in range(8) if r * nloc < nf]
    g.update(D=d, FF=ff, NCH=nch, NF=nf, NFP=_ceil8(nf), H=nch, T=t, TOK=tok, TOKH=tok + HALO,
             NT=(tok + HALO) // t, NJMAX=max(b - a for a, b in parts), PARTS=parts, GV=min(512, d),
             IG=min(4, nch), KBC=tok // 128, NP1=_ceil8(2 * nch), NCP=_ceil8(nch), NVG=d // min(512, d),
             NVC=min(4, nch), KR=min(256, d))
    assert (tok + HALO) % t == 0 and tok % 128 == 0 and ff % 128 == 0 and nch % 2 == 0
    g.update(O_NF1=0, O_NMIX=2 * nch, O_NF2=4 * nch, O_KVN=6 * nch, O_FIN=7 * nch, O_PW1B=8 * nch,
             O_DWB=10 * nch, O_LNG=11 * nch, O_LNB=12 * nch, O_PW2B=13 * nch, O_DWW=14 * nch,
             O_BF=45 * nch, O_META=45 * nch + 1, NS=45 * nch + 1 + 16)


set_cfg()

STAGE_LIMIT = None
ARENA_B = 110592


def _freeze(fn):
    if getattr(fn, "__closure__", None) is None:
        return fn
    cells = []
    for c in fn.__closure__:
        try:
            cells.append(types.CellType(c.cell_contents))
        except ValueError:
            cells.append(c)
    g = types.FunctionType(fn.__code__, fn.__globals__, fn.__name__, fn.__defaults__, tuple(cells))
    g.__kwdefaults__ = fn.__kwdefaults__
    return g


class Res:
    __slots__ = ("name", "w", "r")

    def __init__(self, name):
        self.name = name
        self.w = None
        self.r = {}


class DSem:
    def __init__(self, nc, name):
        self.sem = nc.alloc_semaphore(name)
        self.val = 0


class Prog:
    ENG = ("pe", "act", "dve", "pool", "sp")

    def __init__(self, nc):
        self.nc = nc
        self.q = {e: [] for e in self.ENG}
        self.sem = {e: nc.alloc_semaphore("prog_" + e) for e in ("pe", "act", "dve", "pool")}
        self.cnt = {e: 0 for e in self.sem}
        self.waited = {e: {} for e in self.ENG}
        self.semobj = {}
        self.dsems = []
        self.n_inst = 0

    def dsem(self, name):
        if name in self.semobj:
            return self.semobj[name]
        d = DSem(self.nc, name)
        self.semobj[name] = d
        self.dsems.append(d)
        return d

    def _wait(self, e, tok):
        if tok is None:
            return
        sem, val = tok
        if e == "pe" and sem is self.sem["pe"]:
            return
        key = id(sem)
        if self.waited[e].get(key, 0) >= val:
            return
        self.waited[e][key] = val
        self.q[e].append(lambda h, sem=sem, val=val: h.wait_ge(sem, val))

    def _deps(self, e, reads, writes):
        for r in reads:
            self._wait(e, r.w)
        for w in writes:
            self._wait(e, w.w)
            for sem, val in w.r.values():
                self._wait(e, (sem, val))

    def _mark(self, tok, reads, writes):
        k = id(tok[0])
        for r in reads:
            r.r[k] = tok
        for w in writes:
            w.w = tok
            w.r = {}

    def op(self, e, fn, reads=(), writes=()):
        fn = _freeze(fn)
        self._deps(e, reads, writes)
        self.cnt[e] += 1
        sem = self.sem[e]
        tok = (sem, self.cnt[e])
        self.q[e].append(lambda h, fn=fn, sem=sem: fn(h).then_inc(sem, 1))
        self._mark(tok, reads, writes)
        self.n_inst += 1

    def mm(self, ps, mms, reads, start=True, stop=True):
        self._deps("pe", reads, [ps])
        self.cnt["pe"] += 1
        sem = self.sem["pe"]
        tok = (sem, self.cnt["pe"])
        n = len(mms)

        def thunk(h, mms=mms, n=n, sem=sem, start=start, stop=stop):
            for k, (o, l, r) in enumerate(mms):
                ins = h.matmul(o, lhsT=l, rhs=r, start=(start and k == 0), stop=(stop and k == n - 1))
            ins.then_inc(sem, 1)
        self.q["pe"].append(thunk)
        self._mark(tok, reads, [ps])
        self.n_inst += n

    def tr(self, ps, out, in_, ident, reads):
        self._deps("pe", reads, [ps])
        self.cnt["pe"] += 1
        sem = self.sem["pe"]
        tok = (sem, self.cnt["pe"])
        self.q["pe"].append(lambda h, sem=sem: h.transpose(out, in_, ident).then_inc(sem, 1))
        self._mark(tok, reads, [ps])
        self.n_inst += 1

    def dma(self, e, dsem, fn, reads=(), writes=()):
        fn = _freeze(fn)
        self._wait(e, (dsem.sem, dsem.val))
        self._deps(e, reads, writes)
        dsem.val += 16
        tok = (dsem.sem, dsem.val)
        s = dsem.sem
        self.q[e].append(lambda h, fn=fn, s=s: fn(h).then_inc(s, 16))
        self._mark(tok, reads, writes)

    def cc(self, dsem, fn, reads=(), writes=()):
        fn = _freeze(fn)
        self._wait("pool", (dsem.sem, dsem.val))
        self._deps("pool", reads, writes)
        dsem.val += 1
        tok = (dsem.sem, dsem.val)
        s = dsem.sem
        self.q["pool"].append(lambda h, fn=fn, s=s: fn(h).then_inc(s, 1))
        self._mark(tok, reads, writes)

    def barrier(self):
        toks = [(self.sem[e], self.cnt[e]) for e in self.sem if self.cnt[e] > 0]
        toks += [(d.sem, d.val) for d in self.dsems if d.val > 0 and not getattr(d, "nobar", False)]
        for e in self.ENG:
            for t in toks:
                self._wait(e, t)

    def emit(self, final_tok):
        nc = self.nc
        q = self.q
        with nc.Block() as block:
            @block.tensor
            def _(h):
                for f in q["pe"]:
                    f(h)

            @block.scalar
            def _(h):
                for f in q["act"]:
                    f(h)

            @block.vector
            def _(h):
                for f in q["dve"]:
                    f(h)

            @block.gpsimd
            def _(h):
                for f in q["pool"]:
                    f(h)

            @block.sync
            def _(h):
                for f in q["sp"]:
                    f(h)


def build_nc(stop_after=None):
    nc = bass.Bass("TRN2", target_bir_lowering=False)
    P = Prog(nc)

    def din(name, shape, dt=F32):
        return nc.dram_tensor(name, list(shape), dt, kind="ExternalInput").ap()

    xin = din("xin", [D, TOKH])
    small_in = din("small", [128, NS])
    wf_in = din("wf", [128, NCH * H])
    wspec = []

    def wdecl(name, rows, cols):
        ap = din(name, [rows, cols])
        nloc = rows // 128
        wb = nc.dram_tensor(name + "_b", [rows, cols], BF16).ap()
        wg = nc.dram_tensor(name + "_g", [8 * rows, cols], BF16).ap()
        w4 = nc.dram_tensor(name + "_q", [4 * rows, cols], BF16).ap()
        r = {"name": name, "in": ap, "b": wb, "g": wg, "q": w4, "rb": Res(name + "_b"), "rg": [],
             "rows": rows, "cols": cols, "nloc": nloc}

        def off(j, nloc=nloc):
            rk, jl = j // nloc, j % nloc
            return ((jl * 4 + rk % 4) * 2 + rk // 4) * 128
        r["off"] = off
        wspec.append(r)
        return r

    Wt = {}
    for l in range(2):
        for f in (1, 2):
            Wt[f"w1_{l}{f}"] = wdecl(f"w1_{l}{f}", NFP // 8 * 128, D)
            Wt[f"w3_{l}{f}"] = wdecl(f"w3_{l}{f}", NFP // 8 * 128, D)
            Wt[f"w2_{l}{f}"] = wdecl(f"w2_{l}{f}", NFP // 8 * 128, D)
    Wt["pw1"] = wdecl("pw1", NP1 // 8 * 128, D)
    Wt["pw2"] = wdecl("pw2", NCP // 8 * 128, D)
    Wt["wk"] = wdecl("wk", NCP // 8 * 128, D)
    Wt["wv"] = wdecl("wv", NVC * 128, (NCH // NVC) * GV)
    Wt["wq"] = wdecl("wq", NCP // 8 * 128, D)
    Wt["wo"] = wdecl("wo", NCP // 8 * 128, D)
    worder = ["w1_01", "w3_01", "w2_01", "pw1", "pw2", "w1_02", "w3_02", "w2_02", "wk", "wv",
              "w1_11", "w3_11", "w2_11", "wq", "wo", "w1_12", "w3_12", "w2_12"]

    outT = nc.dram_tensor("outT", [D, TOK], F32, kind="ExternalOutput").ap()

    XS = nc.dram_tensor("xs", [NT, 128, NCH * T], F32).ap()
    Kp = nc.dram_tensor("kp", [D, TOK], BF16).ap()
    Vp = nc.dram_tensor("vp", [TOK, D], BF16).ap()
    LSp = nc.dram_tensor("lsp", [H, TOK], F32).ap()
    NKC = D // KR
    Kgp = nc.dram_tensor("kgp", [NKC * 7 * KR, TOK], BF16).ap()
    Vgp = nc.dram_tensor("vgp", [KBC * 7 * 128, D], BF16).ap()
    LSgp = nc.dram_tensor("lsgp", [7 * H, TOK], F32).ap()
    Cd = nc.dram_tensor("cd", [H, 4 * TOK], F32).ap()
    Kw = nc.dram_tensor("kw", [4 * D, TOK], BF16).ap()
    Vw = nc.dram_tensor("vw", [4 * TOK, D], BF16).ap()
    rKw, rVw = Res("kw"), Res("vw")
    rXS = [Res(f"xs{i}") for i in range(NT)]
    rKp, rVp, rLSp = Res("kp"), Res("vp"), Res("lsp")
    rKgp, rVgp, rLSgp, rCd = Res("kgp"), Res("vgp"), Res("lsgp"), Res("cd")
    rKpad, rVpad, rLSpad = Res("kpad"), Res("vpad"), Res("lspad")

    Xt = nc.alloc_sbuf_tensor("X", [128, NCH * T], F32)
    HNt = nc.alloc_sbuf_tensor("HN", [128, NCH * T], BF16)
    SM = nc.alloc_sbuf_tensor("SM", [128, NS], F32)
    ONES32 = nc.alloc_sbuf_tensor("ones32", [128, 128], F32)
    ONESB = nc.alloc_sbuf_tensor("onesb", [128, 128], BF16)
    ID32 = nc.alloc_sbuf_tensor("id32", [128, 128], F32)
    TMPt = [nc.alloc_sbuf_tensor(f"tmp{i}", [128, T], F32) for i in range(3)]
    RSTDt = nc.alloc_sbuf_tensor("rstd", [128, T], F32)
    CARRYt = nc.alloc_sbuf_tensor("carry", [128, NCH * 30], F32)
    FWt = nc.alloc_sbuf_tensor("fw", [128, NCH * H], BF16)
    rFW = Res("fw")
    AR = nc.alloc_sbuf_tensor("arena", [128, ARENA_B // 2], BF16)
    PS = [nc.alloc_psum_tensor(f"ps{i}", [128, 512], F32) for i in range(8)]
    rPS = [Res(f"ps{i}") for i in range(8)]

    Xv = Xt[:, :].rearrange("p (c t) -> p c t", t=T)
    HNv = HNt[:, :].rearrange("p (c t) -> p c t", t=T)
    CARRYv = CARRYt[:, :].rearrange("p (c t) -> p c t", t=30)
    rX, rHN, rSM, rRSTD, rCARRY, rCONST = Res("X"), Res("HN"), Res("SM"), Res("RSTD"), Res("CARRY"), Res("CONST")
    rTMP = [Res(f"tmp{i}") for i in range(3)]

    def ar_bf(off, n):
        return AR[:, off // 2: off // 2 + n]

    def ar_f32(off, n):
        return AR[:, off // 2: off // 2 + 2 * n].bitcast(F32)

    def smc(col, p0=0, p1=128):
        return SM[p0:p1, col:col + 1]

    class Slots:
        def __init__(self, base, n, halves, half_elems, tag):
            self.n = n
            self.ap = [[ar_bf(base + (s * halves + hh) * half_elems * 2, half_elems) for hh in range(halves)]
                       for s in range(n)]
            self.res = [[Res(f"{tag}{s}_{hh}") for hh in range(halves)] for s in range(n)]
            self.ds = [[P.dsem(f"d_{tag}{s}_{hh}") for hh in range(halves)] for s in range(n)]

    sem_misc = {"sp": [P.dsem(f"d_misc{i}") for i in range(6)], "pool": [P.dsem(f"d_miscp{i}") for i in range(3)]}
    misc_i = [0]

    def misc_sem(e="sp"):
        misc_i[0] += 1
        return sem_misc[e][misc_i[0] % len(sem_misc[e])]

    P.op("pool", lambda h: h.memset(ONES32[:, :], 1.0), writes=[rCONST])
    P.op("pool", lambda h: h.memset(ONESB[:, :], 1.0), writes=[rCONST])
    P.op("pool", lambda h: h.memset(ID32[:, :], 1.0), writes=[rCONST])
    P.op("pool", lambda h: h.affine_select(out=ID32[:, :], in_=ID32[:, :], pattern=[[1, 128]],
                                           compare_op=ALU.is_equal, fill=0.0, base=0, channel_multiplier=-1),
         reads=[rCONST], writes=[rCONST])
    P.dma("sp", misc_sem(), lambda h: h.dma_start(out=SM[:, :], in_=small_in[:, :]), writes=[rSM])
    P.dma("pool", misc_sem("pool"), lambda h: h.dma_start(out=FWt[:, :], in_=wf_in[:, :]), writes=[rFW])
    rAZ = Res("az")
    ZN = max(3 * KR * TOK // 128, 3 * D, 2 * TOK)
    ZB = ar_bf(0, ZN)
    P.op("pool", lambda h: h.memset(ZB, 0.0), writes=[rAZ])
    for k in range(NKC):
        P.dma("pool", misc_sem("pool"), lambda h, k=k: h.dma_start(
            out=Kgp[k * 7 * KR: (k * 7 + 3) * KR, :].rearrange("(p a) t -> p (a t)", p=128),
            in_=ar_bf(0, 3 * KR * TOK // 128)[:, :]), reads=[rAZ], writes=[rKpad])
    for k in range(KBC):
        P.dma("pool", misc_sem("pool"), lambda h, k=k: h.dma_start(
            out=Vgp[k * 7 * 128: (k * 7 + 3) * 128, :].rearrange("(p a) d -> p (a d)", p=128),
            in_=ar_bf(0, 3 * D)[:, :]), reads=[rAZ], writes=[rVpad])
    P.dma("pool", misc_sem("pool"), lambda h: h.dma_start(
        out=LSgp[0:3 * H, :], in_=ar_f32(0, TOK)[0:3 * H, :]), reads=[rAZ], writes=[rLSpad])

    ccsems = [P.dsem(f"d_cc{i}") for i in range(8)]
    for c_ in ccsems:
        c_.nobar = True
    cci = [0]

    def next_cc():
        cci[0] += 1
        return ccsems[cci[0] % len(ccsems)]

    def cc_group(kind_groups, in_ap, out_ap, reads, toks):
        ds_ = next_cc()
        r = Res("cc")
        P.cc(ds_, lambda h: h.collective_compute("AllGather", ALU.bypass, replica_groups=kind_groups,
                                                 ins=[in_ap], outs=[out_ap]), reads=reads, writes=[r])
        toks[id(ds_.sem)] = r.w
        return r

    def toks_to_res(toks):
        out = []
        for t in toks.values():
            r = Res("ccsum")
            r.w = t
            out.append(r)
        return out

    castsem = [P.dsem("d_cast0"), P.dsem("d_cast1")]
    for cs_ in castsem:
        cs_.nobar = True
    GRP4 = [[0, 1, 2, 3], [4, 5, 6, 7]]
    PAIRS = [[0, 4], [1, 5], [2, 6], [3, 7]]
    for wi, name in enumerate(worder):
        w = Wt[name]
        rows = w["rows"]
        nsp = 4 if rows >= 512 else 1
        rr = rows // nsp
        for k in range(nsp):
            P.dma("pool", castsem[(wi * 4 + k) % 2], lambda h, w=w, k=k, rr=rr: h.dma_start(
                out=w["b"][k * rr:(k + 1) * rr, :], in_=w["in"][k * rr:(k + 1) * rr, :]), writes=[w["rb"]])
        for cs in castsem:
            P._wait("pool", (cs.sem, cs.val))
        toks = {}
        for jl in range(w["nloc"]):
            rq = cc_group(GRP4, w["b"][jl * 128:(jl + 1) * 128, :], w["q"][jl * 512:(jl + 1) * 512, :], [w["rb"]], {})
            for r4 in range(4):
                cc_group(PAIRS, w["q"][(jl * 4 + r4) * 128:(jl * 4 + r4 + 1) * 128, :],
                         w["g"][(jl * 4 + r4) * 256:(jl * 4 + r4 + 1) * 256, :], [rq], toks)
        w["rg"] = toks_to_res(toks)

    def load_x(i):
        P.dma("sp", misc_sem(), lambda h: h.dma_start(
            out=Xv, in_=xin.rearrange("(c p) t -> p c t", p=128)[:, :, i * T:(i + 1) * T]), writes=[rX])

    def rmsnorm(gcol, to_out=None):
        for c in range(NCH):
            tb = c % 2
            P.op("act", lambda h, c=c, tb=tb: h.activation(out=TMPt[tb][:, :], in_=Xv[:, c, :], func=AF.Square),
                 reads=[rX], writes=[rTMP[tb]])
            P.mm(rPS[7], [(PS[7][:, 0:T], ONES32[:, :], TMPt[tb][:, :])], reads=[rTMP[tb], rCONST],
                 start=(c == 0), stop=(c == NCH - 1))
        P.op("act", lambda h: h.activation(out=TMPt[2][:, :], in_=PS[7][:, 0:T], func=AF.Sqrt, bias=EPS, scale=1.0 / D),
             reads=[rPS[7]], writes=[rTMP[2]])
        P.op("dve", lambda h: h.reciprocal(out=RSTDt[:, :], in_=TMPt[2][:, :]), reads=[rTMP[2]], writes=[rRSTD])
        for c in range(NCH):
            if to_out is None:
                P.op("dve", lambda h, c=c: h.scalar_tensor_tensor(
                    out=HNv[:, c, :], in0=Xv[:, c, :], scalar=smc(gcol + c), in1=RSTDt[:, :],
                    op0=ALU.mult, op1=ALU.mult), reads=[rX, rRSTD, rSM], writes=[rHN])
            else:
                ov, rov = to_out
                P.op("dve", lambda h, c=c, ov=ov: h.scalar_tensor_tensor(
                    out=ov[:, c, :], in0=Xv[:, c, :], scalar=smc(gcol + c), in1=RSTDt[:, :],
                    op0=ALU.mult, op1=ALU.mult), reads=[rX, rRSTD, rSM], writes=[rov])

    def stream(items, nslots, load, consume):
        n = len(items)
        depth = nslots - 1
        for k in range(min(depth, n)):
            load(k, items[k], k % nslots)
        for k in range(n):
            if k + depth < n:
                load(k + depth, items[k + depth], (k + depth) % nslots)
            consume(k, items[k], k % nslots)

    def ffn(l, f, gcol):
        P.barrier()
        WA = Slots(0, 3, 2, D, "wa")
        ACT_OFF = 12 * D
        ACTv = ar_bf(ACT_OFF, NJMAX * T).rearrange("p (j t) -> p j t", t=T)
        rACT = Res("actb")
        GW = IG * 128
        NIG = NCH // IG
        W2S = Slots(ACT_OFF + NJMAX * T * 2, 2, 1, NJMAX * GW, "w2s")
        w1, w3, w2 = Wt[f"w1_{l}{f}"], Wt[f"w3_{l}{f}"], Wt[f"w2_{l}{f}"]
        rmsnorm(gcol)

        def load13(k, j, s):
            P.dma("sp", WA.ds[s][0], lambda h: h.dma_start(out=WA.ap[s][0], in_=w1["g"][w1["off"](j):w1["off"](j) + 128, :]),
                  reads=w1["rg"], writes=[WA.res[s][0]])
            P.dma("sp", WA.ds[s][1], lambda h: h.dma_start(out=WA.ap[s][1], in_=w3["g"][w3["off"](j):w3["off"](j) + 128, :]),
                  reads=w3["rg"], writes=[WA.res[s][1]])

        w2items = [(q, ig) for q in range(len(PARTS)) for ig in range(NIG)]

        def load2(k, it, s):
            q, ig = it
            j0, j1 = PARTS[q]
            nj = j1 - j0
            dst = W2S.ap[s][0][:, 0:nj * GW].rearrange("p (j d) -> p j d", d=GW)
            rk = j0 // w2["nloc"]
            x = (rk % 4) * 2 + rk // 4
            src = w2["g"].rearrange("(jl x p) d -> x p jl d", x=8, p=128)[x, :, 0:nj, ig * GW:(ig + 1) * GW]
            P.dma("sp", W2S.ds[s][0], lambda h: h.dma_start(out=dst, in_=src), reads=w2["rg"], writes=[W2S.res[s][0]])

        st13 = {"k": 0}
        st2 = {"k": 0}
        n13 = NF
        for k in range(min(2, n13)):
            load13(k, k, k % 3)
        st13["k"] = min(2, n13)
        load2(0, w2items[0], 0)
        st2["k"] = 1
        jglob = 0
        for q, (j0, j1) in enumerate(PARTS):
            nj = j1 - j0
            for j in range(j0, j1):
                if st13["k"] < n13:
                    load13(st13["k"], st13["k"], st13["k"] % 3)
                    st13["k"] += 1
                s = j % 3
                pb = (j % 2) * 2
                wa, wb = WA.ap[s][0], WA.ap[s][1]
                P.mm(rPS[pb], [(PS[pb][:, 0:T], wa[:, c * 128:(c + 1) * 128], HNv[:, c, :]) for c in range(NCH)],
                     reads=[rHN, WA.res[s][0]])
                P.mm(rPS[pb + 1], [(PS[pb + 1][:, 0:T], wb[:, c * 128:(c + 1) * 128], HNv[:, c, :]) for c in range(NCH)],
                     reads=[rHN, WA.res[s][1]])
                tb = j % 2
                P.op("act", lambda h, pb=pb, tb=tb: h.activation(out=TMPt[tb][:, :], in_=PS[pb][:, 0:T], func=AF.Silu),
                     reads=[rPS[pb]], writes=[rTMP[tb]])
                P.op("dve", lambda h, pb=pb, tb=tb, jl=j - j0: h.tensor_tensor(
                    out=ACTv[:, jl, :], in0=TMPt[tb][:, :], in1=PS[pb + 1][:, 0:T], op=ALU.mult),
                    reads=[rTMP[tb], rPS[pb + 1]], writes=[rACT])
            for ig in range(NIG):
                k2 = q * NIG + ig
                if st2["k"] < len(w2items):
                    load2(st2["k"], w2items[st2["k"]], st2["k"] % 2)
                    st2["k"] += 1
                s2 = k2 % 2
                w2v = W2S.ap[s2][0][:, 0:nj * GW].rearrange("p (j d) -> p j d", d=GW)
                for ii in range(IG):
                    i = ig * IG + ii
                    pb = 4 + ii
                    P.mm(rPS[pb], [(PS[pb][:, 0:T], w2v[:, jl, ii * 128:(ii + 1) * 128], ACTv[:, jl, :]) for jl in range(nj)],
                         reads=[rACT, W2S.res[s2][0]])
                    P.op("dve", lambda h, pb=pb, i=i: h.scalar_tensor_tensor(
                        out=Xv[:, i, :], in0=PS[pb][:, 0:T], scalar=0.5, in1=Xv[:, i, :], op0=ALU.mult, op1=ALU.add),
                        reads=[rPS[pb], rX], writes=[rX])

    def proj_resid(w, bias_col):
        P.barrier()
        WA = Slots(NCH * T * 2, 3, 2, D, "wa")
        items = list(range(NCH // 2))

        def load(k, m, s):
            for hh in range(2):
                j = 2 * m + hh
                P.dma("sp", WA.ds[s][hh], lambda h, hh=hh, j=j: h.dma_start(
                    out=WA.ap[s][hh], in_=w["g"][w["off"](j):w["off"](j) + 128, :]), reads=w["rg"], writes=[WA.res[s][hh]])

        def consume(k, m, s):
            for hh in range(2):
                i = 2 * m + hh
                pb = i % 4
                wa = WA.ap[s][hh]
                P.mm(rPS[pb], [(PS[pb][:, 0:T], wa[:, c * 128:(c + 1) * 128], HNv[:, c, :]) for c in range(NCH)],
                     reads=[rHN, WA.res[s][hh]])
                if bias_col is None:
                    P.op("dve", lambda h, pb=pb, i=i: h.tensor_tensor(
                        out=Xv[:, i, :], in0=PS[pb][:, 0:T], in1=Xv[:, i, :], op=ALU.add),
                        reads=[rPS[pb], rX], writes=[rX])
                else:
                    P.op("dve", lambda h, pb=pb, i=i: h.scalar_tensor_tensor(
                        out=Xv[:, i, :], in0=PS[pb][:, 0:T], scalar=smc(bias_col + i), in1=Xv[:, i, :],
                        op0=ALU.add, op1=ALU.add), reads=[rPS[pb], rX, rSM], writes=[rX])
        stream(items, 3, load, consume)

    def conv_mixer(i):
        P.barrier()
        WA = Slots(0, 3, 2, D, "wa")
        UOFF = 12 * D
        UW = 30 + T
        Uv = ar_f32(UOFF, NCH * UW).rearrange("p (c t) -> p c t", t=UW)
        rU = [Res(f"u{c}") for c in range(NCH)]
        w = Wt["pw1"]
        rmsnorm(O_NMIX)
        if i == 0:
            P.op("dve", lambda h: h.memset(Uv[:, :, 0:30], 0.0), writes=rU)
        else:
            P.op("dve", lambda h: h.tensor_copy(out=Uv[:, :, 0:30], in_=CARRYv), reads=[rCARRY], writes=rU)

        def load(k, m, s):
            for hh in range(2):
                j = m + NCH * hh
                P.dma("sp", WA.ds[s][hh], lambda h, hh=hh, j=j: h.dma_start(
                    out=WA.ap[s][hh], in_=w["g"][w["off"](j):w["off"](j) + 128, :]), reads=w["rg"], writes=[WA.res[s][hh]])

        def consume(k, m, s):
            pb = (m % 2) * 2
            tb = m % 2
            for hh in range(2):
                wa = WA.ap[s][hh]
                P.mm(rPS[pb + hh], [(PS[pb + hh][:, 0:T], wa[:, c * 128:(c + 1) * 128], HNv[:, c, :]) for c in range(NCH)],
                     reads=[rHN, WA.res[s][hh]])
            P.op("act", lambda h: h.activation(out=TMPt[tb][:, :], in_=PS[pb + 1][:, 0:T], func=AF.Sigmoid,
                                               bias=smc(O_PW1B + NCH + m), scale=1.0),
                 reads=[rPS[pb + 1], rSM], writes=[rTMP[tb]])
            P.op("dve", lambda h: h.scalar_tensor_tensor(
                out=Uv[:, m, 30:UW], in0=PS[pb][:, 0:T], scalar=smc(O_PW1B + m), in1=TMPt[tb][:, :],
                op0=ALU.add, op1=ALU.mult), reads=[rPS[pb], rTMP[tb], rSM], writes=[rU[m]])
        stream(list(range(NCH)), 3, load, consume)
        if i == 0:
            P.op("dve", lambda h: h.tensor_scalar_mul(out=Uv[:, :, 30:30 + HALO], in0=Uv[:, :, 30:30 + HALO],
                                                      scalar1=smc(O_META)), reads=rU + [rSM], writes=rU)
        P.op("dve", lambda h: h.tensor_copy(out=CARRYv, in_=Uv[:, :, T:UW]), reads=rU, writes=[rCARRY])
        for c in range(NCH):
            e = "dve"
            tb = 1 if e == "pool" else 0
            acc = TMPt[tb]
            P.op(e, lambda h, c=c, acc=acc: h.tensor_scalar_mul(
                out=acc[:, :], in0=Uv[:, c, 0:T], scalar1=smc(O_DWW + c * CW)), reads=[rU[c], rSM], writes=[rTMP[tb]])
            for k in range(1, CW):
                P.op(e, lambda h, c=c, k=k, acc=acc: h.scalar_tensor_tensor(
                    out=acc[:, :], in0=Uv[:, c, k:k + T], scalar=smc(O_DWW + c * CW + k), in1=acc[:, :],
                    op0=ALU.mult, op1=ALU.add), reads=[rU[c], rSM, rTMP[tb]], writes=[rTMP[tb]])
            P.op(e, lambda h, c=c, acc=acc: h.tensor_scalar_add(
                out=Uv[:, c, 0:T], in0=acc[:, :], scalar1=smc(O_DWB + c)), reads=[rTMP[tb], rSM], writes=[rU[c]])
        for c in range(NCH):
            P.mm(rPS[6], [(PS[6][:, 0:T], ONES32[:, :], Uv[:, c, 0:T])], reads=[rU[c], rCONST],
                 start=(c == 0), stop=(c == NCH - 1))
            tb = c % 2
            P.op("act", lambda h, c=c, tb=tb: h.activation(out=TMPt[tb][:, :], in_=Uv[:, c, 0:T], func=AF.Square),
                 reads=[rU[c]], writes=[rTMP[tb]])
            P.mm(rPS[7], [(PS[7][:, 0:T], ONES32[:, :], TMPt[tb][:, :])], reads=[rTMP[tb], rCONST],
                 start=(c == 0), stop=(c == NCH - 1))
        MU = TMPt[2]
        rMU = rTMP[2]
        P.op("dve", lambda h: h.tensor_scalar_mul(out=MU[:, :], in0=PS[6][:, 0:T], scalar1=1.0 / D),
             reads=[rPS[6]], writes=[rMU])
        P.op("dve", lambda h: h.tensor_tensor(out=TMPt[0][:, :], in0=MU[:, :], in1=MU[:, :], op=ALU.mult),
             reads=[rMU], writes=[rTMP[0]])
        P.op("dve", lambda h: h.scalar_tensor_tensor(out=TMPt[0][:, :], in0=PS[7][:, 0:T], scalar=1.0 / D,
                                                     in1=TMPt[0][:, :], op0=ALU.mult, op1=ALU.subtract),
             reads=[rPS[7], rTMP[0]], writes=[rTMP[0]])
        P.op("act", lambda h: h.activation(out=TMPt[0][:, :], in_=TMPt[0][:, :], func=AF.Sqrt, bias=EPS, scale=1.0),
             reads=[rTMP[0]], writes=[rTMP[0]])
        P.op("dve", lambda h: h.reciprocal(out=RSTDt[:, :], in_=TMPt[0][:, :]), reads=[rTMP[0]], writes=[rRSTD])
        for c in range(NCH):
            tb = c % 2
            P.op("dve", lambda h, c=c, tb=tb: h.tensor_tensor(out=TMPt[tb][:, :], in0=Uv[:, c, 0:T], in1=MU[:, :],
                                                              op=ALU.subtract), reads=[rU[c], rMU], writes=[rTMP[tb]])
            P.op("dve", lambda h, tb=tb: h.tensor_tensor(out=TMPt[tb][:, :], in0=TMPt[tb][:, :], in1=RSTDt[:, :],
                                                         op=ALU.mult), reads=[rTMP[tb], rRSTD], writes=[rTMP[tb]])
            P.op("act", lambda h, c=c, tb=tb: h.activation(out=HNv[:, c, :], in_=TMPt[tb][:, :], func=AF.Silu,
                                                           bias=smc(O_LNB + c), scale=smc(O_LNG + c)),
                 reads=[rTMP[tb], rSM], writes=[rHN])
        proj_resid(Wt["pw2"], O_PW2B)

    def real_cols(i):
        c0 = HALO if i == 0 else 0
        r0 = i * T - HALO + c0
        return c0, r0, T - c0

    def kv_stage(i):
        P.barrier()
        c0, r0, n = real_cols(i)
        WA = Slots(0, 3, 2, D, "wa")
        KST = ar_bf(12 * D, NCH * T).rearrange("p (c t) -> p c t", t=T)
        rKST = Res("kst")
        FW = FWt[:, :].rearrange("p (c m) -> p c m", m=H)
        w = Wt["wk"]
        rmsnorm(O_KVN)

        def load(k, m, s):
            for hh in range(2):
                j = 2 * m + hh
                P.dma("sp", WA.ds[s][hh], lambda h, hh=hh, j=j: h.dma_start(
                    out=WA.ap[s][hh], in_=w["g"][w["off"](j):w["off"](j) + 128, :]), reads=w["rg"], writes=[WA.res[s][hh]])

        def consume(k, m, s):
            for hh in range(2):
                hd = 2 * m + hh
                pb = hd % 4
                wa = WA.ap[s][hh]
                P.mm(rPS[pb], [(PS[pb][:, 0:T], wa[:, c * 128:(c + 1) * 128], HNv[:, c, :]) for c in range(NCH)],
                     reads=[rHN, WA.res[s][hh]])
                e = "act" if hd % 2 == 0 else "dve"
                if e == "act":
                    P.op("act", lambda h, pb=pb, hd=hd: h.copy(out=KST[:, hd, :], in_=PS[pb][:, 0:T]),
                         reads=[rPS[pb]], writes=[rKST])
                else:
                    P.op("dve", lambda h, pb=pb, hd=hd: h.tensor_copy(out=KST[:, hd, :], in_=PS[pb][:, 0:T]),
                         reads=[rPS[pb]], writes=[rKST])
        stream(list(range(NCH // 2)), 3, load, consume)
        P.dma("sp", misc_sem(), lambda h: h.dma_start(
            out=Kp.rearrange("(c p) t -> p c t", p=128)[:, :, r0:r0 + n], in_=KST[:, :, c0:c0 + n]),
            reads=[rKST], writes=[rKp])
        P.mm(rPS[4], [(PS[4][0:H, 0:T], FW[:, c, :], HNv[:, c, :]) for c in range(NCH)], reads=[rHN, rFW])
        XF, A_, M_ = TMPt[0], TMPt[1], TMPt[2]
        P.op("dve", lambda h: h.tensor_scalar_add(out=XF[0:H, :], in0=PS[4][0:H, 0:T], scalar1=smc(O_BF, 0, H)),
             reads=[rPS[4], rSM], writes=[rTMP[0]])
        P.op("act", lambda h: h.activation(out=A_[0:H, :], in_=XF[0:H, :], func=AF.Abs),
             reads=[rTMP[0]], writes=[rTMP[1]])
        P.op("act", lambda h: h.activation(out=A_[0:H, :], in_=A_[0:H, :], func=AF.Exp, scale=-1.0),
             reads=[rTMP[1]], writes=[rTMP[1]])
        P.op("act", lambda h: h.activation(out=A_[0:H, :], in_=A_[0:H, :], func=AF.Ln, bias=1.0, scale=1.0),
             reads=[rTMP[1]], writes=[rTMP[1]])
        P.op("dve", lambda h: h.tensor_scalar_min(out=M_[0:H, :], in0=XF[0:H, :], scalar1=0.0),
             reads=[rTMP[0]], writes=[rTMP[2]])
        P.op("dve", lambda h: h.tensor_tensor(out=M_[0:H, :], in0=M_[0:H, :], in1=A_[0:H, :], op=ALU.subtract),
             reads=[rTMP[2], rTMP[1]], writes=[rTMP[2]])
        P.dma("sp", misc_sem(), lambda h: h.dma_start(out=LSp[:, r0:r0 + n], in_=M_[0:H, c0:c0 + n]),
              reads=[rTMP[2]], writes=[rLSp])
        P.barrier()
        NPC = (T + 127) // 128
        VST = ar_bf(0, NPC * D).rearrange("p (k d) -> p k d", d=D)
        rVST = Res("vst")
        WV = Slots(NPC * D * 2, 2, NVC, (NCH // NVC) * GV, "wv")
        wv = Wt["wv"]
        pieces = []
        a = c0
        while a < T:
            m = min(128, T - a)
            pieces.append((a, m))
            a += m

        def loadv(k, vg, s):
            cpg = NCH // NVC
            for cq in range(NVC):
                o_ = wv["off"](vg * NVC + cq)
                P.dma("sp", WV.ds[s][cq], lambda h, cq=cq, o_=o_: h.dma_start(
                    out=WV.ap[s][cq], in_=wv["g"][o_:o_ + 128, :]), reads=wv["rg"], writes=[WV.res[s][cq]])

        def consv(k, vg, s):
            cpg = NCH // NVC
            wvq = [WV.ap[s][cq].rearrange("p (c d) -> p c d", d=GV) for cq in range(NVC)]
            for pi, (a, m) in enumerate(pieces):
                pb = pi % 4
                P.mm(rPS[pb], [(PS[pb][0:m, 0:GV], HNv[:, c, a:a + m], wvq[c // cpg][:, c % cpg, :]) for c in range(NCH)],
                     reads=[rHN] + WV.res[s])
                if pi % 2 == 0:
                    P.op("act", lambda h, pb=pb, pi=pi, m=m: h.copy(out=VST[0:m, pi, vg * GV:(vg + 1) * GV],
                                                                    in_=PS[pb][0:m, 0:GV]), reads=[rPS[pb]], writes=[rVST])
                else:
                    P.op("dve", lambda h, pb=pb, pi=pi, m=m: h.tensor_copy(out=VST[0:m, pi, vg * GV:(vg + 1) * GV],
                                                                           in_=PS[pb][0:m, 0:GV]), reads=[rPS[pb]], writes=[rVST])
        stream(list(range(NVG)), 2, loadv, consv)
        for pi, (a, m) in enumerate(pieces):
            ra = r0 + (a - c0)
            P.dma("sp", misc_sem(), lambda h, pi=pi, m=m, ra=ra: h.dma_start(out=Vp[ra:ra + m, :], in_=VST[0:m, pi, :]),
                  reads=[rVST], writes=[rVp])

    def spill_x(i):
        P.dma("sp", misc_sem(), lambda h: h.dma_start(out=XS[i], in_=Xt[:, :]), reads=[rX], writes=[rXS[i]])

    def reload_x(i):
        P.dma("sp", misc_sem(), lambda h: h.dma_start(out=Xt[:, :], in_=XS[i]), reads=[rXS[i]], writes=[rX])

    NEGC_OFF = ARENA_B - 4 * KBC * H * 4
    regcache = {}

    def getj(h):
        if "j" not in regcache:
            regcache["j"] = h.snap(h.partition_id() % 4, min_val=0, max_val=3)
        return regcache["j"]
    NEGCv = ar_f32(NEGC_OFF, 4 * KBC * H).rearrange("p (k h) -> p k h", h=H)
    rNEGC = Res("negc")

    def exchange_and_prep():
        P.barrier()
        tk, tv, tl = {}, {}, {}
        for k in range(NKC):
            cc_group(GRP4, Kp[k * KR:(k + 1) * KR, :], Kgp[(k * 7 + 3) * KR:(k * 7 + 7) * KR, :], [rKp, rKpad], tk)
        for k in range(KBC):
            cc_group(GRP4, Vp[k * 128:(k + 1) * 128, :], Vgp[(k * 7 + 3) * 128:(k * 7 + 7) * 128, :], [rVp, rVpad], tv)
        cc_group(GRP4, LSp[:, :], LSgp[3 * H:7 * H, :], [rLSp, rLSpad], tl)
        rKg_, rVg_, rLg_ = toks_to_res(tk), toks_to_res(tv), toks_to_res(tl)
        Kgx = Kgp.rearrange("(k x r) t -> x k r t", x=7, r=KR)
        Vgx = Vgp.rearrange("(k x r) d -> x k r d", x=7, r=128)
        for s in range(4):
            def fkw(h, s=s):
                j = getj(h)
                return h.dma_start(out=Kw[s * D:(s + 1) * D, :].rearrange("(k r) t -> k r t", r=KR),
                                   in_=Kgx[bass.ds(j + s, 1)][0])
            P.dma("sp", misc_sem(), fkw, reads=rKg_, writes=[rKw])

            def fvw(h, s=s):
                j = getj(h)
                return h.dma_start(out=Vw[s * TOK:(s + 1) * TOK, :].rearrange("(k r) d -> k r d", r=128),
                                   in_=Vgx[bass.ds(j + s, 1)][0])
            P.dma("sp", misc_sem(), fvw, reads=rVg_, writes=[rVw])
        LSW = ar_f32(0, 4 * TOK)
        CREL = ar_f32(16 * TOK, 4 * TOK)
        ONE = ar_f32(32 * TOK, TOK)
        rLSW, rCREL, rONE = Res("lsw"), Res("crel"), Res("one")
        P.op("pool", lambda h: h.memset(ONE[0:H, :], 1.0), writes=[rONE])
        for s in range(4):
            def f(h, s=s):
                j = getj(h)
                return h.dma_start(out=LSW[0:H, s * TOK:(s + 1) * TOK], in_=LSgp[bass.ds((j + s) * H, H), :])
            P.dma("sp", misc_sem(), f, reads=rLg_, writes=[rLSW])
        for s in range(4):
            P.op("dve", lambda h, s=s: h.tensor_scalar_mul(out=LSW[0:H, s * TOK:(s + 1) * TOK],
                                                           in0=LSW[0:H, s * TOK:(s + 1) * TOK],
                                                           scalar1=smc(O_META + 1 + s, 0, H)),
                 reads=[rLSW, rSM], writes=[rLSW])
        for s in range(4):
            init = 0.0 if s == 0 else CREL[0:H, s * TOK - 1:s * TOK]
            P.op("dve", lambda h, s=s, init=init: h.tensor_tensor_scan(
                out=CREL[0:H, s * TOK:(s + 1) * TOK], data0=ONE[0:H, :], data1=LSW[0:H, s * TOK:(s + 1) * TOK],
                initial=init, op0=ALU.mult, op1=ALU.add), reads=[rLSW, rONE, rCREL], writes=[rCREL])
        P.dma("sp", misc_sem(), lambda h: h.dma_start(out=Cd[:, :], in_=CREL[0:H, :]), reads=[rCREL], writes=[rCd])
        for s in range(4):
            pb = s
            for kk in range(KBC):
                kb = s * KBC + kk
                P.tr(rPS[pb], PS[pb][:, kk * H:(kk + 1) * H], CREL[0:H, kb * 128:(kb + 1) * 128], ID32[0:H, 0:H],
                     reads=[rCREL, rCONST])
            P.op("dve", lambda h, s=s, pb=pb: h.tensor_scalar(
                out=NEGCv[:, s * KBC:(s + 1) * KBC, :], in0=PS[pb][:, 0:KBC * H].rearrange("p (k h) -> p k h", h=H),
                scalar1=-1.0, scalar2=smc(O_META + 5 + s), op0=ALU.mult, op1=ALU.add),
                reads=[rPS[pb], rSM], writes=[rNEGC])

    def attention(i):
        P.barrier()
        rmsnorm(O_NMIX + NCH)
        QT = ar_bf(0, NCH * T).rearrange("p (c t) -> p c t", t=T)
        rQT = Res("qt")
        WA = Slots(NCH * T * 2, 3, 2, D, "wa")
        w = Wt["wq"]

        def load(k, m, s):
            for hh in range(2):
                j = 2 * m + hh
                P.dma("sp", WA.ds[s][hh], lambda h, hh=hh, j=j: h.dma_start(
                    out=WA.ap[s][hh], in_=w["g"][w["off"](j):w["off"](j) + 128, :]), reads=w["rg"], writes=[WA.res[s][hh]])

        def consume(k, m, s):
            for hh in range(2):
                hd = 2 * m + hh
                pb = hd % 4
                wa = WA.ap[s][hh]
                P.mm(rPS[pb], [(PS[pb][:, 0:T], wa[:, c * 128:(c + 1) * 128], HNv[:, c, :]) for c in range(NCH)],
                     reads=[rHN, WA.res[s][hh]])
                if hd % 2 == 0:
                    P.op("act", lambda h, pb=pb, hd=hd: h.copy(out=QT[:, hd, :], in_=PS[pb][:, 0:T]),
                         reads=[rPS[pb]], writes=[rQT])
                else:
                    P.op("dve", lambda h, pb=pb, hd=hd: h.tensor_copy(out=QT[:, hd, :], in_=PS[pb][:, 0:T]),
                         reads=[rPS[pb]], writes=[rQT])
        stream(list(range(NCH // 2)), 3, load, consume)
        P.barrier()
        KOFF = NCH * T * 2
        KT = [ar_bf(KOFF + b * 8 * TOK, 4 * TOK) for b in range(2)]
        VV = [ar_bf(KOFF + 16 * TOK + b * 8 * TOK, 4 * TOK).rearrange("p (k d) -> p k d", d=128) for b in range(2)]
        CQ = [ar_f32(KOFF + 32 * TOK + b * 4 * T, T) for b in range(2)]
        PT = [ar_bf(KOFF + 32 * TOK + 8 * T + b * 2 * T, T) for b in range(3)]
        assert KOFF + 32 * TOK + 14 * T <= NEGC_OFF
        rKT = [[Res(f"kt{b}{s}") for s in range(4)] for b in range(2)]
        rVV = [[Res(f"vv{b}{s}") for s in range(4)] for b in range(2)]
        rCQ = [Res(f"cq{b}") for b in range(2)]
        rPT = [Res(f"pt{b}") for b in range(3)]
        dK = [[P.dsem(f"d_kt_{b}{s}") for s in range(4)] for b in range(2)]
        dV = [[P.dsem(f"d_vv_{b}{s}") for s in range(4)] for b in range(2)]
        dC = [P.dsem(f"d_cq_{b}") for b in range(2)]
        c0, r0, n = real_cols(i)
        rq0 = i * T - HALO
        nown = (rq0 + T - 1) // 128 + 1
        blocks = list(range(3 * KBC)) + [3 * KBC + kk for kk in range(nown)]
        scale = 1.0 / float(np.sqrt(128.0))

        def loadh(hd):
            b = hd % 2
            for s in range(4):
                def fk(h, s=s, b=b, hd=hd):
                    return h.dma_start(out=KT[b][:, s * TOK:(s + 1) * TOK],
                                       in_=Kw[s * D + hd * 128:s * D + (hd + 1) * 128, :])
                P.dma("sp", dK[b][s], fk, reads=[rKw], writes=[rKT[b][s]])

                def fv(h, s=s, b=b, hd=hd):
                    return h.dma_start(out=VV[b][:, s * KBC:(s + 1) * KBC, :],
                                       in_=Vw[s * TOK:(s + 1) * TOK, hd * 128:(hd + 1) * 128]
                                       .rearrange("(k p) d -> p k d", p=128))
                P.dma("sp", dV[b][s], fv, reads=[rVw], writes=[rVV[b][s]])
            P.dma("sp", dC[b], lambda h, b=b, hd=hd: h.dma_start(
                out=CQ[b], in_=Cd[hd, 3 * TOK + rq0:3 * TOK + rq0 + T].partition_broadcast(128)),
                reads=[rCd], writes=[rCQ[b]])

        loadh(0)
        gi = 0
        for hd in range(H):
            if hd + 1 < H:
                loadh(hd + 1)
            b = hd % 2
            po, pl = 2 + b, 4 + b
            nb = len(blocks)

            def qk(bi):
                kb = blocks[bi]
                sp = bi % 2
                P.mm(rPS[sp], [(PS[sp][:, 0:T], KT[b][:, kb * 128:(kb + 1) * 128], QT[:, hd, :])],
                     reads=[rKT[b][kb // KBC], rQT])
            qk(0)
            for bi, kb in enumerate(blocks):
                if bi + 1 < nb:
                    qk(bi + 1)
                sp = bi % 2
                tb = bi % 2
                pt = gi % 3
                gi += 1
                P.op("dve", lambda h, sp=sp, tb=tb: h.scalar_tensor_tensor(
                    out=TMPt[tb][:, :], in0=PS[sp][:, 0:T], scalar=scale, in1=CQ[b], op0=ALU.mult, op1=ALU.add),
                    reads=[rPS[sp], rCQ[b]], writes=[rTMP[tb]])
                if kb >= 3 * KBC:
                    kl = kb - 3 * KBC
                    if kl * 128 + 127 > rq0:
                        def fsel(h, tb=tb, kl=kl):
                            if "neg" not in regcache:
                                regcache["neg"] = h.to_reg(NEG)
                            return h.affine_select(
                                out=TMPt[tb][:, :], in_=TMPt[tb][:, :], pattern=[[1, T]], compare_op=ALU.is_ge,
                                fill=regcache["neg"], base=rq0 - 128 * kl, channel_multiplier=-1)
                        P.op("pool", fsel, reads=[rTMP[tb]], writes=[rTMP[tb]])
                P.op("act", lambda h, tb=tb, pt=pt, kb=kb: h.activation(
                    out=PT[pt], in_=TMPt[tb][:, :], func=AF.Exp, bias=NEGCv[:, kb, hd:hd + 1], scale=1.0),
                    reads=[rTMP[tb], rNEGC], writes=[rPT[pt]])
                P.mm(rPS[po], [(PS[po][:, 0:T], VV[b][:, kb, :], PT[pt])], reads=[rVV[b][kb // KBC], rPT[pt]],
                     start=(bi == 0), stop=(bi == nb - 1))
                P.mm(rPS[pl], [(PS[pl][:, 0:T], ONESB[:, :], PT[pt])], reads=[rPT[pt], rCONST],
                     start=(bi == 0), stop=(bi == nb - 1))
            P.op("dve", lambda h, pl=pl: h.tensor_scalar_max(out=RSTDt[:, :], in0=PS[pl][:, 0:T], scalar1=1e-30),
                 reads=[rPS[pl]], writes=[rRSTD])
            P.op("dve", lambda h: h.reciprocal(out=RSTDt[:, :], in_=RSTDt[:, :]), reads=[rRSTD], writes=[rRSTD])
            P.op("dve", lambda h, po=po, hd=hd: h.tensor_tensor(out=HNv[:, hd, :], in0=PS[po][:, 0:T], in1=RSTDt[:, :],
                                                                op=ALU.mult), reads=[rPS[po], rRSTD], writes=[rHN])
        proj_resid(Wt["wo"], None)

    def final_out(i):
        P.barrier()
        c0, r0, n = real_cols(i)
        OUTB = ar_f32(0, NCH * T).rearrange("p (c t) -> p c t", t=T)
        rOUT = Res("outb")
        rmsnorm(O_FIN, to_out=(OUTB, rOUT))
        P.dma("sp", misc_sem(), lambda h: h.dma_start(
            out=outT.rearrange("(c p) t -> p c t", p=128)[:, :, r0:r0 + n], in_=OUTB[:, :, c0:c0 + n]),
            reads=[rOUT], writes=[Res("outdram")])

    stage_ctr = [0]

    def run(fn, *a):
        stage_ctr[0] += 1
        if STAGE_LIMIT is not None and stage_ctr[0] > STAGE_LIMIT:
            return
        fn(*a)

    stages1 = [(ffn, (0, 1, O_NF1), False), (conv_mixer, (), True), (ffn, (0, 2, O_NF2), False),
               (kv_stage, (), True), (ffn, (1, 1, O_NF1 + NCH), False)]
    for si, (fn, args, per_tile) in enumerate(stages1):
        for i in range(NT):
            run(P.barrier)
            if si == 0:
                run(load_x, i)
            else:
                run(reload_x, i)
            if per_tile:
                run(fn, i)
            else:
                run(fn, *args)
            run(spill_x, i)
    run(exchange_and_prep)
    for i in range(NT):
        run(P.barrier)
        run(reload_x, i)
        run(attention, i)
        run(ffn, 1, 2, O_NF2 + NCH)
        run(final_out, i)
    P.barrier()
    P.emit(None)
    print("kernel build: n_inst", P.n_inst, {e: len(v) for e, v in P.q.items()}, flush=True)
    return nc


def _chunk_layout(W, nchp):
    n = W.shape[1] // 128
    t = W.reshape(NCH, 128, n, 128).transpose(2, 1, 0, 3).reshape(n * 128, D)
    if nchp > n:
        t = np.concatenate([t, np.zeros(((nchp - n) * 128, D), np.float32)], axis=0)
    return np.ascontiguousarray(t)


def _col(v):
    return np.ascontiguousarray(np.asarray(v, np.float32).reshape(-1, 128).T)


_NC_CACHE = {}


def make_in_maps(x, norm_ffn1, ffn1_w1, ffn1_w3, ffn1_w2, norm_mix, norm_ffn2, ffn2_w1, ffn2_w3, ffn2_w2,
                 conv_pw1_w, conv_pw1_b, conv_dw_w, conv_dw_b, conv_ln_g, conv_ln_b, conv_pw2_w, conv_pw2_b,
                 kv_norm, w_kvf, b_f, attn_wq, attn_wo, final_norm):
    x = np.asarray(x, np.float32)
    full = {}
    for l in range(2):
        for f, (a, b, c) in ((1, (ffn1_w1, ffn1_w3, ffn1_w2)), (2, (ffn2_w1, ffn2_w3, ffn2_w2))):
            full[f"w1_{l}{f}"] = _chunk_layout(np.asarray(a[l], np.float32), NFP)
            full[f"w3_{l}{f}"] = _chunk_layout(np.asarray(b[l], np.float32), NFP)
            w2 = np.asarray(c[l], np.float32)
            full[f"w2_{l}{f}"] = np.concatenate([w2, np.zeros((NFP * 128 - FF, D), np.float32)], axis=0)
    full["pw1"] = _chunk_layout(np.asarray(conv_pw1_w[0], np.float32), NP1)
    full["pw2"] = _chunk_layout(np.asarray(conv_pw2_w[0], np.float32), NCP)
    wkvf = np.asarray(w_kvf, np.float32)
    full["wk"] = _chunk_layout(np.ascontiguousarray(wkvf[:, 0:D]), NCP)
    wv = wkvf[:, D:2 * D]
    cpg = NCH // NVC
    wvl = wv.reshape(NVC, cpg, 128, NVG, GV).transpose(3, 0, 2, 1, 4).reshape(NVG * NVC * 128, cpg * GV)
    if NVG < 8:
        wvl = np.concatenate([wvl, np.zeros(((8 - NVG) * NVC * 128, cpg * GV), np.float32)], axis=0)
    full["wv"] = np.ascontiguousarray(wvl)
    full["wq"] = _chunk_layout(np.asarray(attn_wq[0], np.float32), NCP)
    full["wo"] = _chunk_layout(np.asarray(attn_wo[0], np.float32), NCP)
    wf = np.ascontiguousarray(wkvf[:, 2 * D:2 * D + H].reshape(NCH, 128, H).transpose(1, 0, 2).reshape(128, NCH * H))

    small = np.zeros((128, NS), np.float32)
    n = NCH
    small[:, O_NF1:O_NF1 + n] = _col(norm_ffn1[0]); small[:, O_NF1 + n:O_NF1 + 2 * n] = _col(norm_ffn1[1])
    small[:, O_NMIX:O_NMIX + n] = _col(norm_mix[0]); small[:, O_NMIX + n:O_NMIX + 2 * n] = _col(norm_mix[1])
    small[:, O_NF2:O_NF2 + n] = _col(norm_ffn2[0]); small[:, O_NF2 + n:O_NF2 + 2 * n] = _col(norm_ffn2[1])
    small[:, O_KVN:O_KVN + n] = _col(kv_norm)
    small[:, O_FIN:O_FIN + n] = _col(final_norm)
    small[:, O_PW1B:O_PW1B + 2 * n] = _col(conv_pw1_b[0])
    small[:, O_DWB:O_DWB + n] = _col(conv_dw_b[0])
    small[:, O_LNG:O_LNG + n] = _col(conv_ln_g[0])
    small[:, O_LNB:O_LNB + n] = _col(conv_ln_b[0])
    small[:, O_PW2B:O_PW2B + n] = _col(conv_pw2_b[0])
    dww = np.asarray(conv_dw_w[0], np.float32)
    small[:, O_DWW:O_DWW + NCH * CW] = dww.T.reshape(NCH, 128, CW).transpose(1, 0, 2).reshape(128, NCH * CW)
    small[0:H, O_BF] = np.asarray(b_f, np.float32)

    in_maps = []
    for r in range(8):
        b, j = r // 4, r % 4
        st = j * TOK
        xs = np.zeros((TOKH, D), np.float32)
        if j > 0:
            xs[:] = x[b, st - HALO:st + TOK]
        else:
            xs[HALO:] = x[b, 0:TOK]
        sm = small.copy()
        sm[:, O_META] = 0.0 if j == 0 else 1.0
        for s in range(4):
            valid = (j + s - 3) >= 0
            sm[:, O_META + 1 + s] = 1.0 if valid else 0.0
            sm[:, O_META + 5 + s] = 0.0 if valid else NEG
        m = {"xin": np.ascontiguousarray(xs.T), "small": sm, "wf": wf}
        for name, arr in full.items():
            rows = arr.shape[0] // 8
            m[name] = arr[r * rows:(r + 1) * rows]
        in_maps.append(m)
    return in_maps


def assemble(results):
    out = np.empty((2, 4 * TOK, D), np.float32)
    for r in range(8):
        b, j = r // 4, r % 4
        out[b, j * TOK:(j + 1) * TOK, :] = np.asarray(results[r]["outT"], np.float32).T
    return out


def kernel(**inputs):
    set_cfg()
    in_maps = make_in_maps(**inputs)
    if "nc" not in _NC_CACHE:
        _NC_CACHE["nc"] = build_nc()
    res = run_bass_kernel_spmd(_NC_CACHE["nc"], in_maps, core_ids=list(range(8)))
    return assemble(res.results)
```

```python
import types
import numpy as np
import ml_dtypes
import concourse.bass as bass
import concourse.mybir as mybir
from concourse.bass_utils import run_bass_kernel_spmd

F32 = mybir.dt.float32
BF16 = mybir.dt.bfloat16
ALU = mybir.AluOpType
AF = mybir.ActivationFunctionType

D = FF = NCH = NF = NFP = H = T = NT = TOK = TOKH = NJMAX = GV = IG = KBC = NP1 = NCP = NVG = NVC = KR = 0
PARTS = []
O_NF1 = O_NMIX = O_NF2 = O_KVN = O_FIN = O_PW1B = O_DWB = O_LNG = O_LNB = O_PW2B = O_DWW = O_BF = O_META = NS = 0
CW = 31
HALO = 32
EPS = 1e-6
NEG = -30000.0


def _ceil8(n):
    return (n + 7) // 8 * 8


def set_cfg(d=4096, ff=11008, tok=2048, t=416, njmax=18):
    g = globals()
    nch = d // 128
    nf = ff // 128
    nloc = _ceil8(nf) // 8
    parts = [(r * nloc, min((r + 1) * nloc, nf)) for r in range(8) if r * nloc < nf]
    g.update(D=d, FF=ff, NCH=nch, NF=nf, NFP=_ceil8(nf), H=nch, T=t, TOK=tok, TOKH=tok + HALO,
             NT=(tok + HALO) // t, NJMAX=max(b - a for a, b in parts), PARTS=parts, GV=min(512, d),
             IG=min(4, nch), KBC=tok // 128, NP1=_ceil8(2 * nch), NCP=_ceil8(nch), NVG=d // min(512, d),
             NVC=min(4, nch), KR=min(256, d))
    assert (tok + HALO) % t == 0 and tok % 128 == 0 and ff % 128 == 0 and nch % 2 == 0
    g.update(O_NF1=0, O_NMIX=2 * nch, O_NF2=4 * nch, O_KVN=6 * nch, O_FIN=7 * nch, O_PW1B=8 * nch,
             O_DWB=10 * nch, O_LNG=11 * nch, O_LNB=12 * nch, O_PW2B=13 * nch, O_DWW=14 * nch,
             O_BF=45 * nch, O_META=45 * nch + 1, NS=45 * nch + 1 + 16)


set_cfg()

STAGE_LIMIT = None
ARENA_B = 110592


def _freeze(fn):
    if getattr(fn, "__closure__", None) is None:
        return fn
    cells = []
    for c in fn.__closure__:
        try:
            cells.append(types.CellType(c.cell_contents))
        except ValueError:
            cells.append(c)
    g = types.FunctionType(fn.__code__, fn.__globals__, fn.__name__, fn.__defaults__, tuple(cells))
    g.__kwdefaults__ = fn.__kwdefaults__
    return g


class Res:
    __slots__ = ("name", "w", "r")

    def __init__(self, name):
        self.name = name
        self.w = None
        self.r = {}


class DSem:
    def __init__(self, nc, name):
        self.sem = nc.alloc_semaphore(name)
        self.val = 0


class Prog:
    ENG = ("pe", "act", "dve", "pool", "sp")

    def __init__(self, nc):
        self.nc = nc
        self.q = {e: [] for e in self.ENG}
        self.sem = {e: nc.alloc_semaphore("prog_" + e) for e in ("pe", "act", "dve", "pool")}
        self.cnt = {e: 0 for e in self.sem}
        self.waited = {e: {} for e in self.ENG}
        self.semobj = {}
        self.dsems = []
        self.n_inst = 0

    def dsem(self, name):
        if name in self.semobj:
            return self.semobj[name]
        d = DSem(self.nc, name)
        self.semobj[name] = d
        self.dsems.append(d)
        return d

    def _wait(self, e, tok):
        if tok is None:
            return
        sem, val = tok
        if e == "pe" and sem is self.sem["pe"]:
            return
        key = id(sem)
        if self.waited[e].get(key, 0) >= val:
            return
        self.waited[e][key] = val
        self.q[e].append(lambda h, sem=sem, val=val: h.wait_ge(sem, val))

    def _deps(self, e, reads, writes):
        for r in reads:
            self._wait(e, r.w)
        for w in writes:
            self._wait(e, w.w)
            for sem, val in w.r.values():
                self._wait(e, (sem, val))

    def _mark(self, tok, reads, writes):
        k = id(tok[0])
        for r in reads:
            r.r[k] = tok
        for w in writes:
            w.w = tok
            w.r = {}

    def op(self, e, fn, reads=(), writes=()):
        fn = _freeze(fn)
        self._deps(e, reads, writes)
        self.cnt[e] += 1
        sem = self.sem[e]
        tok = (sem, self.cnt[e])
        self.q[e].append(lambda h, fn=fn, sem=sem: fn(h).then_inc(sem, 1))
        self._mark(tok, reads, writes)
        self.n_inst += 1

    def mm(self, ps, mms, reads, start=True, stop=True):
        self._deps("pe", reads, [ps])
        self.cnt["pe"] += 1
        sem = self.sem["pe"]
        tok = (sem, self.cnt["pe"])
        n = len(mms)

        def thunk(h, mms=mms, n=n, sem=sem, start=start, stop=stop):
            for k, (o, l, r) in enumerate(mms):
                ins = h.matmul(o, lhsT=l, rhs=r, start=(start and k == 0), stop=(stop and k == n - 1))
            ins.then_inc(sem, 1)
        self.q["pe"].append(thunk)
        self._mark(tok, reads, [ps])
        self.n_inst += n

    def tr(self, ps, out, in_, ident, reads):
        self._deps("pe", reads, [ps])
        self.cnt["pe"] += 1
        sem = self.sem["pe"]
        tok = (sem, self.cnt["pe"])
        self.q["pe"].append(lambda h, sem=sem: h.transpose(out, in_, ident).then_inc(sem, 1))
        self._mark(tok, reads, [ps])
        self.n_inst += 1

    def dma(self, e, dsem, fn, reads=(), writes=()):
        fn = _freeze(fn)
        self._wait(e, (dsem.sem, dsem.val))
        self._deps(e, reads, writes)
        dsem.val += 16
        tok = (dsem.sem, dsem.val)
        s = dsem.sem
        self.q[e].append(lambda h, fn=fn, s=s: fn(h).then_inc(s, 16))
        self._mark(tok, reads, writes)

    def cc(self, dsem, fn, reads=(), writes=()):
        fn = _freeze(fn)
        self._wait("pool", (dsem.sem, dsem.val))
        self._deps("pool", reads, writes)
        dsem.val += 1
        tok = (dsem.sem, dsem.val)
        s = dsem.sem
        self.q["pool"].append(lambda h, fn=fn, s=s: fn(h).then_inc(s, 1))
        self._mark(tok, reads, writes)

    def barrier(self):
        toks = [(self.sem[e], self.cnt[e]) for e in self.sem if self.cnt[e] > 0]
        toks += [(d.sem, d.val) for d in self.dsems if d.val > 0 and not getattr(d, "nobar", False)]
        for e in self.ENG:
            for t in toks:
                self._wait(e, t)

    def emit(self, final_tok):
        nc = self.nc
        q = self.q
        with nc.Block() as block:
            @block.tensor
            def _(h):
                for f in q["pe"]:
                    f(h)

            @block.scalar
            def _(h):
                for f in q["act"]:
                    f(h)

            @block.vector
            def _(h):
                for f in q["dve"]:
                    f(h)

            @block.gpsimd
            def _(h):
                for f in q["pool"]:
                    f(h)

            @block.sync
            def _(h):
                for f in q["sp"]:
                    f(h)


def build_nc(stop_after=None):
    nc = bass.Bass("TRN2", target_bir_lowering=False)
    P = Prog(nc)

    def din(name, shape, dt=F32):
        return nc.dram_tensor(name, list(shape), dt, kind="ExternalInput").ap()

    xin = din("xin", [D, TOKH])
    small_in = din("small", [128, NS])
    wf_in = din("wf", [128, NCH * H])
    wspec = []

    def wdecl(name, rows, cols):
        ap = din(name, [rows, cols])
        nloc = rows // 128
        wb = nc.dram_tensor(name + "_b", [rows, cols], BF16).ap()
        wg = nc.dram_tensor(name + "_g", [8 * rows, cols], BF16).ap()
        w4 = nc.dram_tensor(name + "_q", [4 * rows, cols], BF16).ap()
        r = {"name": name, "in": ap, "b": wb, "g": wg, "q": w4, "rb": Res(name + "_b"), "rg": [],
             "rows": rows, "cols": cols, "nloc": nloc}

        def off(j, nloc=nloc):
            rk, jl = j // nloc, j % nloc
            return ((jl * 4 + rk % 4) * 2 + rk // 4) * 128
        r["off"] = off
        wspec.append(r)
        return r

    Wt = {}
    for l in range(2):
        for f in (1, 2):
            Wt[f"w1_{l}{f}"] = wdecl(f"w1_{l}{f}", NFP // 8 * 128, D)
            Wt[f"w3_{l}{f}"] = wdecl(f"w3_{l}{f}", NFP // 8 * 128, D)
            Wt[f"w2_{l}{f}"] = wdecl(f"w2_{l}{f}", NFP // 8 * 128, D)
    Wt["pw1"] = wdecl("pw1", NP1 // 8 * 128, D)
    Wt["pw2"] = wdecl("pw2", NCP // 8 * 128, D)
    Wt["wk"] = wdecl("wk", NCP // 8 * 128, D)
    Wt["wv"] = wdecl("wv", NVC * 128, (NCH // NVC) * GV)
    Wt["wq"] = wdecl("wq", NCP // 8 * 128, D)
    Wt["wo"] = wdecl("wo", NCP // 8 * 128, D)
    worder = ["w1_01", "w3_01", "w2_01", "pw1", "pw2", "w1_02", "w3_02", "w2_02", "wk", "wv",
              "w1_11", "w3_11", "w2_11", "wq", "wo", "w1_12", "w3_12", "w2_12"]

    outT = nc.dram_tensor("outT", [D, TOK], F32, kind="ExternalOutput").ap()

    XS = nc.dram_tensor("xs", [NT, 128, NCH * T], F32).ap()
    Kp = nc.dram_tensor("kp", [D, TOK], BF16).ap()
    Vp = nc.dram_tensor("vp", [TOK, D], BF16).ap()
    LSp = nc.dram_tensor("lsp", [H, TOK], F32).ap()
    NKC = D // KR
    Kgp = nc.dram_tensor("kgp", [NKC * 7 * KR, TOK], BF16).ap()
    Vgp = nc.dram_tensor("vgp", [KBC * 7 * 128, D], BF16).ap()
    LSgp = nc.dram_tensor("lsgp", [7 * H, TOK], F32).ap()
    Cd = nc.dram_tensor("cd", [H, 4 * TOK], F32).ap()
    Kw = nc.dram_tensor("kw", [4 * D, TOK], BF16).ap()
    Vw = nc.dram_tensor("vw", [4 * TOK, D], BF16).ap()
    rKw, rVw = Res("kw"), Res("vw")
    rXS = [Res(f"xs{i}") for i in range(NT)]
    rKp, rVp, rLSp = Res("kp"), Res("vp"), Res("lsp")
    rKgp, rVgp, rLSgp, rCd = Res("kgp"), Res("vgp"), Res("lsgp"), Res("cd")
    rKpad, rVpad, rLSpad = Res("kpad"), Res("vpad"), Res("lspad")

    Xt = nc.alloc_sbuf_tensor("X", [128, NCH * T], F32)
    HNt = nc.alloc_sbuf_tensor("HN", [128, NCH * T], BF16)
    SM = nc.alloc_sbuf_tensor("SM", [128, NS], F32)
    ONES32 = nc.alloc_sbuf_tensor("ones32", [128, 128], F32)
    ONESB = nc.alloc_sbuf_tensor("onesb", [128, 128], BF16)
    ID32 = nc.alloc_sbuf_tensor("id32", [128, 128], F32)
    TMPt = [nc.alloc_sbuf_tensor(f"tmp{i}", [128, T], F32) for i in range(3)]
    RSTDt = nc.alloc_sbuf_tensor("rstd", [128, T], F32)
    CARRYt = nc.alloc_sbuf_tensor("carry", [128, NCH * 30], F32)
    FWt = nc.alloc_sbuf_tensor("fw", [128, NCH * H], BF16)
    rFW = Res("fw")
    AR = nc.alloc_sbuf_tensor("arena", [128, ARENA_B // 2], BF16)
    PS = [nc.alloc_psum_tensor(f"ps{i}", [128, 512], F32) for i in range(8)]
    rPS = [Res(f"ps{i}") for i in range(8)]

    Xv = Xt[:, :].rearrange("p (c t) -> p c t", t=T)
    HNv = HNt[:, :].rearrange("p (c t) -> p c t", t=T)
    CARRYv = CARRYt[:, :].rearrange("p (c t) -> p c t", t=30)
    rX, rHN, rSM, rRSTD, rCARRY, rCONST = Res("X"), Res("HN"), Res("SM"), Res("RSTD"), Res("CARRY"), Res("CONST")
    rTMP = [Res(f"tmp{i}") for i in range(3)]

    def ar_bf(off, n):
        return AR[:, off // 2: off // 2 + n]

    def ar_f32(off, n):
        return AR[:, off // 2: off // 2 + 2 * n].bitcast(F32)

    def smc(col, p0=0, p1=128):
        return SM[p0:p1, col:col + 1]

    class Slots:
        def __init__(self, base, n, halves, half_elems, tag):
            self.n = n
            self.ap = [[ar_bf(base + (s * halves + hh) * half_elems * 2, half_elems) for hh in range(halves)]
                       for s in range(n)]
            self.res = [[Res(f"{tag}{s}_{hh}") for hh in range(halves)] for s in range(n)]
            self.ds = [[P.dsem(f"d_{tag}{s}_{hh}") for hh in range(halves)] for s in range(n)]

    sem_misc = {"sp": [P.dsem(f"d_misc{i}") for i in range(6)], "pool": [P.dsem(f"d_miscp{i}") for i in range(3)]}
    misc_i = [0]

    def misc_sem(e="sp"):
        misc_i[0] += 1
        return sem_misc[e][misc_i[0] % len(sem_misc[e])]

    P.op("pool", lambda h: h.memset(ONES32[:, :], 1.0), writes=[rCONST])
    P.op("pool", lambda h: h.memset(ONESB[:, :], 1.0), writes=[rCONST])
    P.op("pool", lambda h: h.memset(ID32[:, :], 1.0), writes=[rCONST])
    P.op("pool", lambda h: h.affine_select(out=ID32[:, :], in_=ID32[:, :], pattern=[[1, 128]],
                                           compare_op=ALU.is_equal, fill=0.0, base=0, channel_multiplier=-1),
         reads=[rCONST], writes=[rCONST])
    P.dma("sp", misc_sem(), lambda h: h.dma_start(out=SM[:, :], in_=small_in[:, :]), writes=[rSM])
    P.dma("pool", misc_sem("pool"), lambda h: h.dma_start(out=FWt[:, :], in_=wf_in[:, :]), writes=[rFW])
    rAZ = Res("az")
    ZN = max(3 * KR * TOK // 128, 3 * D, 2 * TOK)
    ZB = ar_bf(0, ZN)
    P.op("pool", lambda h: h.memset(ZB, 0.0), writes=[rAZ])
    for k in range(NKC):
        P.dma("pool", misc_sem("pool"), lambda h, k=k: h.dma_start(
            out=Kgp[k * 7 * KR: (k * 7 + 3) * KR, :].rearrange("(p a) t -> p (a t)", p=128),
            in_=ar_bf(0, 3 * KR * TOK // 128)[:, :]), reads=[rAZ], writes=[rKpad])
    for k in range(KBC):
        P.dma("pool", misc_sem("pool"), lambda h, k=k: h.dma_start(
            out=Vgp[k * 7 * 128: (k * 7 + 3) * 128, :].rearrange("(p a) d -> p (a d)", p=128),
            in_=ar_bf(0, 3 * D)[:, :]), reads=[rAZ], writes=[rVpad])
    P.dma("pool", misc_sem("pool"), lambda h: h.dma_start(
        out=LSgp[0:3 * H, :], in_=ar_f32(0, TOK)[0:3 * H, :]), reads=[rAZ], writes=[rLSpad])

    ccsems = [P.dsem(f"d_cc{i}") for i in range(8)]
    for c_ in ccsems:
        c_.nobar = True
    cci = [0]

    def next_cc():
        cci[0] += 1
        return ccsems[cci[0] % len(ccsems)]

    def cc_group(kind_groups, in_ap, out_ap, reads, toks):
        ds_ = next_cc()
        r = Res("cc")
        P.cc(ds_, lambda h: h.collective_compute("AllGather", ALU.bypass, replica_groups=kind_groups,
                                                 ins=[in_ap], outs=[out_ap]), reads=reads, writes=[r])
        toks[id(ds_.sem)] = r.w
        return r

    def toks_to_res(toks):
        out = []
        for t in toks.values():
            r = Res("ccsum")
            r.w = t
            out.append(r)
        return out

    castsem = [P.dsem("d_cast0"), P.dsem("d_cast1")]
    for cs_ in castsem:
        cs_.nobar = True
    GRP4 = [[0, 1, 2, 3], [4, 5, 6, 7]]
    PAIRS = [[0, 4], [1, 5], [2, 6], [3, 7]]
    for wi, name in enumerate(worder):
        w = Wt[name]
        rows = w["rows"]
        nsp = 4 if rows >= 512 else 1
        rr = rows // nsp
        for k in range(nsp):
            P.dma("pool", castsem[(wi * 4 + k) % 2], lambda h, w=w, k=k, rr=rr: h.dma_start(
                out=w["b"][k * rr:(k + 1) * rr, :], in_=w["in"][k * rr:(k + 1) * rr, :]), writes=[w["rb"]])
        for cs in castsem:
            P._wait("pool", (cs.sem, cs.val))
        toks = {}
        for jl in range(w["nloc"]):
            rq = cc_group(GRP4, w["b"][jl * 128:(jl + 1) * 128, :], w["q"][jl * 512:(jl + 1) * 512, :], [w["rb"]], {})
            for r4 in range(4):
                cc_group(PAIRS, w["q"][(jl * 4 + r4) * 128:(jl * 4 + r4 + 1) * 128, :],
                         w["g"][(jl * 4 + r4) * 256:(jl * 4 + r4 + 1) * 256, :], [rq], toks)
        w["rg"] = toks_to_res(toks)

    def load_x(i):
        P.dma("sp", misc_sem(), lambda h: h.dma_start(
            out=Xv, in_=xin.rearrange("(c p) t -> p c t", p=128)[:, :, i * T:(i + 1) * T]), writes=[rX])

    def rmsnorm(gcol, to_out=None):
        for c in range(NCH):
            tb = c % 2
            P.op("act", lambda h, c=c, tb=tb: h.activation(out=TMPt[tb][:, :], in_=Xv[:, c, :], func=AF.Square),
                 reads=[rX], writes=[rTMP[tb]])
            P.mm(rPS[7], [(PS[7][:, 0:T], ONES32[:, :], TMPt[tb][:, :])], reads=[rTMP[tb], rCONST],
                 start=(c == 0), stop=(c == NCH - 1))
        P.op("act", lambda h: h.activation(out=TMPt[2][:, :], in_=PS[7][:, 0:T], func=AF.Sqrt, bias=EPS, scale=1.0 / D),
             reads=[rPS[7]], writes=[rTMP[2]])
        P.op("dve", lambda h: h.reciprocal(out=RSTDt[:, :], in_=TMPt[2][:, :]), reads=[rTMP[2]], writes=[rRSTD])
        for c in range(NCH):
            if to_out is None:
                P.op("dve", lambda h, c=c: h.scalar_tensor_tensor(
                    out=HNv[:, c, :], in0=Xv[:, c, :], scalar=smc(gcol + c), in1=RSTDt[:, :],
                    op0=ALU.mult, op1=ALU.mult), reads=[rX, rRSTD, rSM], writes=[rHN])
            else:
                ov, rov = to_out
                P.op("dve", lambda h, c=c, ov=ov: h.scalar_tensor_tensor(
                    out=ov[:, c, :], in0=Xv[:, c, :], scalar=smc(gcol + c), in1=RSTDt[:, :],
                    op0=ALU.mult, op1=ALU.mult), reads=[rX, rRSTD, rSM], writes=[rov])

    def stream(items, nslots, load, consume):
        n = len(items)
        depth = nslots - 1
        for k in range(min(depth, n)):
            load(k, items[k], k % nslots)
        for k in range(n):
            if k + depth < n:
                load(k + depth, items[k + depth], (k + depth) % nslots)
            consume(k, items[k], k % nslots)

    def ffn(l, f, gcol):
        P.barrier()
        WA = Slots(0, 3, 2, D, "wa")
        ACT_OFF = 12 * D
        ACTv = ar_bf(ACT_OFF, NJMAX * T).rearrange("p (j t) -> p j t", t=T)
        rACT = Res("actb")
        GW = IG * 128
        NIG = NCH // IG
        W2S = Slots(ACT_OFF + NJMAX * T * 2, 2, 1, NJMAX * GW, "w2s")
        w1, w3, w2 = Wt[f"w1_{l}{f}"], Wt[f"w3_{l}{f}"], Wt[f"w2_{l}{f}"]
        rmsnorm(gcol)

        def load13(k, j, s):
            P.dma("sp", WA.ds[s][0], lambda h: h.dma_start(out=WA.ap[s][0], in_=w1["g"][w1["off"](j):w1["off"](j) + 128, :]),
                  reads=w1["rg"], writes=[WA.res[s][0]])
            P.dma("sp", WA.ds[s][1], lambda h: h.dma_start(out=WA.ap[s][1], in_=w3["g"][w3["off"](j):w3["off"](j) + 128, :]),
                  reads=w3["rg"], writes=[WA.res[s][1]])

        w2items = [(q, ig) for q in range(len(PARTS)) for ig in range(NIG)]

        def load2(k, it, s):
            q, ig = it
            j0, j1 = PARTS[q]
            nj = j1 - j0
            dst = W2S.ap[s][0][:, 0:nj * GW].rearrange("p (j d) -> p j d", d=GW)
            rk = j0 // w2["nloc"]
            x = (rk % 4) * 2 + rk // 4
            src = w2["g"].rearrange("(jl x p) d -> x p jl d", x=8, p=128)[x, :, 0:nj, ig * GW:(ig + 1) * GW]
            P.dma("sp", W2S.ds[s][0], lambda h: h.dma_start(out=dst, in_=src), reads=w2["rg"], writes=[W2S.res[s][0]])

        st13 = {"k": 0}
        st2 = {"k": 0}
        n13 = NF
        for k in range(min(2, n13)):
            load13(k, k, k % 3)
        st13["k"] = min(2, n13)
        load2(0, w2items[0], 0)
        st2["k"] = 1
        jglob = 0
        for q, (j0, j1) in enumerate(PARTS):
            nj = j1 - j0
            for j in range(j0, j1):
                if st13["k"] < n13:
                    load13(st13["k"], st13["k"], st13["k"] % 3)
                    st13["k"] += 1
                s = j % 3
                pb = (j % 2) * 2
                wa, wb = WA.ap[s][0], WA.ap[s][1]
                P.mm(rPS[pb], [(PS[pb][:, 0:T], wa[:, c * 128:(c + 1) * 128], HNv[:, c, :]) for c in range(NCH)],
                     reads=[rHN, WA.res[s][0]])
                P.mm(rPS[pb + 1], [(PS[pb + 1][:, 0:T], wb[:, c * 128:(c + 1) * 128], HNv[:, c, :]) for c in range(NCH)],
                     reads=[rHN, WA.res[s][1]])
                tb = j % 2
                P.op("act", lambda h, pb=pb, tb=tb: h.activation(out=TMPt[tb][:, :], in_=PS[pb][:, 0:T], func=AF.Silu),
                     reads=[rPS[pb]], writes=[rTMP[tb]])
                P.op("dve", lambda h, pb=pb, tb=tb, jl=j - j0: h.tensor_tensor(
                    out=ACTv[:, jl, :], in0=TMPt[tb][:, :], in1=PS[pb + 1][:, 0:T], op=ALU.mult),
                    reads=[rTMP[tb], rPS[pb + 1]], writes=[rACT])
            for ig in range(NIG):
                k2 = q * NIG + ig
                if st2["k"] < len(w2items):
                    load2(st2["k"], w2items[st2["k"]], st2["k"] % 2)
                    st2["k"] += 1
                s2 = k2 % 2
                w2v = W2S.ap[s2][0][:, 0:nj * GW].rearrange("p (j d) -> p j d", d=GW)
                for ii in range(IG):
                    i = ig * IG + ii
                    pb = 4 + ii
                    P.mm(rPS[pb], [(PS[pb][:, 0:T], w2v[:, jl, ii * 128:(ii + 1) * 128], ACTv[:, jl, :]) for jl in range(nj)],
                         reads=[rACT, W2S.res[s2][0]])
                    P.op("dve", lambda h, pb=pb, i=i: h.scalar_tensor_tensor(
                        out=Xv[:, i, :], in0=PS[pb][:, 0:T], scalar=0.5, in1=Xv[:, i, :], op0=ALU.mult, op1=ALU.add),
                        reads=[rPS[pb], rX], writes=[rX])

    def proj_resid(w, bias_col):
        P.barrier()
        WA = Slots(NCH * T * 2, 3, 2, D, "wa")
        items = list(range(NCH // 2))

        def load(k, m, s):
            for hh in range(2):
                j = 2 * m + hh
                P.dma("sp", WA.ds[s][hh], lambda h, hh=hh, j=j: h.dma_start(
                    out=WA.ap[s][hh], in_=w["g"][w["off"](j):w["off"](j) + 128, :]), reads=w["rg"], writes=[WA.res[s][hh]])

        def consume(k, m, s):
            for hh in range(2):
                i = 2 * m + hh
                pb = i % 4
                wa = WA.ap[s][hh]
                P.mm(rPS[pb], [(PS[pb][:, 0:T], wa[:, c * 128:(c + 1) * 128], HNv[:, c, :]) for c in range(NCH)],
                     reads=[rHN, WA.res[s][hh]])
                if bias_col is None:
                    P.op("dve", lambda h, pb=pb, i=i: h.tensor_tensor(
                        out=Xv[:, i, :], in0=PS[pb][:, 0:T], in1=Xv[:, i, :], op=ALU.add),
                        reads=[rPS[pb], rX], writes=[rX])
                else:
                    P.op("dve", lambda h, pb=pb, i=i: h.scalar_tensor_tensor(
                        out=Xv[:, i, :], in0=PS[pb][:, 0:T], scalar=smc(bias_col + i), in1=Xv[:, i, :],
                        op0=ALU.add, op1=ALU.add), reads=[rPS[pb], rX, rSM], writes=[rX])
        stream(items, 3, load, consume)

    def conv_mixer(i):
        P.barrier()
        WA = Slots(0, 3, 2, D, "wa")
        UOFF = 12 * D
        UW = 30 + T
        Uv = ar_f32(UOFF, NCH * UW).rearrange("p (c t) -> p c t", t=UW)
        rU = [Res(f"u{c}") for c in range(NCH)]
        w = Wt["pw1"]
        rmsnorm(O_NMIX)
        if i == 0:
            P.op("dve", lambda h: h.memset(Uv[:, :, 0:30], 0.0), writes=rU)
        else:
            P.op("dve", lambda h: h.tensor_copy(out=Uv[:, :, 0:30], in_=CARRYv), reads=[rCARRY], writes=rU)

        def load(k, m, s):
            for hh in range(2):
                j = m + NCH * hh
                P.dma("sp", WA.ds[s][hh], lambda h, hh=hh, j=j: h.dma_start(
                    out=WA.ap[s][hh], in_=w["g"][w["off"](j):w["off"](j) + 128, :]), reads=w["rg"], writes=[WA.res[s][hh]])

        def consume(k, m, s):
            pb = (m % 2) * 2
            tb = m % 2
            for hh in range(2):
                wa = WA.ap[s][hh]
                P.mm(rPS[pb + hh], [(PS[pb + hh][:, 0:T], wa[:, c * 128:(c + 1) * 128], HNv[:, c, :]) for c in range(NCH)],
                     reads=[rHN, WA.res[s][hh]])
            P.op("act", lambda h: h.activation(out=TMPt[tb][:, :], in_=PS[pb + 1][:, 0:T], func=AF.Sigmoid,
                                               bias=smc(O_PW1B + NCH + m), scale=1.0),
                 reads=[rPS[pb + 1], rSM], writes=[rTMP[tb]])
            P.op("dve", lambda h: h.scalar_tensor_tensor(
                out=Uv[:, m, 30:UW], in0=PS[pb][:, 0:T], scalar=smc(O_PW1B + m), in1=TMPt[tb][:, :],
                op0=ALU.add, op1=ALU.mult), reads=[rPS[pb], rTMP[tb], rSM], writes=[rU[m]])
        stream(list(range(NCH)), 3, load, consume)
        if i == 0:
            P.op("dve", lambda h: h.tensor_scalar_mul(out=Uv[:, :, 30:30 + HALO], in0=Uv[:, :, 30:30 + HALO],
                                                      scalar1=smc(O_META)), reads=rU + [rSM], writes=rU)
        P.op("dve", lambda h: h.tensor_copy(out=CARRYv, in_=Uv[:, :, T:UW]), reads=rU, writes=[rCARRY])
        for c in range(NCH):
            e = "dve"
            tb = 1 if e == "pool" else 0
            acc = TMPt[tb]
            P.op(e, lambda h, c=c, acc=acc: h.tensor_scalar_mul(
                out=acc[:, :], in0=Uv[:, c, 0:T], scalar1=smc(O_DWW + c * CW)), reads=[rU[c], rSM], writes=[rTMP[tb]])
            for k in range(1, CW):
                P.op(e, lambda h, c=c, k=k, acc=acc: h.scalar_tensor_tensor(
                    out=acc[:, :], in0=Uv[:, c, k:k + T], scalar=smc(O_DWW + c * CW + k), in1=acc[:, :],
                    op0=ALU.mult, op1=ALU.add), reads=[rU[c], rSM, rTMP[tb]], writes=[rTMP[tb]])
            P.op(e, lambda h, c=c, acc=acc: h.tensor_scalar_add(
                out=Uv[:, c, 0:T], in0=acc[:, :], scalar1=smc(O_DWB + c)), reads=[rTMP[tb], rSM], writes=[rU[c]])
        for c in range(NCH):
            P.mm(rPS[6], [(PS[6][:, 0:T], ONES32[:, :], Uv[:, c, 0:T])], reads=[rU[c], rCONST],
                 start=(c == 0), stop=(c == NCH - 1))
            tb = c % 2
            P.op("act", lambda h, c=c, tb=tb: h.activation(out=TMPt[tb][:, :], in_=Uv[:, c, 0:T], func=AF.Square),
                 reads=[rU[c]], writes=[rTMP[tb]])
            P.mm(rPS[7], [(PS[7][:, 0:T], ONES32[:, :], TMPt[tb][:, :])], reads=[rTMP[tb], rCONST],
                 start=(c == 0), stop=(c == NCH - 1))
        MU = TMPt[2]
        rMU = rTMP[2]
        P.op("dve", lambda h: h.tensor_scalar_mul(out=MU[:, :], in0=PS[6][:, 0:T], scalar1=1.0 / D),
             reads=[rPS[6]], writes=[rMU])
        P.op("dve", lambda h: h.tensor_tensor(out=TMPt[0][:, :], in0=MU[:, :], in1=MU[:, :], op=ALU.mult),
             reads=[rMU], writes=[rTMP[0]])
        P.op("dve", lambda h: h.scalar_tensor_tensor(out=TMPt[0][:, :], in0=PS[7][:, 0:T], scalar=1.0 / D,
                                                     in1=TMPt[0][:, :], op0=ALU.mult, op1=ALU.subtract),
             reads=[rPS[7], rTMP[0]], writes=[rTMP[0]])
        P.op("act", lambda h: h.activation(out=TMPt[0][:, :], in_=TMPt[0][:, :], func=AF.Sqrt, bias=EPS, scale=1.0),
             reads=[rTMP[0]], writes=[rTMP[0]])
        P.op("dve", lambda h: h.reciprocal(out=RSTDt[:, :], in_=TMPt[0][:, :]), reads=[rTMP[0]], writes=[rRSTD])
        for c in range(NCH):
            tb = c % 2
            P.op("dve", lambda h, c=c, tb=tb: h.tensor_tensor(out=TMPt[tb][:, :], in0=Uv[:, c, 0:T], in1=MU[:, :],
                                                              op=ALU.subtract), reads=[rU[c], rMU], writes=[rTMP[tb]])
            P.op("dve", lambda h, tb=tb: h.tensor_tensor(out=TMPt[tb][:, :], in0=TMPt[tb][:, :], in1=RSTDt[:, :],
                                                         op=ALU.mult), reads=[rTMP[tb], rRSTD], writes=[rTMP[tb]])
            P.op("act", lambda h, c=c, tb=tb: h.activation(out=HNv[:, c, :], in_=TMPt[tb][:, :], func=AF.Silu,
                                                           bias=smc(O_LNB + c), scale=smc(O_LNG + c)),
                 reads=[rTMP[tb], rSM], writes=[rHN])
        proj_resid(Wt["pw2"], O_PW2B)

    def real_cols(i):
        c0 = HALO if i == 0 else 0
        r0 = i * T - HALO + c0
        return c0, r0, T - c0

    def kv_stage(i):
        P.barrier()
        c0, r0, n = real_cols(i)
        WA = Slots(0, 3, 2, D, "wa")
        KST = ar_bf(12 * D, NCH * T).rearrange("p (c t) -> p c t", t=T)
        rKST = Res("kst")
        FW = FWt[:, :].rearrange("p (c m) -> p c m", m=H)
        w = Wt["wk"]
        rmsnorm(O_KVN)

        def load(k, m, s):
            for hh in range(2):
                j = 2 * m + hh
                P.dma("sp", WA.ds[s][hh], lambda h, hh=hh, j=j: h.dma_start(
                    out=WA.ap[s][hh], in_=w["g"][w["off"](j):w["off"](j) + 128, :]), reads=w["rg"], writes=[WA.res[s][hh]])

        def consume(k, m, s):
            for hh in range(2):
                hd = 2 * m + hh
                pb = hd % 4
                wa = WA.ap[s][hh]
                P.mm(rPS[pb], [(PS[pb][:, 0:T], wa[:, c * 128:(c + 1) * 128], HNv[:, c, :]) for c in range(NCH)],
                     reads=[rHN, WA.res[s][hh]])
                e = "act" if hd % 2 == 0 else "dve"
                if e == "act":
                    P.op("act", lambda h, pb=pb, hd=hd: h.copy(out=KST[:, hd, :], in_=PS[pb][:, 0:T]),
                         reads=[rPS[pb]], writes=[rKST])
                else:
                    P.op("dve", lambda h, pb=pb, hd=hd: h.tensor_copy(out=KST[:, hd, :], in_=PS[pb][:, 0:T]),
                         reads=[rPS[pb]], writes=[rKST])
        stream(list(range(NCH // 2)), 3, load, consume)
        P.dma("sp", misc_sem(), lambda h: h.dma_start(
            out=Kp.rearrange("(c p) t -> p c t", p=128)[:, :, r0:r0 + n], in_=KST[:, :, c0:c0 + n]),
            reads=[rKST], writes=[rKp])
        P.mm(rPS[4], [(PS[4][0:H, 0:T], FW[:, c, :], HNv[:, c, :]) for c in range(NCH)], reads=[rHN, rFW])
        XF, A_, M_ = TMPt[0], TMPt[1], TMPt[2]
        P.op("dve", lambda h: h.tensor_scalar_add(out=XF[0:H, :], in0=PS[4][0:H, 0:T], scalar1=smc(O_BF, 0, H)),
             reads=[rPS[4], rSM], writes=[rTMP[0]])
        P.op("act", lambda h: h.activation(out=A_[0:H, :], in_=XF[0:H, :], func=AF.Abs),
             reads=[rTMP[0]], writes=[rTMP[1]])
        P.op("act", lambda h: h.activation(out=A_[0:H, :], in_=A_[0:H, :], func=AF.Exp, scale=-1.0),
             reads=[rTMP[1]], writes=[rTMP[1]])
        P.op("act", lambda h: h.activation(out=A_[0:H, :], in_=A_[0:H, :], func=AF.Ln, bias=1.0, scale=1.0),
             reads=[rTMP[1]], writes=[rTMP[1]])
        P.op("dve", lambda h: h.tensor_scalar_min(out=M_[0:H, :], in0=XF[0:H, :], scalar1=0.0),
             reads=[rTMP[0]], writes=[rTMP[2]])
        P.op("dve", lambda h: h.tensor_tensor(out=M_[0:H, :], in0=M_[0:H, :], in1=A_[0:H, :], op=ALU.subtract),
             reads=[rTMP[2], rTMP[1]], writes=[rTMP[2]])
        P.dma("sp", misc_sem(), lambda h: h.dma_start(out=LSp[:, r0:r0 + n], in_=M_[0:H, c0:c0 + n]),
              reads=[rTMP[2]], writes=[rLSp])
        P.barrier()
        NPC = (T + 127) // 128
        VST = ar_bf(0, NPC * D).rearrange("p (k d) -> p k d", d=D)
        rVST = Res("vst")
        WV = Slots(NPC * D * 2, 2, NVC, (NCH // NVC) * GV, "wv")
        wv = Wt["wv"]
        pieces = []
        a = c0
        while a < T:
            m = min(128, T - a)
            pieces.append((a, m))
            a += m

        def loadv(k, vg, s):
            cpg = NCH // NVC
            for cq in range(NVC):
                o_ = wv["off"](vg * NVC + cq)
                P.dma("sp", WV.ds[s][cq], lambda h, cq=cq, o_=o_: h.dma_start(
                    out=WV.ap[s][cq], in_=wv["g"][o_:o_ + 128, :]), reads=wv["rg"], writes=[WV.res[s][cq]])

        def consv(k, vg, s):
            cpg = NCH // NVC
            wvq = [WV.ap[s][cq].rearrange("p (c d) -> p c d", d=GV) for cq in range(NVC)]
            for pi, (a, m) in enumerate(pieces):
                pb = pi % 4
                P.mm(rPS[pb], [(PS[pb][0:m, 0:GV], HNv[:, c, a:a + m], wvq[c // cpg][:, c % cpg, :]) for c in range(NCH)],
                     reads=[rHN] + WV.res[s])
                if pi % 2 == 0:
                    P.op("act", lambda h, pb=pb, pi=pi, m=m: h.copy(out=VST[0:m, pi, vg * GV:(vg + 1) * GV],
                                                                    in_=PS[pb][0:m, 0:GV]), reads=[rPS[pb]], writes=[rVST])
                else:
                    P.op("dve", lambda h, pb=pb, pi=pi, m=m: h.tensor_copy(out=VST[0:m, pi, vg * GV:(vg + 1) * GV],
                                                                           in_=PS[pb][0:m, 0:GV]), reads=[rPS[pb]], writes=[rVST])
        stream(list(range(NVG)), 2, loadv, consv)
        for pi, (a, m) in enumerate(pieces):
            ra = r0 + (a - c0)
            P.dma("sp", misc_sem(), lambda h, pi=pi, m=m, ra=ra: h.dma_start(out=Vp[ra:ra + m, :], in_=VST[0:m, pi, :]),
                  reads=[rVST], writes=[rVp])

    def spill_x(i):
        P.dma("sp", misc_sem(), lambda h: h.dma_start(out=XS[i], in_=Xt[:, :]), reads=[rX], writes=[rXS[i]])

    def reload_x(i):
        P.dma("sp", misc_sem(), lambda h: h.dma_start(out=Xt[:, :], in_=XS[i]), reads=[rXS[i]], writes=[rX])

    NEGC_OFF = ARENA_B - 4 * KBC * H * 4
    regcache = {}

    def getj(h):
        if "j" not in regcache:
            regcache["j"] = h.snap(h.partition_id() % 4, min_val=0, max_val=3)
        return regcache["j"]
    NEGCv = ar_f32(NEGC_OFF, 4 * KBC * H).rearrange("p (k h) -> p k h", h=H)
    rNEGC = Res("negc")

    def exchange_and_prep():
        P.barrier()
        tk, tv, tl = {}, {}, {}
        for k in range(NKC):
            cc_group(GRP4, Kp[k * KR:(k + 1) * KR, :], Kgp[(k * 7 + 3) * KR:(k * 7 + 7) * KR, :], [rKp, rKpad], tk)
        for k in range(KBC):
            cc_group(GRP4, Vp[k * 128:(k + 1) * 128, :], Vgp[(k * 7 + 3) * 128:(k * 7 + 7) * 128, :], [rVp, rVpad], tv)
        cc_group(GRP4, LSp[:, :], LSgp[3 * H:7 * H, :], [rLSp, rLSpad], tl)
        rKg_, rVg_, rLg_ = toks_to_res(tk), toks_to_res(tv), toks_to_res(tl)
        Kgx = Kgp.rearrange("(k x r) t -> x k r t", x=7, r=KR)
        Vgx = Vgp.rearrange("(k x r) d -> x k r d", x=7, r=128)
        for s in range(4):
            def fkw(h, s=s):
                j = getj(h)
                return h.dma_start(out=Kw[s * D:(s + 1) * D, :].rearrange("(k r) t -> k r t", r=KR),
                                   in_=Kgx[bass.ds(j + s, 1)][0])
            P.dma("sp", misc_sem(), fkw, reads=rKg_, writes=[rKw])

            def fvw(h, s=s):
                j = getj(h)
                return h.dma_start(out=Vw[s * TOK:(s + 1) * TOK, :].rearrange("(k r) d -> k r d", r=128),
                                   in_=Vgx[bass.ds(j + s, 1)][0])
            P.dma("sp", misc_sem(), fvw, reads=rVg_, writes=[rVw])
        LSW = ar_f32(0, 4 * TOK)
        CREL = ar_f32(16 * TOK, 4 * TOK)
        ONE = ar_f32(32 * TOK, TOK)
        rLSW, rCREL, rONE = Res("lsw"), Res("crel"), Res("one")
        P.op("pool", lambda h: h.memset(ONE[0:H, :], 1.0), writes=[rONE])
        for s in range(4):
            def f(h, s=s):
                j = getj(h)
                return h.dma_start(out=LSW[0:H, s * TOK:(s + 1) * TOK], in_=LSgp[bass.ds((j + s) * H, H), :])
            P.dma("sp", misc_sem(), f, reads=rLg_, writes=[rLSW])
        for s in range(4):
            P.op("dve", lambda h, s=s: h.tensor_scalar_mul(out=LSW[0:H, s * TOK:(s + 1) * TOK],
                                                           in0=LSW[0:H, s * TOK:(s + 1) * TOK],
                                                           scalar1=smc(O_META + 1 + s, 0, H)),
                 reads=[rLSW, rSM], writes=[rLSW])
        for s in range(4):
            init = 0.0 if s == 0 else CREL[0:H, s * TOK - 1:s * TOK]
            P.op("dve", lambda h, s=s, init=init: h.tensor_tensor_scan(
                out=CREL[0:H, s * TOK:(s + 1) * TOK], data0=ONE[0:H, :], data1=LSW[0:H, s * TOK:(s + 1) * TOK],
                initial=init, op0=ALU.mult, op1=ALU.add), reads=[rLSW, rONE, rCREL], writes=[rCREL])
        P.dma("sp", misc_sem(), lambda h: h.dma_start(out=Cd[:, :], in_=CREL[0:H, :]), reads=[rCREL], writes=[rCd])
        for s in range(4):
            pb = s
            for kk in range(KBC):
                kb = s * KBC + kk
                P.tr(rPS[pb], PS[pb][:, kk * H:(kk + 1) * H], CREL[0:H, kb * 128:(kb + 1) * 128], ID32[0:H, 0:H],
                     reads=[rCREL, rCONST])
            P.op("dve", lambda h, s=s, pb=pb: h.tensor_scalar(
                out=NEGCv[:, s * KBC:(s + 1) * KBC, :], in0=PS[pb][:, 0:KBC * H].rearrange("p (k h) -> p k h", h=H),
                scalar1=-1.0, scalar2=smc(O_META + 5 + s), op0=ALU.mult, op1=ALU.add),
                reads=[rPS[pb], rSM], writes=[rNEGC])

    def attention(i):
        P.barrier()
        rmsnorm(O_NMIX + NCH)
        QT = ar_bf(0, NCH * T).rearrange("p (c t) -> p c t", t=T)
        rQT = Res("qt")
        WA = Slots(NCH * T * 2, 3, 2, D, "wa")
        w = Wt["wq"]

        def load(k, m, s):
            for hh in range(2):
                j = 2 * m + hh
                P.dma("sp", WA.ds[s][hh], lambda h, hh=hh, j=j: h.dma_start(
                    out=WA.ap[s][hh], in_=w["g"][w["off"](j):w["off"](j) + 128, :]), reads=w["rg"], writes=[WA.res[s][hh]])

        def consume(k, m, s):
            for hh in range(2):
                hd = 2 * m + hh
                pb = hd % 4
                wa = WA.ap[s][hh]
                P.mm(rPS[pb], [(PS[pb][:, 0:T], wa[:, c * 128:(c + 1) * 128], HNv[:, c, :]) for c in range(NCH)],
                     reads=[rHN, WA.res[s][hh]])
                if hd % 2 == 0:
                    P.op("act", lambda h, pb=pb, hd=hd: h.copy(out=QT[:, hd, :], in_=PS[pb][:, 0:T]),
                         reads=[rPS[pb]], writes=[rQT])
                else:
                    P.op("dve", lambda h, pb=pb, hd=hd: h.tensor_copy(out=QT[:, hd, :], in_=PS[pb][:, 0:T]),
                         reads=[rPS[pb]], writes=[rQT])
        stream(list(range(NCH // 2)), 3, load, consume)
        P.barrier()
        KOFF = NCH * T * 2
        KT = [ar_bf(KOFF + b * 8 * TOK, 4 * TOK) for b in range(2)]
        VV = [ar_bf(KOFF + 16 * TOK + b * 8 * TOK, 4 * TOK).rearrange("p (k d) -> p k d", d=128) for b in range(2)]
        CQ = [ar_f32(KOFF + 32 * TOK + b * 4 * T, T) for b in range(2)]
        NPT = 4
        SB = [0, 1, 6, 7]
        PT = [ar_bf(KOFF + 32 * TOK + 8 * T + b * 2 * T, T) for b in range(NPT)]
        assert KOFF + 32 * TOK + 8 * T + NPT * 2 * T <= NEGC_OFF
        rKT = [[Res(f"kt{b}{s}") for s in range(4)] for b in range(2)]
        rVV = [[Res(f"vv{b}{s}") for s in range(4)] for b in range(2)]
        rCQ = [Res(f"cq{b}") for b in range(2)]
        rPT = [Res(f"pt{b}") for b in range(NPT)]
        dK = [[P.dsem(f"d_kt_{b}{s}") for s in range(4)] for b in range(2)]
        dV = [[P.dsem(f"d_vv_{b}{s}") for s in range(4)] for b in range(2)]
        dC = [P.dsem(f"d_cq_{b}") for b in range(2)]
        c0, r0, n = real_cols(i)
        rq0 = i * T - HALO
        nown = (rq0 + T - 1) // 128 + 1
        blocks = list(range(3 * KBC)) + [3 * KBC + kk for kk in range(nown)]
        scale = 1.0 / float(np.sqrt(128.0))

        def loadh(hd):
            b = hd % 2
            for s in range(4):
                def fk(h, s=s, b=b, hd=hd):
                    return h.dma_start(out=KT[b][:, s * TOK:(s + 1) * TOK],
                                       in_=Kw[s * D + hd * 128:s * D + (hd + 1) * 128, :])
                P.dma("sp", dK[b][s], fk, reads=[rKw], writes=[rKT[b][s]])

                def fv(h, s=s, b=b, hd=hd):
                    return h.dma_start(out=VV[b][:, s * KBC:(s + 1) * KBC, :],
                                       in_=Vw[s * TOK:(s + 1) * TOK, hd * 128:(hd + 1) * 128]
                                       .rearrange("(k p) d -> p k d", p=128))
                P.dma("sp", dV[b][s], fv, reads=[rVw], writes=[rVV[b][s]])
            P.dma("sp", dC[b], lambda h, b=b, hd=hd: h.dma_start(
                out=CQ[b], in_=Cd[hd, 3 * TOK + rq0:3 * TOK + rq0 + T].partition_broadcast(128)),
                reads=[rCd], writes=[rCQ[b]])

        loadh(0)
        gi = 0
        for hd in range(H):
            if hd + 1 < H:
                loadh(hd + 1)
            b = hd % 2
            po, pl = 2 + b, 4 + b
            nb = len(blocks)

            def qk(bi):
                kb = blocks[bi]
                sp = SB[bi % 4]
                P.mm(rPS[sp], [(PS[sp][:, 0:T], KT[b][:, kb * 128:(kb + 1) * 128], QT[:, hd, :])],
                     reads=[rKT[b][kb // KBC], rQT])
            qk(0)
            if nb > 1:
                qk(1)
            for bi, kb in enumerate(blocks):
                if bi + 2 < nb:
                    qk(bi + 2)
                sp = SB[bi % 4]
                tb = bi % 3
                pt = gi % NPT
                gi += 1
                P.op("dve", lambda h, sp=sp, tb=tb: h.scalar_tensor_tensor(
                    out=TMPt[tb][:, :], in0=PS[sp][:, 0:T], scalar=scale, in1=CQ[b], op0=ALU.mult, op1=ALU.add),
                    reads=[rPS[sp], rCQ[b]], writes=[rTMP[tb]])
                if kb >= 3 * KBC:
                    kl = kb - 3 * KBC
                    if kl * 128 + 127 > rq0:
                        def fsel(h, tb=tb, kl=kl):
                            if "neg" not in regcache:
                                regcache["neg"] = h.to_reg(NEG)
                            return h.affine_select(
                                out=TMPt[tb][:, :], in_=TMPt[tb][:, :], pattern=[[1, T]], compare_op=ALU.is_ge,
                                fill=regcache["neg"], base=rq0 - 128 * kl, channel_multiplier=-1)
                        P.op("pool", fsel, reads=[rTMP[tb]], writes=[rTMP[tb]])
                P.op("act", lambda h, tb=tb, pt=pt, kb=kb: h.activation(
                    out=PT[pt], in_=TMPt[tb][:, :], func=AF.Exp, bias=NEGCv[:, kb, hd:hd + 1], scale=1.0),
                    reads=[rTMP[tb], rNEGC], writes=[rPT[pt]])
                P.mm(rPS[po], [(PS[po][:, 0:T], VV[b][:, kb, :], PT[pt])], reads=[rVV[b][kb // KBC], rPT[pt]],
                     start=(bi == 0), stop=(bi == nb - 1))
                P.mm(rPS[pl], [(PS[pl][:, 0:T], ONESB[:, :], PT[pt])], reads=[rPT[pt], rCONST],
                     start=(bi == 0), stop=(bi == nb - 1))
            P.op("dve", lambda h, pl=pl: h.tensor_scalar_max(out=RSTDt[:, :], in0=PS[pl][:, 0:T], scalar1=1e-30),
                 reads=[rPS[pl]], writes=[rRSTD])
            P.op("dve", lambda h: h.reciprocal(out=RSTDt[:, :], in_=RSTDt[:, :]), reads=[rRSTD], writes=[rRSTD])
            P.op("dve", lambda h, po=po, hd=hd: h.tensor_tensor(out=HNv[:, hd, :], in0=PS[po][:, 0:T], in1=RSTDt[:, :],
                                                                op=ALU.mult), reads=[rPS[po], rRSTD], writes=[rHN])
        proj_resid(Wt["wo"], None)

    def final_out(i):
        P.barrier()
        c0, r0, n = real_cols(i)
        OUTB = ar_f32(0, NCH * T).rearrange("p (c t) -> p c t", t=T)
        rOUT = Res("outb")
        rmsnorm(O_FIN, to_out=(OUTB, rOUT))
        P.dma("sp", misc_sem(), lambda h: h.dma_start(
            out=outT.rearrange("(c p) t -> p c t", p=128)[:, :, r0:r0 + n], in_=OUTB[:, :, c0:c0 + n]),
            reads=[rOUT], writes=[Res("outdram")])

    stage_ctr = [0]

    def run(fn, *a):
        stage_ctr[0] += 1
        if STAGE_LIMIT is not None and stage_ctr[0] > STAGE_LIMIT:
            return
        fn(*a)

    stages1 = [(ffn, (0, 1, O_NF1), False), (conv_mixer, (), True), (ffn, (0, 2, O_NF2), False),
               (kv_stage, (), True), (ffn, (1, 1, O_NF1 + NCH), False)]
    for si, (fn, args, per_tile) in enumerate(stages1):
        for i in range(NT):
            run(P.barrier)
            if si == 0:
                run(load_x, i)
            else:
                run(reload_x, i)
            if per_tile:
                run(fn, i)
            else:
                run(fn, *args)
            run(spill_x, i)
    run(exchange_and_prep)
    for i in range(NT):
        run(P.barrier)
        run(reload_x, i)
        run(attention, i)
        run(ffn, 1, 2, O_NF2 + NCH)
        run(final_out, i)
    P.barrier()
    P.emit(None)
    print("kernel build: n_inst", P.n_inst, {e: len(v) for e, v in P.q.items()}, flush=True)
    return nc


def _chunk_layout(W, nchp):
    n = W.shape[1] // 128
    t = W.reshape(NCH, 128, n, 128).transpose(2, 1, 0, 3).reshape(n * 128, D)
    if nchp > n:
        t = np.concatenate([t, np.zeros(((nchp - n) * 128, D), np.float32)], axis=0)
    return np.ascontiguousarray(t)


def _col(v):
    return np.ascontiguousarray(np.asarray(v, np.float32).reshape(-1, 128).T)


_NC_CACHE = {}


def make_in_maps(x, norm_ffn1, ffn1_w1, ffn1_w3, ffn1_w2, norm_mix, norm_ffn2, ffn2_w1, ffn2_w3, ffn2_w2,
                 conv_pw1_w, conv_pw1_b, conv_dw_w, conv_dw_b, conv_ln_g, conv_ln_b, conv_pw2_w, conv_pw2_b,
                 kv_norm, w_kvf, b_f, attn_wq, attn_wo, final_norm):
    x = np.asarray(x, np.float32)
    full = {}
    for l in range(2):
        for f, (a, b, c) in ((1, (ffn1_w1, ffn1_w3, ffn1_w2)), (2, (ffn2_w1, ffn2_w3, ffn2_w2))):
            full[f"w1_{l}{f}"] = _chunk_layout(np.asarray(a[l], np.float32), NFP)
            full[f"w3_{l}{f}"] = _chunk_layout(np.asarray(b[l], np.float32), NFP)
            w2 = np.asarray(c[l], np.float32)
            full[f"w2_{l}{f}"] = np.concatenate([w2, np.zeros((NFP * 128 - FF, D), np.float32)], axis=0)
    full["pw1"] = _chunk_layout(np.asarray(conv_pw1_w[0], np.float32), NP1)
    full["pw2"] = _chunk_layout(np.asarray(conv_pw2_w[0], np.float32), NCP)
    wkvf = np.asarray(w_kvf, np.float32)
    full["wk"] = _chunk_layout(np.ascontiguousarray(wkvf[:, 0:D]), NCP)
    wv = wkvf[:, D:2 * D]
    cpg = NCH // NVC
    wvl = wv.reshape(NVC, cpg, 128, NVG, GV).transpose(3, 0, 2, 1, 4).reshape(NVG * NVC * 128, cpg * GV)
    if NVG < 8:
        wvl = np.concatenate([wvl, np.zeros(((8 - NVG) * NVC * 128, cpg * GV), np.float32)], axis=0)
    full["wv"] = np.ascontiguousarray(wvl)
    full["wq"] = _chunk_layout(np.asarray(attn_wq[0], np.float32), NCP)
    full["wo"] = _chunk_layout(np.asarray(attn_wo[0], np.float32), NCP)
    wf = np.ascontiguousarray(wkvf[:, 2 * D:2 * D + H].reshape(NCH, 128, H).transpose(1, 0, 2).reshape(128, NCH * H))

    small = np.zeros((128, NS), np.float32)
    n = NCH
    small[:, O_NF1:O_NF1 + n] = _col(norm_ffn1[0]); small[:, O_NF1 + n:O_NF1 + 2 * n] = _col(norm_ffn1[1])
    small[:, O_NMIX:O_NMIX + n] = _col(norm_mix[0]); small[:, O_NMIX + n:O_NMIX + 2 * n] = _col(norm_mix[1])
    small[:, O_NF2:O_NF2 + n] = _col(norm_ffn2[0]); small[:, O_NF2 + n:O_NF2 + 2 * n] = _col(norm_ffn2[1])
    small[:, O_KVN:O_KVN + n] = _col(kv_norm)
    small[:, O_FIN:O_FIN + n] = _col(final_norm)
    small[:, O_PW1B:O_PW1B + 2 * n] = _col(conv_pw1_b[0])
    small[:, O_DWB:O_DWB + n] = _col(conv_dw_b[0])
    small[:, O_LNG:O_LNG + n] = _col(conv_ln_g[0])
    small[:, O_LNB:O_LNB + n] = _col(conv_ln_b[0])
    small[:, O_PW2B:O_PW2B + n] = _col(conv_pw2_b[0])
    dww = np.asarray(conv_dw_w[0], np.float32)
    small[:, O_DWW:O_DWW + NCH * CW] = dww.T.reshape(NCH, 128, CW).transpose(1, 0, 2).reshape(128, NCH * CW)
    small[0:H, O_BF] = np.asarray(b_f, np.float32)

    in_maps = []
    for r in range(8):
        b, j = r // 4, r % 4
        st = j * TOK
        xs = np.zeros((TOKH, D), np.float32)
        if j > 0:
            xs[:] = x[b, st - HALO:st + TOK]
        else:
            xs[HALO:] = x[b, 0:TOK]
        sm = small.copy()
        sm[:, O_META] = 0.0 if j == 0 else 1.0
        for s in range(4):
            valid = (j + s - 3) >= 0
            sm[:, O_META + 1 + s] = 1.0 if valid else 0.0
            sm[:, O_META + 5 + s] = 0.0 if valid else NEG
        m = {"xin": np.ascontiguousarray(xs.T), "small": sm, "wf": wf}
        for name, arr in full.items():
            rows = arr.shape[0] // 8
            m[name] = arr[r * rows:(r + 1) * rows]
        in_maps.append(m)
    return in_maps


def assemble(results):
    out = np.empty((2, 4 * TOK, D), np.float32)
    for r in range(8):
        b, j = r // 4, r % 4
        out[b, j * TOK:(j + 1) * TOK, :] = np.asarray(results[r]["outT"], np.float32).T
    return out


def kernel(**inputs):
    set_cfg()
    in_maps = make_in_maps(**inputs)
    if "nc" not in _NC_CACHE:
        _NC_CACHE["nc"] = build_nc()
    res = run_bass_kernel_spmd(_NC_CACHE["nc"], in_maps, core_ids=list(range(8)))
    return assemble(res.results)
```

```python
import types
import numpy as np
import ml_dtypes
import concourse.bass as bass
import concourse.mybir as mybir
from concourse.bass_utils import run_bass_kernel_spmd

F32 = mybir.dt.float32
BF16 = mybir.dt.bfloat16
ALU = mybir.AluOpType
AF = mybir.ActivationFunctionType

D = FF = NCH = NF = NFP = H = T = NT = TOK = TOKH = NJMAX = GV = IG = KBC = NP1 = NCP = NVG = NVC = KR = 0
PARTS = []
O_NF1 = O_NMIX = O_NF2 = O_KVN = O_FIN = O_PW1B = O_DWB = O_LNG = O_LNB = O_PW2B = O_DWW = O_BF = O_META = NS = 0
CW = 31
HALO = 32
EPS = 1e-6
NEG = -30000.0


def _ceil8(n):
    return (n + 7) // 8 * 8


def set_cfg(d=4096, ff=11008, tok=2048, t=416, njmax=18):
    g = globals()
    nch = d // 128
    nf = ff // 128
    nloc = _ceil8(nf) // 8
    parts = [(r * nloc, min((r + 1) * nloc, nf)) for r in range(8) if r * nloc < nf]
    g.update(D=d, FF=ff, NCH=nch, NF=nf, NFP=_ceil8(nf), H=nch, T=t, TOK=tok, TOKH=tok + HALO,
             NT=(tok + HALO) // t, NJMAX=max(b - a for a, b in parts), PARTS=parts, GV=min(512, d),
             IG=min(4, nch), KBC=tok // 128, NP1=_ceil8(2 * nch), NCP=_ceil8(nch), NVG=d // min(512, d),
             NVC=min(4, nch), KR=min(256, d))
    assert (tok + HALO) % t == 0 and tok % 128 == 0 and ff % 128 == 0 and nch % 2 == 0
    g.update(O_NF1=0, O_NMIX=2 * nch, O_NF2=4 * nch, O_KVN=6 * nch, O_FIN=7 * nch, O_PW1B=8 * nch,
             O_DWB=10 * nch, O_LNG=11 * nch, O_LNB=12 * nch, O_PW2B=13 * nch, O_DWW=14 * nch,
             O_BF=45 * nch, O_META=45 * nch + 1, NS=45 * nch + 1 + 16)


set_cfg()

STAGE_LIMIT = None
ARENA_B = 110592


def _freeze(fn):
    if getattr(fn, "__closure__", None) is None:
        return fn
    cells = []
    for c in fn.__closure__:
        try:
            cells.append(types.CellType(c.cell_contents))
        except ValueError:
            cells.append(c)
    g = types.FunctionType(fn.__code__, fn.__globals__, fn.__name__, fn.__defaults__, tuple(cells))
    g.__kwdefaults__ = fn.__kwdefaults__
    return g


class Res:
    __slots__ = ("name", "w", "r")

    def __init__(self, name):
        self.name = name
        self.w = None
        self.r = {}


class DSem:
    def __init__(self, nc, name):
        self.sem = nc.alloc_semaphore(name)
        self.val = 0


class Prog:
    ENG = ("pe", "act", "dve", "pool", "sp")

    def __init__(self, nc):
        self.nc = nc
        self.q = {e: [] for e in self.ENG}
        self.sem = {e: nc.alloc_semaphore("prog_" + e) for e in ("pe", "act", "dve", "pool")}
        self.cnt = {e: 0 for e in self.sem}
        self.waited = {e: {} for e in self.ENG}
        self.semobj = {}
        self.dsems = []
        self.n_inst = 0

    def dsem(self, name):
        if name in self.semobj:
            return self.semobj[name]
        d = DSem(self.nc, name)
        self.semobj[name] = d
        self.dsems.append(d)
        return d

    def _wait(self, e, tok):
        if tok is None:
            return
        sem, val = tok
        if e == "pe" and sem is self.sem["pe"]:
            return
        key = id(sem)
        if self.waited[e].get(key, 0) >= val:
            return
        self.waited[e][key] = val
        self.q[e].append(lambda h, sem=sem, val=val: h.wait_ge(sem, val))

    def _deps(self, e, reads, writes):
        for r in reads:
            self._wait(e, r.w)
        for w in writes:
            self._wait(e, w.w)
            for sem, val in w.r.values():
                self._wait(e, (sem, val))

    def _mark(self, tok, reads, writes):
        k = id(tok[0])
        for r in reads:
            r.r[k] = tok
        for w in writes:
            w.w = tok
            w.r = {}

    def op(self, e, fn, reads=(), writes=()):
        fn = _freeze(fn)
        self._deps(e, reads, writes)
        self.cnt[e] += 1
        sem = self.sem[e]
        tok = (sem, self.cnt[e])
        self.q[e].append(lambda h, fn=fn, sem=sem: fn(h).then_inc(sem, 1))
        self._mark(tok, reads, writes)
        self.n_inst += 1

    def mm(self, ps, mms, reads, start=True, stop=True):
        self._deps("pe", reads, [ps])
        self.cnt["pe"] += 1
        sem = self.sem["pe"]
        tok = (sem, self.cnt["pe"])
        n = len(mms)

        def thunk(h, mms=mms, n=n, sem=sem, start=start, stop=stop):
            for k, (o, l, r) in enumerate(mms):
                ins = h.matmul(o, lhsT=l, rhs=r, start=(start and k == 0), stop=(stop and k == n - 1))
            ins.then_inc(sem, 1)
        self.q["pe"].append(thunk)
        self._mark(tok, reads, [ps])
        self.n_inst += n

    def tr(self, ps, out, in_, ident, reads):
        self._deps("pe", reads, [ps])
        self.cnt["pe"] += 1
        sem = self.sem["pe"]
        tok = (sem, self.cnt["pe"])
        self.q["pe"].append(lambda h, sem=sem: h.transpose(out, in_, ident).then_inc(sem, 1))
        self._mark(tok, reads, [ps])
        self.n_inst += 1

    def dma(self, e, dsem, fn, reads=(), writes=()):
        fn = _freeze(fn)
        self._wait(e, (dsem.sem, dsem.val))
        self._deps(e, reads, writes)
        dsem.val += 16
        tok = (dsem.sem, dsem.val)
        s = dsem.sem
        self.q[e].append(lambda h, fn=fn, s=s: fn(h).then_inc(s, 16))
        self._mark(tok, reads, writes)

    def cc(self, dsem, fn, reads=(), writes=()):
        fn = _freeze(fn)
        self._wait("pool", (dsem.sem, dsem.val))
        self._deps("pool", reads, writes)
        dsem.val += 1
        tok = (dsem.sem, dsem.val)
        s = dsem.sem
        self.q["pool"].append(lambda h, fn=fn, s=s: fn(h).then_inc(s, 1))
        self._mark(tok, reads, writes)

    def barrier(self):
        toks = [(self.sem[e], self.cnt[e]) for e in self.sem if self.cnt[e] > 0]
        toks += [(d.sem, d.val) for d in self.dsems if d.val > 0 and not getattr(d, "nobar", False)]
        for e in self.ENG:
            for t in toks:
                self._wait(e, t)

    def emit(self, final_tok):
        nc = self.nc
        q = self.q
        with nc.Block() as block:
            @block.tensor
            def _(h):
                for f in q["pe"]:
                    f(h)

            @block.scalar
            def _(h):
                for f in q["act"]:
                    f(h)

            @block.vector
            def _(h):
                for f in q["dve"]:
                    f(h)

            @block.gpsimd
            def _(h):
                for f in q["pool"]:
                    f(h)

            @block.sync
            def _(h):
                for f in q["sp"]:
                    f(h)


def build_nc(stop_after=None):
    nc = bass.Bass("TRN2", target_bir_lowering=False)
    P = Prog(nc)

    def din(name, shape, dt=F32):
        return nc.dram_tensor(name, list(shape), dt, kind="ExternalInput").ap()

    xin = din("xin", [D, TOKH])
    small_in = din("small", [128, NS])
    wf_in = din("wf", [128, NCH * H])
    wspec = []

    def wdecl(name, rows, cols):
        ap = din(name, [rows, cols])
        nloc = rows // 128
        wb = nc.dram_tensor(name + "_b", [rows, cols], BF16).ap()
        wg = nc.dram_tensor(name + "_g", [8 * rows, cols], BF16).ap()
        w4 = nc.dram_tensor(name + "_q", [4 * rows, cols], BF16).ap()
        r = {"name": name, "in": ap, "b": wb, "g": wg, "q": w4, "rb": Res(name + "_b"), "rg": [],
             "rows": rows, "cols": cols, "nloc": nloc}

        def off(j, nloc=nloc):
            rk, jl = j // nloc, j % nloc
            return ((jl * 4 + rk % 4) * 2 + rk // 4) * 128
        r["off"] = off
        wspec.append(r)
        return r

    Wt = {}
    for l in range(2):
        for f in (1, 2):
            Wt[f"w1_{l}{f}"] = wdecl(f"w1_{l}{f}", NFP // 8 * 128, D)
            Wt[f"w3_{l}{f}"] = wdecl(f"w3_{l}{f}", NFP // 8 * 128, D)
            Wt[f"w2_{l}{f}"] = wdecl(f"w2_{l}{f}", NFP // 8 * 128, D)
    Wt["pw1"] = wdecl("pw1", NP1 // 8 * 128, D)
    Wt["pw2"] = wdecl("pw2", NCP // 8 * 128, D)
    Wt["wk"] = wdecl("wk", NCP // 8 * 128, D)
    Wt["wv"] = wdecl("wv", NVC * 128, (NCH // NVC) * GV)
    Wt["wq"] = wdecl("wq", NCP // 8 * 128, D)
    Wt["wo"] = wdecl("wo", NCP // 8 * 128, D)
    worder = ["w1_01", "w3_01", "w2_01", "pw1", "pw2", "w1_02", "w3_02", "w2_02", "wk", "wv",
              "w1_11", "w3_11", "w2_11", "wq", "wo", "w1_12", "w3_12", "w2_12"]

    outT = nc.dram_tensor("outT", [D, TOK], F32, kind="ExternalOutput").ap()

    XS = nc.dram_tensor("xs", [NT, 128, NCH * T], F32).ap()
    Kp = nc.dram_tensor("kp", [D, TOK], BF16).ap()
    Vp = nc.dram_tensor("vp", [TOK, D], BF16).ap()
    LSp = nc.dram_tensor("lsp", [H, TOK], F32).ap()
    NKC = D // KR
    Kgp = nc.dram_tensor("kgp", [NKC * 7 * KR, TOK], BF16).ap()
    Vgp = nc.dram_tensor("vgp", [KBC * 7 * 128, D], BF16).ap()
    LSgp = nc.dram_tensor("lsgp", [7 * H, TOK], F32).ap()
    Cd = nc.dram_tensor("cd", [H, 4 * TOK], F32).ap()
    Kw = nc.dram_tensor("kw", [4 * D, TOK], BF16).ap()
    Vw = nc.dram_tensor("vw", [4 * TOK, D], BF16).ap()
    rKw, rVw = Res("kw"), Res("vw")
    rXS = [Res(f"xs{i}") for i in range(NT)]
    rKp, rVp, rLSp = Res("kp"), Res("vp"), Res("lsp")
    rKgp, rVgp, rLSgp, rCd = Res("kgp"), Res("vgp"), Res("lsgp"), Res("cd")
    rKpad, rVpad, rLSpad = Res("kpad"), Res("vpad"), Res("lspad")

    Xt = nc.alloc_sbuf_tensor("X", [128, NCH * T], F32)
    HNt = nc.alloc_sbuf_tensor("HN", [128, NCH * T], BF16)
    SM = nc.alloc_sbuf_tensor("SM", [128, NS], F32)
    ONES32 = nc.alloc_sbuf_tensor("ones32", [128, 128], F32)
    ONESB = nc.alloc_sbuf_tensor("onesb", [128, 128], BF16)
    ID32 = nc.alloc_sbuf_tensor("id32", [128, 128], F32)
    TMPt = [nc.alloc_sbuf_tensor(f"tmp{i}", [128, T], F32) for i in range(3)]
    RSTDt = nc.alloc_sbuf_tensor("rstd", [128, T], F32)
    CARRYt = nc.alloc_sbuf_tensor("carry", [128, NCH * 30], F32)
    FWt = nc.alloc_sbuf_tensor("fw", [128, NCH * H], BF16)
    rFW = Res("fw")
    AR = nc.alloc_sbuf_tensor("arena", [128, ARENA_B // 2], BF16)
    PS = [nc.alloc_psum_tensor(f"ps{i}", [128, 512], F32) for i in range(8)]
    rPS = [Res(f"ps{i}") for i in range(8)]

    Xv = Xt[:, :].rearrange("p (c t) -> p c t", t=T)
    HNv = HNt[:, :].rearrange("p (c t) -> p c t", t=T)
    CARRYv = CARRYt[:, :].rearrange("p (c t) -> p c t", t=30)
    rX, rHN, rSM, rRSTD, rCARRY, rCONST = Res("X"), Res("HN"), Res("SM"), Res("RSTD"), Res("CARRY"), Res("CONST")
    rTMP = [Res(f"tmp{i}") for i in range(3)]

    def ar_bf(off, n):
        return AR[:, off // 2: off // 2 + n]

    def ar_f32(off, n):
        return AR[:, off // 2: off // 2 + 2 * n].bitcast(F32)

    def smc(col, p0=0, p1=128):
        return SM[p0:p1, col:col + 1]

    class Slots:
        def __init__(self, base, n, halves, half_elems, tag):
            self.n = n
            self.ap = [[ar_bf(base + (s * halves + hh) * half_elems * 2, half_elems) for hh in range(halves)]
                       for s in range(n)]
            self.res = [[Res(f"{tag}{s}_{hh}") for hh in range(halves)] for s in range(n)]
            self.ds = [[P.dsem(f"d_{tag}{s}_{hh}") for hh in range(halves)] for s in range(n)]

    sem_misc = {"sp": [P.dsem(f"d_misc{i}") for i in range(6)], "pool": [P.dsem(f"d_miscp{i}") for i in range(3)]}
    misc_i = [0]

    def misc_sem(e="sp"):
        misc_i[0] += 1
        return sem_misc[e][misc_i[0] % len(sem_misc[e])]

    P.op("pool", lambda h: h.memset(ONES32[:, :], 1.0), writes=[rCONST])
    P.op("pool", lambda h: h.memset(ONESB[:, :], 1.0), writes=[rCONST])
    P.op("pool", lambda h: h.memset(ID32[:, :], 1.0), writes=[rCONST])
    P.op("pool", lambda h: h.affine_select(out=ID32[:, :], in_=ID32[:, :], pattern=[[1, 128]],
                                           compare_op=ALU.is_equal, fill=0.0, base=0, channel_multiplier=-1),
         reads=[rCONST], writes=[rCONST])
    P.dma("sp", misc_sem(), lambda h: h.dma_start(out=SM[:, :], in_=small_in[:, :]), writes=[rSM])
    P.dma("pool", misc_sem("pool"), lambda h: h.dma_start(out=FWt[:, :], in_=wf_in[:, :]), writes=[rFW])
    rAZ = Res("az")
    ZN = max(3 * KR * TOK // 128, 3 * D, 2 * TOK)
    ZB = ar_bf(0, ZN)
    P.op("pool", lambda h: h.memset(ZB, 0.0), writes=[rAZ])
    for k in range(NKC):
        P.dma("pool", misc_sem("pool"), lambda h, k=k: h.dma_start(
            out=Kgp[k * 7 * KR: (k * 7 + 3) * KR, :].rearrange("(p a) t -> p (a t)", p=128),
            in_=ar_bf(0, 3 * KR * TOK // 128)[:, :]), reads=[rAZ], writes=[rKpad])
    for k in range(KBC):
        P.dma("pool", misc_sem("pool"), lambda h, k=k: h.dma_start(
            out=Vgp[k * 7 * 128: (k * 7 + 3) * 128, :].rearrange("(p a) d -> p (a d)", p=128),
            in_=ar_bf(0, 3 * D)[:, :]), reads=[rAZ], writes=[rVpad])
    P.dma("pool", misc_sem("pool"), lambda h: h.dma_start(
        out=LSgp[0:3 * H, :], in_=ar_f32(0, TOK)[0:3 * H, :]), reads=[rAZ], writes=[rLSpad])

    ccsems = [P.dsem(f"d_cc{i}") for i in range(8)]
    for c_ in ccsems:
        c_.nobar = True
    cci = [0]

    def next_cc():
        cci[0] += 1
        return ccsems[cci[0] % len(ccsems)]

    def cc_group(kind_groups, in_ap, out_ap, reads, toks):
        ds_ = next_cc()
        r = Res("cc")
        P.cc(ds_, lambda h: h.collective_compute("AllGather", ALU.bypass, replica_groups=kind_groups,
                                                 ins=[in_ap], outs=[out_ap]), reads=reads, writes=[r])
        toks[id(ds_.sem)] = r.w
        return r

    def toks_to_res(toks):
        out = []
        for t in toks.values():
            r = Res("ccsum")
            r.w = t
            out.append(r)
        return out

    castsem = [P.dsem("d_cast0"), P.dsem("d_cast1")]
    for cs_ in castsem:
        cs_.nobar = True
    GRP4 = [[0, 1, 2, 3], [4, 5, 6, 7]]
    PAIRS = [[0, 4], [1, 5], [2, 6], [3, 7]]
    for wi, name in enumerate(worder):
        w = Wt[name]
        rows = w["rows"]
        nsp = 4 if rows >= 512 else 1
        rr = rows // nsp
        for k in range(nsp):
            P.dma("pool", castsem[(wi * 4 + k) % 2], lambda h, w=w, k=k, rr=rr: h.dma_start(
                out=w["b"][k * rr:(k + 1) * rr, :], in_=w["in"][k * rr:(k + 1) * rr, :]), writes=[w["rb"]])
        for cs in castsem:
            P._wait("pool", (cs.sem, cs.val))
        toks = {}
        for jl in range(w["nloc"]):
            rq = cc_group(GRP4, w["b"][jl * 128:(jl + 1) * 128, :], w["q"][jl * 512:(jl + 1) * 512, :], [w["rb"]], {})
            for r4 in range(4):
                cc_group(PAIRS, w["q"][(jl * 4 + r4) * 128:(jl * 4 + r4 + 1) * 128, :],
                         w["g"][(jl * 4 + r4) * 256:(jl * 4 + r4 + 1) * 256, :], [rq], toks)
        w["rg"] = toks_to_res(toks)

    def load_x(i):
        P.dma("sp", misc_sem(), lambda h: h.dma_start(
            out=Xv, in_=xin.rearrange("(c p) t -> p c t", p=128)[:, :, i * T:(i + 1) * T]), writes=[rX])

    def rmsnorm(gcol, to_out=None):
        for c in range(NCH):
            tb = c % 2
            P.op("act", lambda h, c=c, tb=tb: h.activation(out=TMPt[tb][:, :], in_=Xv[:, c, :], func=AF.Square),
                 reads=[rX], writes=[rTMP[tb]])
            P.mm(rPS[7], [(PS[7][:, 0:T], ONES32[:, :], TMPt[tb][:, :])], reads=[rTMP[tb], rCONST],
                 start=(c == 0), stop=(c == NCH - 1))
        P.op("act", lambda h: h.activation(out=TMPt[2][:, :], in_=PS[7][:, 0:T], func=AF.Sqrt, bias=EPS, scale=1.0 / D),
             reads=[rPS[7]], writes=[rTMP[2]])
        P.op("dve", lambda h: h.reciprocal(out=RSTDt[:, :], in_=TMPt[2][:, :]), reads=[rTMP[2]], writes=[rRSTD])
        for c in range(NCH):
            if to_out is None:
                P.op("dve", lambda h, c=c: h.scalar_tensor_tensor(
                    out=HNv[:, c, :], in0=Xv[:, c, :], scalar=smc(gcol + c), in1=RSTDt[:, :],
                    op0=ALU.mult, op1=ALU.mult), reads=[rX, rRSTD, rSM], writes=[rHN])
            else:
                ov, rov = to_out
                P.op("dve", lambda h, c=c, ov=ov: h.scalar_tensor_tensor(
                    out=ov[:, c, :], in0=Xv[:, c, :], scalar=smc(gcol + c), in1=RSTDt[:, :],
                    op0=ALU.mult, op1=ALU.mult), reads=[rX, rRSTD, rSM], writes=[rov])

    def stream(items, nslots, load, consume):
        n = len(items)
        depth = nslots - 1
        for k in range(min(depth, n)):
            load(k, items[k], k % nslots)
        for k in range(n):
            if k + depth < n:
                load(k + depth, items[k + depth], (k + depth) % nslots)
            consume(k, items[k], k % nslots)

    def ffn(l, f, gcol):
        P.barrier()
        WA = Slots(0, 3, 2, D, "wa")
        ACT_OFF = 12 * D
        ACTv = ar_bf(ACT_OFF, NJMAX * T).rearrange("p (j t) -> p j t", t=T)
        rACT = Res("actb")
        GW = IG * 128
        NIG = NCH // IG
        W2S = Slots(ACT_OFF + NJMAX * T * 2, 2, 1, NJMAX * GW, "w2s")
        w1, w3, w2 = Wt[f"w1_{l}{f}"], Wt[f"w3_{l}{f}"], Wt[f"w2_{l}{f}"]
        rmsnorm(gcol)

        def load13(k, j, s):
            P.dma("sp", WA.ds[s][0], lambda h: h.dma_start(out=WA.ap[s][0], in_=w1["g"][w1["off"](j):w1["off"](j) + 128, :]),
                  reads=w1["rg"], writes=[WA.res[s][0]])
            P.dma("sp", WA.ds[s][1], lambda h: h.dma_start(out=WA.ap[s][1], in_=w3["g"][w3["off"](j):w3["off"](j) + 128, :]),
                  reads=w3["rg"], writes=[WA.res[s][1]])

        w2items = [(q, ig) for q in range(len(PARTS)) for ig in range(NIG)]

        def load2(k, it, s):
            q, ig = it
            j0, j1 = PARTS[q]
            nj = j1 - j0
            dst = W2S.ap[s][0][:, 0:nj * GW].rearrange("p (j d) -> p j d", d=GW)
            rk = j0 // w2["nloc"]
            x = (rk % 4) * 2 + rk // 4
            src = w2["g"].rearrange("(jl x p) d -> x p jl d", x=8, p=128)[x, :, 0:nj, ig * GW:(ig + 1) * GW]
            P.dma("sp", W2S.ds[s][0], lambda h: h.dma_start(out=dst, in_=src), reads=w2["rg"], writes=[W2S.res[s][0]])

        st13 = {"k": 0}
        st2 = {"k": 0}
        n13 = NF
        for k in range(min(2, n13)):
            load13(k, k, k % 3)
        st13["k"] = min(2, n13)
        load2(0, w2items[0], 0)
        st2["k"] = 1
        jglob = 0
        for q, (j0, j1) in enumerate(PARTS):
            nj = j1 - j0
            for j in range(j0, j1):
                if st13["k"] < n13:
                    load13(st13["k"], st13["k"], st13["k"] % 3)
                    st13["k"] += 1
                s = j % 3
                pb = (j % 2) * 2
                wa, wb = WA.ap[s][0], WA.ap[s][1]
                P.mm(rPS[pb], [(PS[pb][:, 0:T], wa[:, c * 128:(c + 1) * 128], HNv[:, c, :]) for c in range(NCH)],
                     reads=[rHN, WA.res[s][0]])
                P.mm(rPS[pb + 1], [(PS[pb + 1][:, 0:T], wb[:, c * 128:(c + 1) * 128], HNv[:, c, :]) for c in range(NCH)],
                     reads=[rHN, WA.res[s][1]])
                tb = j % 2
                P.op("act", lambda h, pb=pb, tb=tb: h.activation(out=TMPt[tb][:, :], in_=PS[pb][:, 0:T], func=AF.Silu),
                     reads=[rPS[pb]], writes=[rTMP[tb]])
                P.op("dve", lambda h, pb=pb, tb=tb, jl=j - j0: h.tensor_tensor(
                    out=ACTv[:, jl, :], in0=TMPt[tb][:, :], in1=PS[pb + 1][:, 0:T], op=ALU.mult),
                    reads=[rTMP[tb], rPS[pb + 1]], writes=[rACT])
            for ig in range(NIG):
                k2 = q * NIG + ig
                if st2["k"] < len(w2items):
                    load2(st2["k"], w2items[st2["k"]], st2["k"] % 2)
                    st2["k"] += 1
                s2 = k2 % 2
                w2v = W2S.ap[s2][0][:, 0:nj * GW].rearrange("p (j d) -> p j d", d=GW)
                for ii in range(IG):
                    i = ig * IG + ii
                    pb = 4 + ii
                    P.mm(rPS[pb], [(PS[pb][:, 0:T], w2v[:, jl, ii * 128:(ii + 1) * 128], ACTv[:, jl, :]) for jl in range(nj)],
                         reads=[rACT, W2S.res[s2][0]])
                    P.op("dve", lambda h, pb=pb, i=i: h.scalar_tensor_tensor(
                        out=Xv[:, i, :], in0=PS[pb][:, 0:T], scalar=0.5, in1=Xv[:, i, :], op0=ALU.mult, op1=ALU.add),
                        reads=[rPS[pb], rX], writes=[rX])

    def proj_resid(w, bias_col):
        P.barrier()
        WA = Slots(NCH * T * 2, 3, 2, D, "wa")
        items = list(range(NCH // 2))

        def load(k, m, s):
            for hh in range(2):
                j = 2 * m + hh
                P.dma("sp", WA.ds[s][hh], lambda h, hh=hh, j=j: h.dma_start(
                    out=WA.ap[s][hh], in_=w["g"][w["off"](j):w["off"](j) + 128, :]), reads=w["rg"], writes=[WA.res[s][hh]])

        def consume(k, m, s):
            for hh in range(2):
                i = 2 * m + hh
                pb = i % 4
                wa = WA.ap[s][hh]
                P.mm(rPS[pb], [(PS[pb][:, 0:T], wa[:, c * 128:(c + 1) * 128], HNv[:, c, :]) for c in range(NCH)],
                     reads=[rHN, WA.res[s][hh]])
                if bias_col is None:
                    P.op("dve", lambda h, pb=pb, i=i: h.tensor_tensor(
                        out=Xv[:, i, :], in0=PS[pb][:, 0:T], in1=Xv[:, i, :], op=ALU.add),
                        reads=[rPS[pb], rX], writes=[rX])
                else:
                    P.op("dve", lambda h, pb=pb, i=i: h.scalar_tensor_tensor(
                        out=Xv[:, i, :], in0=PS[pb][:, 0:T], scalar=smc(bias_col + i), in1=Xv[:, i, :],
                        op0=ALU.add, op1=ALU.add), reads=[rPS[pb], rX, rSM], writes=[rX])
        stream(items, 3, load, consume)

    def conv_mixer(i):
        P.barrier()
        WA = Slots(0, 3, 2, D, "wa")
        UOFF = 12 * D
        UW = 30 + T
        Uv = ar_f32(UOFF, NCH * UW).rearrange("p (c t) -> p c t", t=UW)
        rU = [Res(f"u{c}") for c in range(NCH)]
        w = Wt["pw1"]
        rmsnorm(O_NMIX)
        if i == 0:
            P.op("dve", lambda h: h.memset(Uv[:, :, 0:30], 0.0), writes=rU)
        else:
            P.op("dve", lambda h: h.tensor_copy(out=Uv[:, :, 0:30], in_=CARRYv), reads=[rCARRY], writes=rU)

        def conv_chunk(c):
            acc = TMPt[2]
            if i == 0:
                P.op("dve", lambda h: h.tensor_scalar_mul(out=Uv[:, c, 30:30 + HALO], in0=Uv[:, c, 30:30 + HALO],
                                                          scalar1=smc(O_META)), reads=[rU[c], rSM], writes=[rU[c]])
            P.op("dve", lambda h: h.tensor_scalar_mul(
                out=acc[:, :], in0=Uv[:, c, 0:T], scalar1=smc(O_DWW + c * CW)), reads=[rU[c], rSM], writes=[rTMP[2]])
            for k in range(1, CW):
                P.op("dve", lambda h: h.scalar_tensor_tensor(
                    out=acc[:, :], in0=Uv[:, c, k:k + T], scalar=smc(O_DWW + c * CW + k), in1=acc[:, :],
                    op0=ALU.mult, op1=ALU.add), reads=[rU[c], rSM, rTMP[2]], writes=[rTMP[2]])
            P.op("dve", lambda h: h.tensor_scalar_add(
                out=Uv[:, c, 0:T], in0=acc[:, :], scalar1=smc(O_DWB + c)), reads=[rTMP[2], rSM], writes=[rU[c]])

        def load(k, m, s):
            for hh in range(2):
                j = m + NCH * hh
                P.dma("sp", WA.ds[s][hh], lambda h, hh=hh, j=j: h.dma_start(
                    out=WA.ap[s][hh], in_=w["g"][w["off"](j):w["off"](j) + 128, :]), reads=w["rg"], writes=[WA.res[s][hh]])

        def consume(k, m, s):
            pb = (m % 2) * 2
            tb = m % 2
            for hh in range(2):
                wa = WA.ap[s][hh]
                P.mm(rPS[pb + hh], [(PS[pb + hh][:, 0:T], wa[:, c * 128:(c + 1) * 128], HNv[:, c, :]) for c in range(NCH)],
                     reads=[rHN, WA.res[s][hh]])
            P.op("act", lambda h: h.activation(out=TMPt[tb][:, :], in_=PS[pb + 1][:, 0:T], func=AF.Sigmoid,
                                               bias=smc(O_PW1B + NCH + m), scale=1.0),
                 reads=[rPS[pb + 1], rSM], writes=[rTMP[tb]])
            P.op("dve", lambda h: h.scalar_tensor_tensor(
                out=Uv[:, m, 30:UW], in0=PS[pb][:, 0:T], scalar=smc(O_PW1B + m), in1=TMPt[tb][:, :],
                op0=ALU.add, op1=ALU.mult), reads=[rPS[pb], rTMP[tb], rSM], writes=[rU[m]])
            if m >= 1:
                conv_chunk(m - 1)
        stream(list(range(NCH)), 3, load, consume)
        conv_chunk(NCH - 1)
        P.op("dve", lambda h: h.tensor_copy(out=CARRYv, in_=Uv[:, :, T:UW]), reads=rU, writes=[rCARRY])
        for c in range(NCH):
            P.mm(rPS[6], [(PS[6][:, 0:T], ONES32[:, :], Uv[:, c, 0:T])], reads=[rU[c], rCONST],
                 start=(c == 0), stop=(c == NCH - 1))
            tb = c % 2
            P.op("act", lambda h, c=c, tb=tb: h.activation(out=TMPt[tb][:, :], in_=Uv[:, c, 0:T], func=AF.Square),
                 reads=[rU[c]], writes=[rTMP[tb]])
            P.mm(rPS[7], [(PS[7][:, 0:T], ONES32[:, :], TMPt[tb][:, :])], reads=[rTMP[tb], rCONST],
                 start=(c == 0), stop=(c == NCH - 1))
        MU = TMPt[2]
        rMU = rTMP[2]
        P.op("dve", lambda h: h.tensor_scalar_mul(out=MU[:, :], in0=PS[6][:, 0:T], scalar1=1.0 / D),
             reads=[rPS[6]], writes=[rMU])
        P.op("dve", lambda h: h.tensor_tensor(out=TMPt[0][:, :], in0=MU[:, :], in1=MU[:, :], op=ALU.mult),
             reads=[rMU], writes=[rTMP[0]])
        P.op("dve", lambda h: h.scalar_tensor_tensor(out=TMPt[0][:, :], in0=PS[7][:, 0:T], scalar=1.0 / D,
                                                     in1=TMPt[0][:, :], op0=ALU.mult, op1=ALU.subtract),
             reads=[rPS[7], rTMP[0]], writes=[rTMP[0]])
        P.op("act", lambda h: h.activation(out=TMPt[0][:, :], in_=TMPt[0][:, :], func=AF.Sqrt, bias=EPS, scale=1.0),
             reads=[rTMP[0]], writes=[rTMP[0]])
        P.op("dve", lambda h: h.reciprocal(out=RSTDt[:, :], in_=TMPt[0][:, :]), reads=[rTMP[0]], writes=[rRSTD])
        for c in range(NCH):
            tb = c % 2
            P.op("dve", lambda h, c=c, tb=tb: h.tensor_tensor(out=TMPt[tb][:, :], in0=Uv[:, c, 0:T], in1=MU[:, :],
                                                              op=ALU.subtract), reads=[rU[c], rMU], writes=[rTMP[tb]])
            P.op("dve", lambda h, tb=tb: h.tensor_tensor(out=TMPt[tb][:, :], in0=TMPt[tb][:, :], in1=RSTDt[:, :],
                                                         op=ALU.mult), reads=[rTMP[tb], rRSTD], writes=[rTMP[tb]])
            P.op("act", lambda h, c=c, tb=tb: h.activation(out=HNv[:, c, :], in_=TMPt[tb][:, :], func=AF.Silu,
                                                           bias=smc(O_LNB + c), scale=smc(O_LNG + c)),
                 reads=[rTMP[tb], rSM], writes=[rHN])
        proj_resid(Wt["pw2"], O_PW2B)

    def real_cols(i):
        c0 = HALO if i == 0 else 0
        r0 = i * T - HALO + c0
        return c0, r0, T - c0

    def kv_stage(i):
        P.barrier()
        c0, r0, n = real_cols(i)
        WA = Slots(0, 3, 2, D, "wa")
        KST = ar_bf(12 * D, NCH * T).rearrange("p (c t) -> p c t", t=T)
        rKST = Res("kst")
        FW = FWt[:, :].rearrange("p (c m) -> p c m", m=H)
        w = Wt["wk"]
        rmsnorm(O_KVN)

        def load(k, m, s):
            for hh in range(2):
                j = 2 * m + hh
                P.dma("sp", WA.ds[s][hh], lambda h, hh=hh, j=j: h.dma_start(
                    out=WA.ap[s][hh], in_=w["g"][w["off"](j):w["off"](j) + 128, :]), reads=w["rg"], writes=[WA.res[s][hh]])

        def consume(k, m, s):
            for hh in range(2):
                hd = 2 * m + hh
                pb = hd % 4
                wa = WA.ap[s][hh]
                P.mm(rPS[pb], [(PS[pb][:, 0:T], wa[:, c * 128:(c + 1) * 128], HNv[:, c, :]) for c in range(NCH)],
                     reads=[rHN, WA.res[s][hh]])
                e = "act" if hd % 2 == 0 else "dve"
                if e == "act":
                    P.op("act", lambda h, pb=pb, hd=hd: h.copy(out=KST[:, hd, :], in_=PS[pb][:, 0:T]),
                         reads=[rPS[pb]], writes=[rKST])
                else:
                    P.op("dve", lambda h, pb=pb, hd=hd: h.tensor_copy(out=KST[:, hd, :], in_=PS[pb][:, 0:T]),
                         reads=[rPS[pb]], writes=[rKST])
        stream(list(range(NCH // 2)), 3, load, consume)
        P.dma("sp", misc_sem(), lambda h: h.dma_start(
            out=Kp.rearrange("(c p) t -> p c t", p=128)[:, :, r0:r0 + n], in_=KST[:, :, c0:c0 + n]),
            reads=[rKST], writes=[rKp])
        P.mm(rPS[4], [(PS[4][0:H, 0:T], FW[:, c, :], HNv[:, c, :]) for c in range(NCH)], reads=[rHN, rFW])
        XF, A_, M_ = TMPt[0], TMPt[1], TMPt[2]
        P.op("dve", lambda h: h.tensor_scalar_add(out=XF[0:H, :], in0=PS[4][0:H, 0:T], scalar1=smc(O_BF, 0, H)),
             reads=[rPS[4], rSM], writes=[rTMP[0]])
        P.op("act", lambda h: h.activation(out=A_[0:H, :], in_=XF[0:H, :], func=AF.Abs),
             reads=[rTMP[0]], writes=[rTMP[1]])
        P.op("act", lambda h: h.activation(out=A_[0:H, :], in_=A_[0:H, :], func=AF.Exp, scale=-1.0),
             reads=[rTMP[1]], writes=[rTMP[1]])
        P.op("act", lambda h: h.activation(out=A_[0:H, :], in_=A_[0:H, :], func=AF.Ln, bias=1.0, scale=1.0),
             reads=[rTMP[1]], writes=[rTMP[1]])
        P.op("dve", lambda h: h.tensor_scalar_min(out=M_[0:H, :], in0=XF[0:H, :], scalar1=0.0),
             reads=[rTMP[0]], writes=[rTMP[2]])
        P.op("dve", lambda h: h.tensor_tensor(out=M_[0:H, :], in0=M_[0:H, :], in1=A_[0:H, :], op=ALU.subtract),
             reads=[rTMP[2], rTMP[1]], writes=[rTMP[2]])
        P.dma("sp", misc_sem(), lambda h: h.dma_start(out=LSp[:, r0:r0 + n], in_=M_[0:H, c0:c0 + n]),
              reads=[rTMP[2]], writes=[rLSp])
        P.barrier()
        NPC = (T + 127) // 128
        VST = ar_bf(0, NPC * D).rearrange("p (k d) -> p k d", d=D)
        rVST = Res("vst")
        WV = Slots(NPC * D * 2, 2, NVC, (NCH // NVC) * GV, "wv")
        wv = Wt["wv"]
        pieces = []
        a = c0
        while a < T:
            m = min(128, T - a)
            pieces.append((a, m))
            a += m

        def loadv(k, vg, s):
            cpg = NCH // NVC
            for cq in range(NVC):
                o_ = wv["off"](vg * NVC + cq)
                P.dma("sp", WV.ds[s][cq], lambda h, cq=cq, o_=o_: h.dma_start(
                    out=WV.ap[s][cq], in_=wv["g"][o_:o_ + 128, :]), reads=wv["rg"], writes=[WV.res[s][cq]])

        def consv(k, vg, s):
            cpg = NCH // NVC
            wvq = [WV.ap[s][cq].rearrange("p (c d) -> p c d", d=GV) for cq in range(NVC)]
            for pi, (a, m) in enumerate(pieces):
                pb = pi % 4
                P.mm(rPS[pb], [(PS[pb][0:m, 0:GV], HNv[:, c, a:a + m], wvq[c // cpg][:, c % cpg, :]) for c in range(NCH)],
                     reads=[rHN] + WV.res[s])
                if pi % 2 == 0:
                    P.op("act", lambda h, pb=pb, pi=pi, m=m: h.copy(out=VST[0:m, pi, vg * GV:(vg + 1) * GV],
                                                                    in_=PS[pb][0:m, 0:GV]), reads=[rPS[pb]], writes=[rVST])
                else:
                    P.op("dve", lambda h, pb=pb, pi=pi, m=m: h.tensor_copy(out=VST[0:m, pi, vg * GV:(vg + 1) * GV],
                                                                           in_=PS[pb][0:m, 0:GV]), reads=[rPS[pb]], writes=[rVST])
        stream(list(range(NVG)), 2, loadv, consv)
        for pi, (a, m) in enumerate(pieces):
            ra = r0 + (a - c0)
            P.dma("sp", misc_sem(), lambda h, pi=pi, m=m, ra=ra: h.dma_start(out=Vp[ra:ra + m, :], in_=VST[0:m, pi, :]),
                  reads=[rVST], writes=[rVp])

    def spill_x(i):
        P.dma("sp", misc_sem(), lambda h: h.dma_start(out=XS[i], in_=Xt[:, :]), reads=[rX], writes=[rXS[i]])

    def reload_x(i):
        P.dma("sp", misc_sem(), lambda h: h.dma_start(out=Xt[:, :], in_=XS[i]), reads=[rXS[i]], writes=[rX])

    NEGC_OFF = ARENA_B - 4 * KBC * H * 4
    regcache = {}

    def getj(h):
        if "j" not in regcache:
            regcache["j"] = h.snap(h.partition_id() % 4, min_val=0, max_val=3)
        return regcache["j"]
    NEGCv = ar_f32(NEGC_OFF, 4 * KBC * H).rearrange("p (k h) -> p k h", h=H)
    rNEGC = Res("negc")

    def exchange_and_prep():
        P.barrier()
        tk, tv, tl = {}, {}, {}
        for k in range(NKC):
            cc_group(GRP4, Kp[k * KR:(k + 1) * KR, :], Kgp[(k * 7 + 3) * KR:(k * 7 + 7) * KR, :], [rKp, rKpad], tk)
        for k in range(KBC):
            cc_group(GRP4, Vp[k * 128:(k + 1) * 128, :], Vgp[(k * 7 + 3) * 128:(k * 7 + 7) * 128, :], [rVp, rVpad], tv)
        cc_group(GRP4, LSp[:, :], LSgp[3 * H:7 * H, :], [rLSp, rLSpad], tl)
        rKg_, rVg_, rLg_ = toks_to_res(tk), toks_to_res(tv), toks_to_res(tl)
        Kgx = Kgp.rearrange("(k x r) t -> x k r t", x=7, r=KR)
        Vgx = Vgp.rearrange("(k x r) d -> x k r d", x=7, r=128)
        for s in range(4):
            def fkw(h, s=s):
                j = getj(h)
                return h.dma_start(out=Kw[s * D:(s + 1) * D, :].rearrange("(k r) t -> k r t", r=KR),
                                   in_=Kgx[bass.ds(j + s, 1)][0])
            P.dma("sp", misc_sem(), fkw, reads=rKg_, writes=[rKw])

            def fvw(h, s=s):
                j = getj(h)
                return h.dma_start(out=Vw[s * TOK:(s + 1) * TOK, :].rearrange("(k r) d -> k r d", r=128),
                                   in_=Vgx[bass.ds(j + s, 1)][0])
            P.dma("sp", misc_sem(), fvw, reads=rVg_, writes=[rVw])
        LSW = ar_f32(0, 4 * TOK)
        CREL = ar_f32(16 * TOK, 4 * TOK)
        ONE = ar_f32(32 * TOK, TOK)
        rLSW, rCREL, rONE = Res("lsw"), Res("crel"), Res("one")
        P.op("pool", lambda h: h.memset(ONE[0:H, :], 1.0), writes=[rONE])
        for s in range(4):
            def f(h, s=s):
                j = getj(h)
                return h.dma_start(out=LSW[0:H, s * TOK:(s + 1) * TOK], in_=LSgp[bass.ds((j + s) * H, H), :])
            P.dma("sp", misc_sem(), f, reads=rLg_, writes=[rLSW])
        for s in range(4):
            P.op("dve", lambda h, s=s: h.tensor_scalar_mul(out=LSW[0:H, s * TOK:(s + 1) * TOK],
                                                           in0=LSW[0:H, s * TOK:(s + 1) * TOK],
                                                           scalar1=smc(O_META + 1 + s, 0, H)),
                 reads=[rLSW, rSM], writes=[rLSW])
        for s in range(4):
            init = 0.0 if s == 0 else CREL[0:H, s * TOK - 1:s * TOK]
            P.op("dve", lambda h, s=s, init=init: h.tensor_tensor_scan(
                out=CREL[0:H, s * TOK:(s + 1) * TOK], data0=ONE[0:H, :], data1=LSW[0:H, s * TOK:(s + 1) * TOK],
                initial=init, op0=ALU.mult, op1=ALU.add), reads=[rLSW, rONE, rCREL], writes=[rCREL])
        P.dma("sp", misc_sem(), lambda h: h.dma_start(out=Cd[:, :], in_=CREL[0:H, :]), reads=[rCREL], writes=[rCd])
        for s in range(4):
            pb = s
            for kk in range(KBC):
                kb = s * KBC + kk
                P.tr(rPS[pb], PS[pb][:, kk * H:(kk + 1) * H], CREL[0:H, kb * 128:(kb + 1) * 128], ID32[0:H, 0:H],
                     reads=[rCREL, rCONST])
            P.op("dve", lambda h, s=s, pb=pb: h.tensor_scalar(
                out=NEGCv[:, s * KBC:(s + 1) * KBC, :], in0=PS[pb][:, 0:KBC * H].rearrange("p (k h) -> p k h", h=H),
                scalar1=-1.0, scalar2=smc(O_META + 5 + s), op0=ALU.mult, op1=ALU.add),
                reads=[rPS[pb], rSM], writes=[rNEGC])

    def attention(i):
        P.barrier()
        rmsnorm(O_NMIX + NCH)
        QT = ar_bf(0, NCH * T).rearrange("p (c t) -> p c t", t=T)
        rQT = Res("qt")
        WA = Slots(NCH * T * 2, 3, 2, D, "wa")
        w = Wt["wq"]

        def load(k, m, s):
            for hh in range(2):
                j = 2 * m + hh
                P.dma("sp", WA.ds[s][hh], lambda h, hh=hh, j=j: h.dma_start(
                    out=WA.ap[s][hh], in_=w["g"][w["off"](j):w["off"](j) + 128, :]), reads=w["rg"], writes=[WA.res[s][hh]])

        def consume(k, m, s):
            for hh in range(2):
                hd = 2 * m + hh
                pb = hd % 4
                wa = WA.ap[s][hh]
                P.mm(rPS[pb], [(PS[pb][:, 0:T], wa[:, c * 128:(c + 1) * 128], HNv[:, c, :]) for c in range(NCH)],
                     reads=[rHN, WA.res[s][hh]])
                if hd % 2 == 0:
                    P.op("act", lambda h, pb=pb, hd=hd: h.copy(out=QT[:, hd, :], in_=PS[pb][:, 0:T]),
                         reads=[rPS[pb]], writes=[rQT])
                else:
                    P.op("dve", lambda h, pb=pb, hd=hd: h.tensor_copy(out=QT[:, hd, :], in_=PS[pb][:, 0:T]),
                         reads=[rPS[pb]], writes=[rQT])
        stream(list(range(NCH // 2)), 3, load, consume)
        P.barrier()
        KOFF = NCH * T * 2
        KT = [ar_bf(KOFF + b * 8 * TOK, 4 * TOK) for b in range(2)]
        VV = [ar_bf(KOFF + 16 * TOK + b * 8 * TOK, 4 * TOK).rearrange("p (k d) -> p k d", d=128) for b in range(2)]
        CQ = [ar_f32(KOFF + 32 * TOK + b * 4 * T, T) for b in range(2)]
        NPT = 4
        SB = [0, 1, 6, 7]
        PT = [ar_bf(KOFF + 32 * TOK + 8 * T + b * 2 * T, T) for b in range(NPT)]
        assert KOFF + 32 * TOK + 8 * T + NPT * 2 * T <= NEGC_OFF
        rKT = [[Res(f"kt{b}{s}") for s in range(4)] for b in range(2)]
        rVV = [[Res(f"vv{b}{s}") for s in range(4)] for b in range(2)]
        rCQ = [Res(f"cq{b}") for b in range(2)]
        rPT = [Res(f"pt{b}") for b in range(NPT)]
        dK = [[P.dsem(f"d_kt_{b}{s}") for s in range(4)] for b in range(2)]
        dV = [[P.dsem(f"d_vv_{b}{s}") for s in range(4)] for b in range(2)]
        dC = [P.dsem(f"d_cq_{b}") for b in range(2)]
        c0, r0, n = real_cols(i)
        rq0 = i * T - HALO
        nown = (rq0 + T - 1) // 128 + 1
        blocks = list(range(3 * KBC)) + [3 * KBC + kk for kk in range(nown)]
        scale = 1.0 / float(np.sqrt(128.0))

        def loadh(hd):
            b = hd % 2
            for s in range(4):
                def fk(h, s=s, b=b, hd=hd):
                    return h.dma_start(out=KT[b][:, s * TOK:(s + 1) * TOK],
                                       in_=Kw[s * D + hd * 128:s * D + (hd + 1) * 128, :])
                P.dma("sp", dK[b][s], fk, reads=[rKw], writes=[rKT[b][s]])

                def fv(h, s=s, b=b, hd=hd):
                    return h.dma_start(out=VV[b][:, s * KBC:(s + 1) * KBC, :],
                                       in_=Vw[s * TOK:(s + 1) * TOK, hd * 128:(hd + 1) * 128]
                                       .rearrange("(k p) d -> p k d", p=128))
                P.dma("sp", dV[b][s], fv, reads=[rVw], writes=[rVV[b][s]])
            P.dma("sp", dC[b], lambda h, b=b, hd=hd: h.dma_start(
                out=CQ[b], in_=Cd[hd, 3 * TOK + rq0:3 * TOK + rq0 + T].partition_broadcast(128)),
                reads=[rCd], writes=[rCQ[b]])

        loadh(0)
        gi = 0
        for hd in range(H):
            if hd + 1 < H:
                loadh(hd + 1)
            b = hd % 2
            po, pl = 2 + b, 4 + b
            nb = len(blocks)

            def qk(bi):
                kb = blocks[bi]
                sp = SB[bi % 4]
                P.mm(rPS[sp], [(PS[sp][:, 0:T], KT[b][:, kb * 128:(kb + 1) * 128], QT[:, hd, :])],
                     reads=[rKT[b][kb // KBC], rQT])
            qk(0)
            if nb > 1:
                qk(1)
            for bi, kb in enumerate(blocks):
                if bi + 2 < nb:
                    qk(bi + 2)
                sp = SB[bi % 4]
                tb = bi % 3
                pt = gi % NPT
                gi += 1
                P.op("dve", lambda h, sp=sp, tb=tb: h.scalar_tensor_tensor(
                    out=TMPt[tb][:, :], in0=PS[sp][:, 0:T], scalar=scale, in1=CQ[b], op0=ALU.mult, op1=ALU.add),
                    reads=[rPS[sp], rCQ[b]], writes=[rTMP[tb]])
                if kb >= 3 * KBC:
                    kl = kb - 3 * KBC
                    if kl * 128 + 127 > rq0:
                        def fsel(h, tb=tb, kl=kl):
                            if "neg" not in regcache:
                                regcache["neg"] = h.to_reg(NEG)
                            return h.affine_select(
                                out=TMPt[tb][:, :], in_=TMPt[tb][:, :], pattern=[[1, T]], compare_op=ALU.is_ge,
                                fill=regcache["neg"], base=rq0 - 128 * kl, channel_multiplier=-1)
                        P.op("pool", fsel, reads=[rTMP[tb]], writes=[rTMP[tb]])
                P.op("act", lambda h, tb=tb, pt=pt, kb=kb: h.activation(
                    out=PT[pt], in_=TMPt[tb][:, :], func=AF.Exp, bias=NEGCv[:, kb, hd:hd + 1], scale=1.0),
                    reads=[rTMP[tb], rNEGC], writes=[rPT[pt]])
                P.mm(rPS[po], [(PS[po][:, 0:T], VV[b][:, kb, :], PT[pt])], reads=[rVV[b][kb // KBC], rPT[pt]],
                     start=(bi == 0), stop=(bi == nb - 1))
                P.mm(rPS[pl], [(PS[pl][:, 0:T], ONESB[:, :], PT[pt])], reads=[rPT[pt], rCONST],
                     start=(bi == 0), stop=(bi == nb - 1))
            P.op("dve", lambda h, pl=pl: h.tensor_scalar_max(out=RSTDt[:, :], in0=PS[pl][:, 0:T], scalar1=1e-30),
                 reads=[rPS[pl]], writes=[rRSTD])
            P.op("dve", lambda h: h.reciprocal(out=RSTDt[:, :], in_=RSTDt[:, :]), reads=[rRSTD], writes=[rRSTD])
            P.op("dve", lambda h, po=po, hd=hd: h.tensor_tensor(out=HNv[:, hd, :], in0=PS[po][:, 0:T], in1=RSTDt[:, :],
                                                                op=ALU.mult), reads=[rPS[po], rRSTD], writes=[rHN])
        proj_resid(Wt["wo"], None)

    def final_out(i):
        P.barrier()
        c0, r0, n = real_cols(i)
        OUTB = ar_f32(0, NCH * T).rearrange("p (c t) -> p c t", t=T)
        rOUT = Res("outb")
        rmsnorm(O_FIN, to_out=(OUTB, rOUT))
        P.dma("sp", misc_sem(), lambda h: h.dma_start(
            out=outT.rearrange("(c p) t -> p c t", p=128)[:, :, r0:r0 + n], in_=OUTB[:, :, c0:c0 + n]),
            reads=[rOUT], writes=[Res("outdram")])

    stage_ctr = [0]

    def run(fn, *a):
        stage_ctr[0] += 1
        if STAGE_LIMIT is not None and stage_ctr[0] > STAGE_LIMIT:
            return
        fn(*a)

    stages1 = [(ffn, (0, 1, O_NF1), False), (conv_mixer, (), True), (ffn, (0, 2, O_NF2), False),
               (kv_stage, (), True), (ffn, (1, 1, O_NF1 + NCH), False)]
    for si, (fn, args, per_tile) in enumerate(stages1):
        for i in range(NT):
            run(P.barrier)
            if si == 0:
                run(load_x, i)
            else:
                run(reload_x, i)
            if per_tile:
                run(fn, i)
            else:
                run(fn, *args)
            run(spill_x, i)
    run(exchange_and_prep)
    for i in range(NT):
        run(P.barrier)
        run(reload_x, i)
        run(attention, i)
        run(ffn, 1, 2, O_NF2 + NCH)
        run(final_out, i)
    P.barrier()
    P.emit(None)
    print("kernel build: n_inst", P.n_inst, {e: len(v) for e, v in P.q.items()}, flush=True)
    return nc


def _chunk_layout(W, nchp):
    n = W.shape[1] // 128
    t = W.reshape(NCH, 128, n, 128).transpose(2, 1, 0, 3).reshape(n * 128, D)
    if nchp > n:
        t = np.concatenate([t, np.zeros(((nchp - n) * 128, D), np.float32)], axis=0)
    return np.ascontiguousarray(t)


def _col(v):
    return np.ascontiguousarray(np.asarray(v, np.float32).reshape(-1, 128).T)


_NC_CACHE = {}


def make_in_maps(x, norm_ffn1, ffn1_w1, ffn1_w3, ffn1_w2, norm_mix, norm_ffn2, ffn2_w1, ffn2_w3, ffn2_w2,
                 conv_pw1_w, conv_pw1_b, conv_dw_w, conv_dw_b, conv_ln_g, conv_ln_b, conv_pw2_w, conv_pw2_b,
                 kv_norm, w_kvf, b_f, attn_wq, attn_wo, final_norm):
    x = np.asarray(x, np.float32)
    full = {}
    for l in range(2):
        for f, (a, b, c) in ((1, (ffn1_w1, ffn1_w3, ffn1_w2)), (2, (ffn2_w1, ffn2_w3, ffn2_w2))):
            full[f"w1_{l}{f}"] = _chunk_layout(np.asarray(a[l], np.float32), NFP)
            full[f"w3_{l}{f}"] = _chunk_layout(np.asarray(b[l], np.float32), NFP)
            w2 = np.asarray(c[l], np.float32)
            full[f"w2_{l}{f}"] = np.concatenate([w2, np.zeros((NFP * 128 - FF, D), np.float32)], axis=0)
    full["pw1"] = _chunk_layout(np.asarray(conv_pw1_w[0], np.float32), NP1)
    full["pw2"] = _chunk_layout(np.asarray(conv_pw2_w[0], np.float32), NCP)
    wkvf = np.asarray(w_kvf, np.float32)
    full["wk"] = _chunk_layout(np.ascontiguousarray(wkvf[:, 0:D]), NCP)
    wv = wkvf[:, D:2 * D]
    cpg = NCH // NVC
    wvl = wv.reshape(NVC, cpg, 128, NVG, GV).transpose(3, 0, 2, 1, 4).reshape(NVG * NVC * 128, cpg * GV)
    if NVG < 8:
        wvl = np.concatenate([wvl, np.zeros(((8 - NVG) * NVC * 128, cpg * GV), np.float32)], axis=0)
    full["wv"] = np.ascontiguousarray(wvl)
    full["wq"] = _chunk_layout(np.asarray(attn_wq[0], np.float32), NCP)
    full["wo"] = _chunk_layout(np.asarray(attn_wo[0], np.float32), NCP)
    wf = np.ascontiguousarray(wkvf[:, 2 * D:2 * D + H].reshape(NCH, 128, H).transpose(1, 0, 2).reshape(128, NCH * H))

    small = np.zeros((128, NS), np.float32)
    n = NCH
    small[:, O_NF1:O_NF1 + n] = _col(norm_ffn1[0]); small[:, O_NF1 + n:O_NF1 + 2 * n] = _col(norm_ffn1[1])
    small[:, O_NMIX:O_NMIX + n] = _col(norm_mix[0]); small[:, O_NMIX + n:O_NMIX + 2 * n] = _col(norm_mix[1])
    small[:, O_NF2:O_NF2 + n] = _col(norm_ffn2[0]); small[:, O_NF2 + n:O_NF2 + 2 * n] = _col(norm_ffn2[1])
    small[:, O_KVN:O_KVN + n] = _col(kv_norm)
    small[:, O_FIN:O_FIN + n] = _col(final_norm)
    small[:, O_PW1B:O_PW1B + 2 * n] = _col(conv_pw1_b[0])
    small[:, O_DWB:O_DWB + n] = _col(conv_dw_b[0])
    small[:, O_LNG:O_LNG + n] = _col(conv_ln_g[0])
    small[:, O_LNB:O_LNB + n] = _col(conv_ln_b[0])
    small[:, O_PW2B:O_PW2B + n] = _col(conv_pw2_b[0])
    dww = np.asarray(conv_dw_w[0], np.float32)
    small[:, O_DWW:O_DWW + NCH * CW] = dww.T.reshape(NCH, 128, CW).transpose(1, 0, 2).reshape(128, NCH * CW)
    small[0:H, O_BF] = np.asarray(b_f, np.float32)

    in_maps = []
    for r in range(8):
        b, j = r // 4, r % 4
        st = j * TOK
        xs = np.zeros((TOKH, D), np.float32)
        if j > 0:
            xs[:] = x[b, st - HALO:st + TOK]
        else:
            xs[HALO:] = x[b, 0:TOK]
        sm = small.copy()
        sm[:, O_META] = 0.0 if j == 0 else 1.0
        for s in range(4):
            valid = (j + s - 3) >= 0
            sm[:, O_META + 1 + s] = 1.0 if valid else 0.0
            sm[:, O_META + 5 + s] = 0.0 if valid else NEG
        m = {"xin": np.ascontiguousarray(xs.T), "small": sm, "wf": wf}
        for name, arr in full.items():
            rows = arr.shape[0] // 8
            m[name] = arr[r * rows:(r + 1) * rows]
        in_maps.append(m)
    return in_maps


def assemble(results):
    out = np.empty((2, 4 * TOK, D), np.float32)
    for r in range(8):
        b, j = r // 4, r % 4
        out[b, j * TOK:(j + 1) * TOK, :] = np.asarray(results[r]["outT"], np.float32).T
    return out


def kernel(**inputs):
    set_cfg()
    in_maps = make_in_maps(**inputs)
    if "nc" not in _NC_CACHE:
        _NC_CACHE["nc"] = build_nc()
    res = run_bass_kernel_spmd(_NC_CACHE["nc"], in_maps, core_ids=list(range(8)))
    return assemble(res.results)
```

```python
import types
import numpy as np
import ml_dtypes
import concourse.bass as bass
import concourse.mybir as mybir
from concourse.bass_utils import run_bass_kernel_spmd

F32 = mybir.dt.float32
BF16 = mybir.dt.bfloat16
ALU = mybir.AluOpType
AF = mybir.ActivationFunctionType

D = FF = NCH = NF = NFP = H = T = NT = TOK = TOKH = NJMAX = GV = IG = KBC = NP1 = NCP = NVG = NVC = KR = 0
PARTS = []
O_NF1 = O_NMIX = O_NF2 = O_KVN = O_FIN = O_PW1B = O_DWB = O_LNG = O_LNB = O_PW2B = O_DWW = O_BF = O_META = NS = 0
CW = 31
HALO = 32
EPS = 1e-6
NEG = -30000.0


def _ceil8(n):
    return (n + 7) // 8 * 8


def set_cfg(d=4096, ff=11008, tok=2048, t=416, njmax=18):
    g = globals()
    nch = d // 128
    nf = ff // 128
    nloc = _ceil8(nf) // 8
    parts = [(r * nloc, min((r + 1) * nloc, nf)) for r in range(8) if r * nloc < nf]
    g.update(D=d, FF=ff, NCH=nch, NF=nf, NFP=_ceil8(nf), H=nch, T=t, TOK=tok, TOKH=tok + HALO,
             NT=(tok + HALO) // t, NJMAX=max(b - a for a, b in parts), PARTS=parts, GV=min(512, d),
             IG=min(4, nch), KBC=tok // 128, NP1=_ceil8(2 * nch), NCP=_ceil8(nch), NVG=d // min(512, d),
             NVC=min(4, nch), KR=min(256, d))
    assert (tok + HALO) % t == 0 and tok % 128 == 0 and ff % 128 == 0 and nch % 2 == 0
    g.update(O_NF1=0, O_NMIX=2 * nch, O_NF2=4 * nch, O_KVN=6 * nch, O_FIN=7 * nch, O_PW1B=8 * nch,
             O_DWB=10 * nch, O_LNG=11 * nch, O_LNB=12 * nch, O_PW2B=13 * nch, O_DWW=14 * nch,
             O_BF=45 * nch, O_META=45 * nch + 1, NS=45 * nch + 1 + 16)


set_cfg()

STAGE_LIMIT = None
ARENA_B = 110592


def _freeze(fn):
    if getattr(fn, "__closure__", None) is None:
        return fn
    cells = []
    for c in fn.__closure__:
        try:
            cells.append(types.CellType(c.cell_contents))
        except ValueError:
            cells.append(c)
    g = types.FunctionType(fn.__code__, fn.__globals__, fn.__name__, fn.__defaults__, tuple(cells))
    g.__kwdefaults__ = fn.__kwdefaults__
    return g


class Res:
    __slots__ = ("name", "w", "r")

    def __init__(self, name):
        self.name = name
        self.w = None
        self.r = {}


class DSem:
    def __init__(self, nc, name):
        self.sem = nc.alloc_semaphore(name)
        self.val = 0


class Prog:
    ENG = ("pe", "act", "dve", "pool", "sp")

    def __init__(self, nc):
        self.nc = nc
        self.q = {e: [] for e in self.ENG}
        self.sem = {e: nc.alloc_semaphore("prog_" + e) for e in ("pe", "act", "dve", "pool")}
        self.cnt = {e: 0 for e in self.sem}
        self.waited = {e: {} for e in self.ENG}
        self.semobj = {}
        self.dsems = []
        self.n_inst = 0

    def dsem(self, name):
        if name in self.semobj:
            return self.semobj[name]
        d = DSem(self.nc, name)
        self.semobj[name] = d
        self.dsems.append(d)
        return d

    def _wait(self, e, tok):
        if tok is None:
            return
        sem, val = tok
        if e == "pe" and sem is self.sem["pe"]:
            return
        key = id(sem)
        if self.waited[e].get(key, 0) >= val:
            return
        self.waited[e][key] = val
        self.q[e].append(lambda h, sem=sem, val=val: h.wait_ge(sem, val))

    def _deps(self, e, reads, writes):
        for r in reads:
            self._wait(e, r.w)
        for w in writes:
            self._wait(e, w.w)
            for sem, val in w.r.values():
                self._wait(e, (sem, val))

    def _mark(self, tok, reads, writes):
        k = id(tok[0])
        for r in reads:
            r.r[k] = tok
        for w in writes:
            w.w = tok
            w.r = {}

    def op(self, e, fn, reads=(), writes=()):
        fn = _freeze(fn)
        self._deps(e, reads, writes)
        self.cnt[e] += 1
        sem = self.sem[e]
        tok = (sem, self.cnt[e])
        self.q[e].append(lambda h, fn=fn, sem=sem: fn(h).then_inc(sem, 1))
        self._mark(tok, reads, writes)
        self.n_inst += 1

    def mm(self, ps, mms, reads, start=True, stop=True):
        self._deps("pe", reads, [ps])
        self.cnt["pe"] += 1
        sem = self.sem["pe"]
        tok = (sem, self.cnt["pe"])
        n = len(mms)

        def thunk(h, mms=mms, n=n, sem=sem, start=start, stop=stop):
            for k, (o, l, r) in enumerate(mms):
                ins = h.matmul(o, lhsT=l, rhs=r, start=(start and k == 0), stop=(stop and k == n - 1))
            ins.then_inc(sem, 1)
        self.q["pe"].append(thunk)
        self._mark(tok, reads, [ps])
        self.n_inst += n

    def tr(self, ps, out, in_, ident, reads):
        self._deps("pe", reads, [ps])
        self.cnt["pe"] += 1
        sem = self.sem["pe"]
        tok = (sem, self.cnt["pe"])
        self.q["pe"].append(lambda h, sem=sem: h.transpose(out, in_, ident).then_inc(sem, 1))
        self._mark(tok, reads, [ps])
        self.n_inst += 1

    def dma(self, e, dsem, fn, reads=(), writes=()):
        fn = _freeze(fn)
        self._wait(e, (dsem.sem, dsem.val))
        self._deps(e, reads, writes)
        dsem.val += 16
        tok = (dsem.sem, dsem.val)
        s = dsem.sem
        self.q[e].append(lambda h, fn=fn, s=s: fn(h).then_inc(s, 16))
        self._mark(tok, reads, writes)

    def cc(self, dsem, fn, reads=(), writes=()):
        fn = _freeze(fn)
        self._wait("pool", (dsem.sem, dsem.val))
        self._deps("pool", reads, writes)
        dsem.val += 1
        tok = (dsem.sem, dsem.val)
        s = dsem.sem
        self.q["pool"].append(lambda h, fn=fn, s=s: fn(h).then_inc(s, 1))
        self._mark(tok, reads, writes)

    def barrier(self):
        toks = [(self.sem[e], self.cnt[e]) for e in self.sem if self.cnt[e] > 0]
        toks += [(d.sem, d.val) for d in self.dsems if d.val > 0 and not getattr(d, "nobar", False)]
        for e in self.ENG:
            for t in toks:
                self._wait(e, t)

    def emit(self, final_tok):
        nc = self.nc
        q = self.q
        with nc.Block() as block:
            @block.tensor
            def _(h):
                for f in q["pe"]:
                    f(h)

            @block.scalar
            def _(h):
                for f in q["act"]:
                    f(h)

            @block.vector
            def _(h):
                for f in q["dve"]:
                    f(h)

            @block.gpsimd
            def _(h):
                for f in q["pool"]:
                    f(h)

            @block.sync
            def _(h):
                for f in q["sp"]:
                    f(h)


def build_nc(stop_after=None):
    nc = bass.Bass("TRN2", target_bir_lowering=False)
    P = Prog(nc)

    def din(name, shape, dt=F32):
        return nc.dram_tensor(name, list(shape), dt, kind="ExternalInput").ap()

    xin = din("xin", [D, TOKH])
    small_in = din("small", [128, NS])
    wf_in = din("wf", [128, NCH * H])
    wspec = []

    def wdecl(name, rows, cols):
        ap = din(name, [rows, cols])
        nloc = rows // 128
        wb = nc.dram_tensor(name + "_b", [rows, cols], BF16).ap()
        wg = nc.dram_tensor(name + "_g", [8 * rows, cols], BF16).ap()
        w4 = nc.dram_tensor(name + "_q", [4 * rows, cols], BF16).ap()
        r = {"name": name, "in": ap, "b": wb, "g": wg, "q": w4, "rb": Res(name + "_b"), "rg": [],
             "rows": rows, "cols": cols, "nloc": nloc}

        def off(j, nloc=nloc):
            rk, jl = j // nloc, j % nloc
            return ((jl * 4 + rk % 4) * 2 + rk // 4) * 128
        r["off"] = off
        wspec.append(r)
        return r

    Wt = {}
    for l in range(2):
        for f in (1, 2):
            Wt[f"w1_{l}{f}"] = wdecl(f"w1_{l}{f}", NFP // 8 * 128, D)
            Wt[f"w3_{l}{f}"] = wdecl(f"w3_{l}{f}", NFP // 8 * 128, D)
            Wt[f"w2_{l}{f}"] = wdecl(f"w2_{l}{f}", NFP // 8 * 128, D)
    Wt["pw1"] = wdecl("pw1", NP1 // 8 * 128, D)
    Wt["pw2"] = wdecl("pw2", NCP // 8 * 128, D)
    Wt["wk"] = wdecl("wk", NCP // 8 * 128, D)
    Wt["wv"] = wdecl("wv", NVC * 128, (NCH // NVC) * GV)
    Wt["wq"] = wdecl("wq", NCP // 8 * 128, D)
    Wt["wo"] = wdecl("wo", NCP // 8 * 128, D)
    worder = ["w1_01", "w3_01", "w2_01", "pw1", "pw2", "w1_02", "w3_02", "w2_02", "wk", "wv",
              "w1_11", "w3_11", "w2_11", "wq", "wo", "w1_12", "w3_12", "w2_12"]

    outT = nc.dram_tensor("outT", [D, TOK], F32, kind="ExternalOutput").ap()

    XS = nc.dram_tensor("xs", [NT, 128, NCH * T], F32).ap()
    Kp = nc.dram_tensor("kp", [D, TOK], BF16).ap()
    Vp = nc.dram_tensor("vp", [TOK, D], BF16).ap()
    LSp = nc.dram_tensor("lsp", [H, TOK], F32).ap()
    NKC = D // KR
    Kgp = nc.dram_tensor("kgp", [NKC * 7 * KR, TOK], BF16).ap()
    Vgp = nc.dram_tensor("vgp", [KBC * 7 * 128, D], BF16).ap()
    LSgp = nc.dram_tensor("lsgp", [7 * H, TOK], F32).ap()
    Cd = nc.dram_tensor("cd", [H, 4 * TOK], F32).ap()
    Kw = nc.dram_tensor("kw", [4 * D, TOK], BF16).ap()
    Vw = nc.dram_tensor("vw", [4 * TOK, D], BF16).ap()
    rKw, rVw = Res("kw"), Res("vw")
    rXS = [Res(f"xs{i}") for i in range(NT)]
    rKp, rVp, rLSp = Res("kp"), Res("vp"), Res("lsp")
    rKgp, rVgp, rLSgp, rCd = Res("kgp"), Res("vgp"), Res("lsgp"), Res("cd")
    rKpad, rVpad, rLSpad = Res("kpad"), Res("vpad"), Res("lspad")

    Xt = nc.alloc_sbuf_tensor("X", [128, NCH * T], F32)
    HNt = nc.alloc_sbuf_tensor("HN", [128, NCH * T], BF16)
    SM = nc.alloc_sbuf_tensor("SM", [128, NS], F32)
    ONES32 = nc.alloc_sbuf_tensor("ones32", [128, 128], F32)
    ONESB = nc.alloc_sbuf_tensor("onesb", [128, 128], BF16)
    ID32 = nc.alloc_sbuf_tensor("id32", [128, 128], F32)
    TMPt = [nc.alloc_sbuf_tensor(f"tmp{i}", [128, T], F32) for i in range(3)]
    RSTDt = nc.alloc_sbuf_tensor("rstd", [128, T], F32)
    CARRYt = nc.alloc_sbuf_tensor("carry", [128, NCH * 30], F32)
    FWt = nc.alloc_sbuf_tensor("fw", [128, NCH * H], BF16)
    rFW = Res("fw")
    AR = nc.alloc_sbuf_tensor("arena", [128, ARENA_B // 2], BF16)
    PS = [nc.alloc_psum_tensor(f"ps{i}", [128, 512], F32) for i in range(8)]
    rPS = [Res(f"ps{i}") for i in range(8)]

    Xv = Xt[:, :].rearrange("p (c t) -> p c t", t=T)
    HNv = HNt[:, :].rearrange("p (c t) -> p c t", t=T)
    CARRYv = CARRYt[:, :].rearrange("p (c t) -> p c t", t=30)
    rX, rHN, rSM, rRSTD, rCARRY, rCONST = Res("X"), Res("HN"), Res("SM"), Res("RSTD"), Res("CARRY"), Res("CONST")
    rTMP = [Res(f"tmp{i}") for i in range(3)]

    def ar_bf(off, n):
        return AR[:, off // 2: off // 2 + n]

    def ar_f32(off, n):
        return AR[:, off // 2: off // 2 + 2 * n].bitcast(F32)

    def smc(col, p0=0, p1=128):
        return SM[p0:p1, col:col + 1]

    class Slots:
        def __init__(self, base, n, halves, half_elems, tag):
            self.n = n
            self.ap = [[ar_bf(base + (s * halves + hh) * half_elems * 2, half_elems) for hh in range(halves)]
                       for s in range(n)]
            self.res = [[Res(f"{tag}{s}_{hh}") for hh in range(halves)] for s in range(n)]
            self.ds = [[P.dsem(f"d_{tag}{s}_{hh}") for hh in range(halves)] for s in range(n)]

    sem_misc = {"sp": [P.dsem(f"d_misc{i}") for i in range(6)], "pool": [P.dsem(f"d_miscp{i}") for i in range(3)]}
    misc_i = [0]

    def misc_sem(e="sp"):
        misc_i[0] += 1
        return sem_misc[e][misc_i[0] % len(sem_misc[e])]

    P.op("pool", lambda h: h.memset(ONES32[:, :], 1.0), writes=[rCONST])
    P.op("pool", lambda h: h.memset(ONESB[:, :], 1.0), writes=[rCONST])
    P.op("pool", lambda h: h.memset(ID32[:, :], 1.0), writes=[rCONST])
    P.op("pool", lambda h: h.affine_select(out=ID32[:, :], in_=ID32[:, :], pattern=[[1, 128]],
                                           compare_op=ALU.is_equal, fill=0.0, base=0, channel_multiplier=-1),
         reads=[rCONST], writes=[rCONST])
    P.dma("sp", misc_sem(), lambda h: h.dma_start(out=SM[:, :], in_=small_in[:, :]), writes=[rSM])
    P.dma("pool", misc_sem("pool"), lambda h: h.dma_start(out=FWt[:, :], in_=wf_in[:, :]), writes=[rFW])
    rAZ = Res("az")
    ZN = max(3 * KR * TOK // 128, 3 * D, 2 * TOK)
    ZB = ar_bf(0, ZN)
    P.op("pool", lambda h: h.memset(ZB, 0.0), writes=[rAZ])
    for k in range(NKC):
        P.dma("pool", misc_sem("pool"), lambda h, k=k: h.dma_start(
            out=Kgp[k * 7 * KR: (k * 7 + 3) * KR, :].rearrange("(p a) t -> p (a t)", p=128),
            in_=ar_bf(0, 3 * KR * TOK // 128)[:, :]), reads=[rAZ], writes=[rKpad])
    for k in range(KBC):
        P.dma("pool", misc_sem("pool"), lambda h, k=k: h.dma_start(
            out=Vgp[k * 7 * 128: (k * 7 + 3) * 128, :].rearrange("(p a) d -> p (a d)", p=128),
            in_=ar_bf(0, 3 * D)[:, :]), reads=[rAZ], writes=[rVpad])
    P.dma("pool", misc_sem("pool"), lambda h: h.dma_start(
        out=LSgp[0:3 * H, :], in_=ar_f32(0, TOK)[0:3 * H, :]), reads=[rAZ], writes=[rLSpad])

    ccsems = [P.dsem(f"d_cc{i}") for i in range(8)]
    for c_ in ccsems:
        c_.nobar = True
    cci = [0]

    def next_cc():
        cci[0] += 1
        return ccsems[cci[0] % len(ccsems)]

    def cc_group(kind_groups, in_ap, out_ap, reads, toks):
        ds_ = next_cc()
        r = Res("cc")
        P.cc(ds_, lambda h: h.collective_compute("AllGather", ALU.bypass, replica_groups=kind_groups,
                                                 ins=[in_ap], outs=[out_ap]), reads=reads, writes=[r])
        toks[id(ds_.sem)] = r.w
        return r

    def toks_to_res(toks):
        out = []
        for t in toks.values():
            r = Res("ccsum")
            r.w = t
            out.append(r)
        return out

    castsem = [P.dsem("d_cast0"), P.dsem("d_cast1")]
    for cs_ in castsem:
        cs_.nobar = True
    GRP4 = [[0, 1, 2, 3], [4, 5, 6, 7]]
    PAIRS = [[0, 4], [1, 5], [2, 6], [3, 7]]
    for wi, name in enumerate(worder):
        w = Wt[name]
        rows = w["rows"]
        nsp = 4 if rows >= 512 else 1
        rr = rows // nsp
        for k in range(nsp):
            P.dma("pool", castsem[(wi * 4 + k) % 2], lambda h, w=w, k=k, rr=rr: h.dma_start(
                out=w["b"][k * rr:(k + 1) * rr, :], in_=w["in"][k * rr:(k + 1) * rr, :]), writes=[w["rb"]])
        for cs in castsem:
            P._wait("pool", (cs.sem, cs.val))
        toks = {}
        for jl in range(w["nloc"]):
            rq = cc_group(GRP4, w["b"][jl * 128:(jl + 1) * 128, :], w["q"][jl * 512:(jl + 1) * 512, :], [w["rb"]], {})
            for r4 in range(4):
                cc_group(PAIRS, w["q"][(jl * 4 + r4) * 128:(jl * 4 + r4 + 1) * 128, :],
                         w["g"][(jl * 4 + r4) * 256:(jl * 4 + r4 + 1) * 256, :], [rq], toks)
        w["rg"] = toks_to_res(toks)

    def load_x(i):
        P.dma("sp", misc_sem(), lambda h: h.dma_start(
            out=Xv, in_=xin.rearrange("(c p) t -> p c t", p=128)[:, :, i * T:(i + 1) * T]), writes=[rX])

    def rmsnorm(gcol, to_out=None):
        for c in range(NCH):
            tb = c % 2
            P.op("act", lambda h, c=c, tb=tb: h.activation(out=TMPt[tb][:, :], in_=Xv[:, c, :], func=AF.Square),
                 reads=[rX], writes=[rTMP[tb]])
            P.mm(rPS[7], [(PS[7][:, 0:T], ONES32[:, :], TMPt[tb][:, :])], reads=[rTMP[tb], rCONST],
                 start=(c == 0), stop=(c == NCH - 1))
        P.op("act", lambda h: h.activation(out=TMPt[2][:, :], in_=PS[7][:, 0:T], func=AF.Sqrt, bias=EPS, scale=1.0 / D),
             reads=[rPS[7]], writes=[rTMP[2]])
        P.op("dve", lambda h: h.reciprocal(out=RSTDt[:, :], in_=TMPt[2][:, :]), reads=[rTMP[2]], writes=[rRSTD])
        for c in range(NCH):
            if to_out is None:
                P.op("dve", lambda h, c=c: h.scalar_tensor_tensor(
                    out=HNv[:, c, :], in0=Xv[:, c, :], scalar=smc(gcol + c), in1=RSTDt[:, :],
                    op0=ALU.mult, op1=ALU.mult), reads=[rX, rRSTD, rSM], writes=[rHN])
            else:
                ov, rov = to_out
                P.op("dve", lambda h, c=c, ov=ov: h.scalar_tensor_tensor(
                    out=ov[:, c, :], in0=Xv[:, c, :], scalar=smc(gcol + c), in1=RSTDt[:, :],
                    op0=ALU.mult, op1=ALU.mult), reads=[rX, rRSTD, rSM], writes=[rov])

    def stream(items, nslots, load, consume):
        n = len(items)
        depth = nslots - 1
        for k in range(min(depth, n)):
            load(k, items[k], k % nslots)
        for k in range(n):
            if k + depth < n:
                load(k + depth, items[k + depth], (k + depth) % nslots)
            consume(k, items[k], k % nslots)

    def ffn(l, f, gcol):
        P.barrier()
        WA = Slots(0, 3, 2, D, "wa")
        ACT_OFF = 12 * D
        ACTv = ar_bf(ACT_OFF, NJMAX * T).rearrange("p (j t) -> p j t", t=T)
        rACT = Res("actb")
        GW = IG * 128
        NIG = NCH // IG
        W2S = Slots(ACT_OFF + NJMAX * T * 2, 2, 1, NJMAX * GW, "w2s")
        w1, w3, w2 = Wt[f"w1_{l}{f}"], Wt[f"w3_{l}{f}"], Wt[f"w2_{l}{f}"]
        rmsnorm(gcol)

        def load13(k, j, s):
            P.dma("sp", WA.ds[s][0], lambda h: h.dma_start(out=WA.ap[s][0], in_=w1["g"][w1["off"](j):w1["off"](j) + 128, :]),
                  reads=w1["rg"], writes=[WA.res[s][0]])
            P.dma("sp", WA.ds[s][1], lambda h: h.dma_start(out=WA.ap[s][1], in_=w3["g"][w3["off"](j):w3["off"](j) + 128, :]),
                  reads=w3["rg"], writes=[WA.res[s][1]])

        w2items = [(q, ig) for q in range(len(PARTS)) for ig in range(NIG)]

        def load2(k, it, s):
            q, ig = it
            j0, j1 = PARTS[q]
            nj = j1 - j0
            dst = W2S.ap[s][0][:, 0:nj * GW].rearrange("p (j d) -> p j d", d=GW)
            rk = j0 // w2["nloc"]
            x = (rk % 4) * 2 + rk // 4
            src = w2["g"].rearrange("(jl x p) d -> x p jl d", x=8, p=128)[x, :, 0:nj, ig * GW:(ig + 1) * GW]
            P.dma("sp", W2S.ds[s][0], lambda h: h.dma_start(out=dst, in_=src), reads=w2["rg"], writes=[W2S.res[s][0]])

        st13 = {"k": 0}
        st2 = {"k": 0}
        n13 = NF
        for k in range(min(2, n13)):
            load13(k, k, k % 3)
        st13["k"] = min(2, n13)
        load2(0, w2items[0], 0)
        st2["k"] = 1
        jglob = 0
        for q, (j0, j1) in enumerate(PARTS):
            nj = j1 - j0
            for j in range(j0, j1):
                if st13["k"] < n13:
                    load13(st13["k"], st13["k"], st13["k"] % 3)
                    st13["k"] += 1
                s = j % 3
                pb = (j % 2) * 2
                wa, wb = WA.ap[s][0], WA.ap[s][1]
                P.mm(rPS[pb], [(PS[pb][:, 0:T], wa[:, c * 128:(c + 1) * 128], HNv[:, c, :]) for c in range(NCH)],
                     reads=[rHN, WA.res[s][0]])
                P.mm(rPS[pb + 1], [(PS[pb + 1][:, 0:T], wb[:, c * 128:(c + 1) * 128], HNv[:, c, :]) for c in range(NCH)],
                     reads=[rHN, WA.res[s][1]])
                tb = j % 2
                P.op("act", lambda h, pb=pb, tb=tb: h.activation(out=TMPt[tb][:, :], in_=PS[pb][:, 0:T], func=AF.Silu),
                     reads=[rPS[pb]], writes=[rTMP[tb]])
                P.op("dve", lambda h, pb=pb, tb=tb, jl=j - j0: h.tensor_tensor(
                    out=ACTv[:, jl, :], in0=TMPt[tb][:, :], in1=PS[pb + 1][:, 0:T], op=ALU.mult),
                    reads=[rTMP[tb], rPS[pb + 1]], writes=[rACT])
            for ig in range(NIG):
                k2 = q * NIG + ig
                if st2["k"] < len(w2items):
                    load2(st2["k"], w2items[st2["k"]], st2["k"] % 2)
                    st2["k"] += 1
                s2 = k2 % 2
                w2v = W2S.ap[s2][0][:, 0:nj * GW].rearrange("p (j d) -> p j d", d=GW)
                for ii in range(IG):
                    i = ig * IG + ii
                    pb = 4 + ii
                    P.mm(rPS[pb], [(PS[pb][:, 0:T], w2v[:, jl, ii * 128:(ii + 1) * 128], ACTv[:, jl, :]) for jl in range(nj)],
                         reads=[rACT, W2S.res[s2][0]])
                    P.op("dve", lambda h, pb=pb, i=i: h.scalar_tensor_tensor(
                        out=Xv[:, i, :], in0=PS[pb][:, 0:T], scalar=0.5, in1=Xv[:, i, :], op0=ALU.mult, op1=ALU.add),
                        reads=[rPS[pb], rX], writes=[rX])

    def proj_resid(w, bias_col):
        P.barrier()
        WA = Slots(NCH * T * 2, 3, 2, D, "wa")
        items = list(range(NCH // 2))

        def load(k, m, s):
            for hh in range(2):
                j = 2 * m + hh
                P.dma("sp", WA.ds[s][hh], lambda h, hh=hh, j=j: h.dma_start(
                    out=WA.ap[s][hh], in_=w["g"][w["off"](j):w["off"](j) + 128, :]), reads=w["rg"], writes=[WA.res[s][hh]])

        def consume(k, m, s):
            for hh in range(2):
                i = 2 * m + hh
                pb = i % 4
                wa = WA.ap[s][hh]
                P.mm(rPS[pb], [(PS[pb][:, 0:T], wa[:, c * 128:(c + 1) * 128], HNv[:, c, :]) for c in range(NCH)],
                     reads=[rHN, WA.res[s][hh]])
                if bias_col is None:
                    P.op("dve", lambda h, pb=pb, i=i: h.tensor_tensor(
                        out=Xv[:, i, :], in0=PS[pb][:, 0:T], in1=Xv[:, i, :], op=ALU.add),
                        reads=[rPS[pb], rX], writes=[rX])
                else:
                    P.op("dve", lambda h, pb=pb, i=i: h.scalar_tensor_tensor(
                        out=Xv[:, i, :], in0=PS[pb][:, 0:T], scalar=smc(bias_col + i), in1=Xv[:, i, :],
                        op0=ALU.add, op1=ALU.add), reads=[rPS[pb], rX, rSM], writes=[rX])
        stream(items, 3, load, consume)

    def conv_mixer(i):
        P.barrier()
        WA = Slots(0, 3, 2, D, "wa")
        UOFF = 12 * D
        UW = 30 + T
        Uv = ar_f32(UOFF, NCH * UW).rearrange("p (c t) -> p c t", t=UW)
        rU = [Res(f"u{c}") for c in range(NCH)]
        w = Wt["pw1"]
        rmsnorm(O_NMIX)
        if i == 0:
            P.op("dve", lambda h: h.memset(Uv[:, :, 0:30], 0.0), writes=rU)
        else:
            P.op("dve", lambda h: h.tensor_copy(out=Uv[:, :, 0:30], in_=CARRYv), reads=[rCARRY], writes=rU)

        def conv_chunk(c):
            acc = TMPt[2]
            if i == 0:
                P.op("dve", lambda h: h.tensor_scalar_mul(out=Uv[:, c, 30:30 + HALO], in0=Uv[:, c, 30:30 + HALO],
                                                          scalar1=smc(O_META)), reads=[rU[c], rSM], writes=[rU[c]])
            P.op("dve", lambda h: h.tensor_scalar_mul(
                out=acc[:, :], in0=Uv[:, c, 0:T], scalar1=smc(O_DWW + c * CW)), reads=[rU[c], rSM], writes=[rTMP[2]])
            for k in range(1, CW):
                P.op("dve", lambda h: h.scalar_tensor_tensor(
                    out=acc[:, :], in0=Uv[:, c, k:k + T], scalar=smc(O_DWW + c * CW + k), in1=acc[:, :],
                    op0=ALU.mult, op1=ALU.add), reads=[rU[c], rSM, rTMP[2]], writes=[rTMP[2]])
            P.op("dve", lambda h: h.tensor_scalar_add(
                out=Uv[:, c, 0:T], in0=acc[:, :], scalar1=smc(O_DWB + c)), reads=[rTMP[2], rSM], writes=[rU[c]])

        def load(k, m, s):
            for hh in range(2):
                j = m + NCH * hh
                P.dma("sp", WA.ds[s][hh], lambda h, hh=hh, j=j: h.dma_start(
                    out=WA.ap[s][hh], in_=w["g"][w["off"](j):w["off"](j) + 128, :]), reads=w["rg"], writes=[WA.res[s][hh]])

        def consume(k, m, s):
            pb = (m % 2) * 2
            tb = m % 2
            for hh in range(2):
                wa = WA.ap[s][hh]
                P.mm(rPS[pb + hh], [(PS[pb + hh][:, 0:T], wa[:, c * 128:(c + 1) * 128], HNv[:, c, :]) for c in range(NCH)],
                     reads=[rHN, WA.res[s][hh]])
            P.op("act", lambda h: h.activation(out=TMPt[tb][:, :], in_=PS[pb + 1][:, 0:T], func=AF.Sigmoid,
                                               bias=smc(O_PW1B + NCH + m), scale=1.0),
                 reads=[rPS[pb + 1], rSM], writes=[rTMP[tb]])
            P.op("dve", lambda h: h.scalar_tensor_tensor(
                out=Uv[:, m, 30:UW], in0=PS[pb][:, 0:T], scalar=smc(O_PW1B + m), in1=TMPt[tb][:, :],
                op0=ALU.add, op1=ALU.mult), reads=[rPS[pb], rTMP[tb], rSM], writes=[rU[m]])
            if m >= 1:
                conv_chunk(m - 1)
        stream(list(range(NCH)), 3, load, consume)
        conv_chunk(NCH - 1)
        P.op("dve", lambda h: h.tensor_copy(out=CARRYv, in_=Uv[:, :, T:UW]), reads=rU, writes=[rCARRY])
        for c in range(NCH):
            P.mm(rPS[6], [(PS[6][:, 0:T], ONES32[:, :], Uv[:, c, 0:T])], reads=[rU[c], rCONST],
                 start=(c == 0), stop=(c == NCH - 1))
            tb = c % 2
            P.op("act", lambda h, c=c, tb=tb: h.activation(out=TMPt[tb][:, :], in_=Uv[:, c, 0:T], func=AF.Square),
                 reads=[rU[c]], writes=[rTMP[tb]])
            P.mm(rPS[7], [(PS[7][:, 0:T], ONES32[:, :], TMPt[tb][:, :])], reads=[rTMP[tb], rCONST],
                 start=(c == 0), stop=(c == NCH - 1))
        MU = TMPt[2]
        rMU = rTMP[2]
        P.op("dve", lambda h: h.tensor_scalar_mul(out=MU[:, :], in0=PS[6][:, 0:T], scalar1=1.0 / D),
             reads=[rPS[6]], writes=[rMU])
        P.op("dve", lambda h: h.tensor_tensor(out=TMPt[0][:, :], in0=MU[:, :], in1=MU[:, :], op=ALU.mult),
             reads=[rMU], writes=[rTMP[0]])
        P.op("dve", lambda h: h.scalar_tensor_tensor(out=TMPt[0][:, :], in0=PS[7][:, 0:T], scalar=1.0 / D,
                                                     in1=TMPt[0][:, :], op0=ALU.mult, op1=ALU.subtract),
             reads=[rPS[7], rTMP[0]], writes=[rTMP[0]])
        P.op("act", lambda h: h.activation(out=TMPt[0][:, :], in_=TMPt[0][:, :], func=AF.Sqrt, bias=EPS, scale=1.0),
             reads=[rTMP[0]], writes=[rTMP[0]])
        P.op("dve", lambda h: h.reciprocal(out=RSTDt[:, :], in_=TMPt[0][:, :]), reads=[rTMP[0]], writes=[rRSTD])
        for c in range(NCH):
            tb = c % 2
            P.op("dve", lambda h, c=c, tb=tb: h.tensor_tensor(out=TMPt[tb][:, :], in0=Uv[:, c, 0:T], in1=MU[:, :],
                                                              op=ALU.subtract), reads=[rU[c], rMU], writes=[rTMP[tb]])
            P.op("dve", lambda h, tb=tb: h.tensor_tensor(out=TMPt[tb][:, :], in0=TMPt[tb][:, :], in1=RSTDt[:, :],
                                                         op=ALU.mult), reads=[rTMP[tb], rRSTD], writes=[rTMP[tb]])
            P.op("act", lambda h, c=c, tb=tb: h.activation(out=HNv[:, c, :], in_=TMPt[tb][:, :], func=AF.Silu,
                                                           bias=smc(O_LNB + c), scale=smc(O_LNG + c)),
                 reads=[rTMP[tb], rSM], writes=[rHN])
        proj_resid(Wt["pw2"], O_PW2B)

    def real_cols(i):
        c0 = HALO if i == 0 else 0
        r0 = i * T - HALO + c0
        return c0, r0, T - c0

    def kv_stage(i):
        P.barrier()
        c0, r0, n = real_cols(i)
        WA = Slots(0, 3, 2, D, "wa")
        KST = ar_bf(12 * D, NCH * T).rearrange("p (c t) -> p c t", t=T)
        rKST = Res("kst")
        FW = FWt[:, :].rearrange("p (c m) -> p c m", m=H)
        w = Wt["wk"]
        rmsnorm(O_KVN)

        def load(k, m, s):
            for hh in range(2):
                j = 2 * m + hh
                P.dma("sp", WA.ds[s][hh], lambda h, hh=hh, j=j: h.dma_start(
                    out=WA.ap[s][hh], in_=w["g"][w["off"](j):w["off"](j) + 128, :]), reads=w["rg"], writes=[WA.res[s][hh]])

        def consume(k, m, s):
            for hh in range(2):
                hd = 2 * m + hh
                pb = hd % 4
                wa = WA.ap[s][hh]
                P.mm(rPS[pb], [(PS[pb][:, 0:T], wa[:, c * 128:(c + 1) * 128], HNv[:, c, :]) for c in range(NCH)],
                     reads=[rHN, WA.res[s][hh]])
                e = "act" if hd % 2 == 0 else "dve"
                if e == "act":
                    P.op("act", lambda h, pb=pb, hd=hd: h.copy(out=KST[:, hd, :], in_=PS[pb][:, 0:T]),
                         reads=[rPS[pb]], writes=[rKST])
                else:
                    P.op("dve", lambda h, pb=pb, hd=hd: h.tensor_copy(out=KST[:, hd, :], in_=PS[pb][:, 0:T]),
                         reads=[rPS[pb]], writes=[rKST])
        stream(list(range(NCH // 2)), 3, load, consume)
        P.dma("sp", misc_sem(), lambda h: h.dma_start(
            out=Kp.rearrange("(c p) t -> p c t", p=128)[:, :, r0:r0 + n], in_=KST[:, :, c0:c0 + n]),
            reads=[rKST], writes=[rKp])
        P.mm(rPS[4], [(PS[4][0:H, 0:T], FW[:, c, :], HNv[:, c, :]) for c in range(NCH)], reads=[rHN, rFW])
        XF, A_, M_ = TMPt[0], TMPt[1], TMPt[2]
        P.op("dve", lambda h: h.tensor_scalar_add(out=XF[0:H, :], in0=PS[4][0:H, 0:T], scalar1=smc(O_BF, 0, H)),
             reads=[rPS[4], rSM], writes=[rTMP[0]])
        P.op("act", lambda h: h.activation(out=A_[0:H, :], in_=XF[0:H, :], func=AF.Abs),
             reads=[rTMP[0]], writes=[rTMP[1]])
        P.op("act", lambda h: h.activation(out=A_[0:H, :], in_=A_[0:H, :], func=AF.Exp, scale=-1.0),
             reads=[rTMP[1]], writes=[rTMP[1]])
        P.op("act", lambda h: h.activation(out=A_[0:H, :], in_=A_[0:H, :], func=AF.Ln, bias=1.0, scale=1.0),
             reads=[rTMP[1]], writes=[rTMP[1]])
        P.op("dve", lambda h: h.tensor_scalar_min(out=M_[0:H, :], in0=XF[0:H, :], scalar1=0.0),
             reads=[rTMP[0]], writes=[rTMP[2]])
        P.op("dve", lambda h: h.tensor_tensor(out=M_[0:H, :], in0=M_[0:H, :], in1=A_[0:H, :], op=ALU.subtract),
             reads=[rTMP[2], rTMP[1]], writes=[rTMP[2]])
        P.dma("sp", misc_sem(), lambda h: h.dma_start(out=LSp[:, r0:r0 + n], in_=M_[0:H, c0:c0 + n]),
              reads=[rTMP[2]], writes=[rLSp])
        P.barrier()
        NPC = (T + 127) // 128
        VST = ar_bf(0, NPC * D).rearrange("p (k d) -> p k d", d=D)
        rVST = Res("vst")
        WV = Slots(NPC * D * 2, 2, NVC, (NCH // NVC) * GV, "wv")
        wv = Wt["wv"]
        pieces = []
        a = c0
        while a < T:
            m = min(128, T - a)
            pieces.append((a, m))
            a += m

        def loadv(k, vg, s):
            cpg = NCH // NVC
            for cq in range(NVC):
                o_ = wv["off"](vg * NVC + cq)
                P.dma("sp", WV.ds[s][cq], lambda h, cq=cq, o_=o_: h.dma_start(
                    out=WV.ap[s][cq], in_=wv["g"][o_:o_ + 128, :]), reads=wv["rg"], writes=[WV.res[s][cq]])

        def consv(k, vg, s):
            cpg = NCH // NVC
            wvq = [WV.ap[s][cq].rearrange("p (c d) -> p c d", d=GV) for cq in range(NVC)]
            for pi, (a, m) in enumerate(pieces):
                pb = pi % 4
                P.mm(rPS[pb], [(PS[pb][0:m, 0:GV], HNv[:, c, a:a + m], wvq[c // cpg][:, c % cpg, :]) for c in range(NCH)],
                     reads=[rHN] + WV.res[s])
                if pi % 2 == 0:
                    P.op("act", lambda h, pb=pb, pi=pi, m=m: h.copy(out=VST[0:m, pi, vg * GV:(vg + 1) * GV],
                                                                    in_=PS[pb][0:m, 0:GV]), reads=[rPS[pb]], writes=[rVST])
                else:
                    P.op("dve", lambda h, pb=pb, pi=pi, m=m: h.tensor_copy(out=VST[0:m, pi, vg * GV:(vg + 1) * GV],
                                                                           in_=PS[pb][0:m, 0:GV]), reads=[rPS[pb]], writes=[rVST])
        stream(list(range(NVG)), 2, loadv, consv)
        for pi, (a, m) in enumerate(pieces):
            ra = r0 + (a - c0)
            P.dma("sp", misc_sem(), lambda h, pi=pi, m=m, ra=ra: h.dma_start(out=Vp[ra:ra + m, :], in_=VST[0:m, pi, :]),
                  reads=[rVST], writes=[rVp])

    def spill_x(i):
        P.dma("sp", misc_sem(), lambda h: h.dma_start(out=XS[i], in_=Xt[:, :]), reads=[rX], writes=[rXS[i]])

    def reload_x(i):
        P.dma("sp", misc_sem(), lambda h: h.dma_start(out=Xt[:, :], in_=XS[i]), reads=[rXS[i]], writes=[rX])

    NEGC_OFF = ARENA_B - 4 * KBC * H * 4
    regcache = {}

    def getj(h):
        key = ("j", id(h))
        if key not in regcache:
            regcache[key] = h.snap(h.partition_id() % 4, min_val=0, max_val=3)
        return regcache[key]
    NEGCv = ar_f32(NEGC_OFF, 4 * KBC * H).rearrange("p (k h) -> p k h", h=H)
    rNEGC = Res("negc")

    exch = {}

    def exchange_issue():
        tk, tv, tl = {}, {}, {}
        for k in range(NKC):
            cc_group(GRP4, Kp[k * KR:(k + 1) * KR, :], Kgp[(k * 7 + 3) * KR:(k * 7 + 7) * KR, :], [rKp, rKpad], tk)
        for k in range(KBC):
            cc_group(GRP4, Vp[k * 128:(k + 1) * 128, :], Vgp[(k * 7 + 3) * 128:(k * 7 + 7) * 128, :], [rVp, rVpad], tv)
        cc_group(GRP4, LSp[:, :], LSgp[3 * H:7 * H, :], [rLSp, rLSpad], tl)
        rKg_, rVg_, rLg_ = toks_to_res(tk), toks_to_res(tv), toks_to_res(tl)
        Kgx = Kgp.rearrange("(k x r) t -> x k r t", x=7, r=KR)
        Vgx = Vgp.rearrange("(k x r) d -> x k r d", x=7, r=128)
        for s in range(4):
            def fkw(h, s=s):
                j = getj(h)
                return h.dma_start(out=Kw[s * D:(s + 1) * D, :].rearrange("(k r) t -> k r t", r=KR),
                                   in_=Kgx[bass.ds(j + s, 1)][0])
            P.dma("pool", misc_sem("pool"), fkw, reads=rKg_, writes=[rKw])

            def fvw(h, s=s):
                j = getj(h)
                return h.dma_start(out=Vw[s * TOK:(s + 1) * TOK, :].rearrange("(k r) d -> k r d", r=128),
                                   in_=Vgx[bass.ds(j + s, 1)][0])
            P.dma("pool", misc_sem("pool"), fvw, reads=rVg_, writes=[rVw])
        exch["rLg"] = rLg_

    def exchange_and_prep():
        P.barrier()
        rLg_ = exch["rLg"]
        LSW = ar_f32(0, 4 * TOK)
        CREL = ar_f32(16 * TOK, 4 * TOK)
        ONE = ar_f32(32 * TOK, TOK)
        rLSW, rCREL, rONE = Res("lsw"), Res("crel"), Res("one")
        P.op("pool", lambda h: h.memset(ONE[0:H, :], 1.0), writes=[rONE])
        for s in range(4):
            def f(h, s=s):
                j = getj(h)
                return h.dma_start(out=LSW[0:H, s * TOK:(s + 1) * TOK], in_=LSgp[bass.ds((j + s) * H, H), :])
            P.dma("sp", misc_sem(), f, reads=rLg_, writes=[rLSW])
        for s in range(4):
            P.op("dve", lambda h, s=s: h.tensor_scalar_mul(out=LSW[0:H, s * TOK:(s + 1) * TOK],
                                                           in0=LSW[0:H, s * TOK:(s + 1) * TOK],
                                                           scalar1=smc(O_META + 1 + s, 0, H)),
                 reads=[rLSW, rSM], writes=[rLSW])
        for s in range(4):
            init = 0.0 if s == 0 else CREL[0:H, s * TOK - 1:s * TOK]
            P.op("dve", lambda h, s=s, init=init: h.tensor_tensor_scan(
                out=CREL[0:H, s * TOK:(s + 1) * TOK], data0=ONE[0:H, :], data1=LSW[0:H, s * TOK:(s + 1) * TOK],
                initial=init, op0=ALU.mult, op1=ALU.add), reads=[rLSW, rONE, rCREL], writes=[rCREL])
        P.dma("sp", misc_sem(), lambda h: h.dma_start(out=Cd[:, :], in_=CREL[0:H, :]), reads=[rCREL], writes=[rCd])
        for s in range(4):
            pb = s
            for kk in range(KBC):
                kb = s * KBC + kk
                P.tr(rPS[pb], PS[pb][:, kk * H:(kk + 1) * H], CREL[0:H, kb * 128:(kb + 1) * 128], ID32[0:H, 0:H],
                     reads=[rCREL, rCONST])
            P.op("dve", lambda h, s=s, pb=pb: h.tensor_scalar(
                out=NEGCv[:, s * KBC:(s + 1) * KBC, :], in0=PS[pb][:, 0:KBC * H].rearrange("p (k h) -> p k h", h=H),
                scalar1=-1.0, scalar2=smc(O_META + 5 + s), op0=ALU.mult, op1=ALU.add),
                reads=[rPS[pb], rSM], writes=[rNEGC])

    def attention(i):
        P.barrier()
        rmsnorm(O_NMIX + NCH)
        QT = ar_bf(0, NCH * T).rearrange("p (c t) -> p c t", t=T)
        rQT = Res("qt")
        WA = Slots(NCH * T * 2, 3, 2, D, "wa")
        w = Wt["wq"]

        def load(k, m, s):
            for hh in range(2):
                j = 2 * m + hh
                P.dma("sp", WA.ds[s][hh], lambda h, hh=hh, j=j: h.dma_start(
                    out=WA.ap[s][hh], in_=w["g"][w["off"](j):w["off"](j) + 128, :]), reads=w["rg"], writes=[WA.res[s][hh]])

        def consume(k, m, s):
            for hh in range(2):
                hd = 2 * m + hh
                pb = hd % 4
                wa = WA.ap[s][hh]
                P.mm(rPS[pb], [(PS[pb][:, 0:T], wa[:, c * 128:(c + 1) * 128], HNv[:, c, :]) for c in range(NCH)],
                     reads=[rHN, WA.res[s][hh]])
                if hd % 2 == 0:
                    P.op("act", lambda h, pb=pb, hd=hd: h.copy(out=QT[:, hd, :], in_=PS[pb][:, 0:T]),
                         reads=[rPS[pb]], writes=[rQT])
                else:
                    P.op("dve", lambda h, pb=pb, hd=hd: h.tensor_copy(out=QT[:, hd, :], in_=PS[pb][:, 0:T]),
                         reads=[rPS[pb]], writes=[rQT])
        stream(list(range(NCH // 2)), 3, load, consume)
        P.barrier()
        KOFF = NCH * T * 2
        KT = [ar_bf(KOFF + b * 8 * TOK, 4 * TOK) for b in range(2)]
        VV = [ar_bf(KOFF + 16 * TOK + b * 8 * TOK, 4 * TOK).rearrange("p (k d) -> p k d", d=128) for b in range(2)]
        CQ = [ar_f32(KOFF + 32 * TOK + b * 4 * T, T) for b in range(2)]
        NPT = 4
        SB = [0, 1, 6, 7]
        PT = [ar_bf(KOFF + 32 * TOK + 8 * T + b * 2 * T, T) for b in range(NPT)]
        assert KOFF + 32 * TOK + 8 * T + NPT * 2 * T <= NEGC_OFF
        rKT = [[Res(f"kt{b}{s}") for s in range(4)] for b in range(2)]
        rVV = [[Res(f"vv{b}{s}") for s in range(4)] for b in range(2)]
        rCQ = [Res(f"cq{b}") for b in range(2)]
        rPT = [Res(f"pt{b}") for b in range(NPT)]
        dK = [[P.dsem(f"d_kt_{b}{s}") for s in range(4)] for b in range(2)]
        dV = [[P.dsem(f"d_vv_{b}{s}") for s in range(4)] for b in range(2)]
        dC = [P.dsem(f"d_cq_{b}") for b in range(2)]
        c0, r0, n = real_cols(i)
        rq0 = i * T - HALO
        nown = (rq0 + T - 1) // 128 + 1
        blocks = list(range(3 * KBC)) + [3 * KBC + kk for kk in range(nown)]
        scale = 1.0 / float(np.sqrt(128.0))

        def loadh(hd):
            b = hd % 2
            for s in range(4):
                def fk(h, s=s, b=b, hd=hd):
                    return h.dma_start(out=KT[b][:, s * TOK:(s + 1) * TOK],
                                       in_=Kw[s * D + hd * 128:s * D + (hd + 1) * 128, :])
                P.dma("sp", dK[b][s], fk, reads=[rKw], writes=[rKT[b][s]])

                def fv(h, s=s, b=b, hd=hd):
                    return h.dma_start(out=VV[b][:, s * KBC:(s + 1) * KBC, :],
                                       in_=Vw[s * TOK:(s + 1) * TOK, hd * 128:(hd + 1) * 128]
                                       .rearrange("(k p) d -> p k d", p=128))
                P.dma("sp", dV[b][s], fv, reads=[rVw], writes=[rVV[b][s]])
            P.dma("sp", dC[b], lambda h, b=b, hd=hd: h.dma_start(
                out=CQ[b], in_=Cd[hd, 3 * TOK + rq0:3 * TOK + rq0 + T].partition_broadcast(128)),
                reads=[rCd], writes=[rCQ[b]])

        loadh(0)
        gi = 0
        for hd in range(H):
            if hd + 1 < H:
                loadh(hd + 1)
            b = hd % 2
            po, pl = 2 + b, 4 + b
            nb = len(blocks)

            def qk(bi):
                kb = blocks[bi]
                sp = SB[bi % 4]
                P.mm(rPS[sp], [(PS[sp][:, 0:T], KT[b][:, kb * 128:(kb + 1) * 128], QT[:, hd, :])],
                     reads=[rKT[b][kb // KBC], rQT])
            qk(0)
            if nb > 1:
                qk(1)
            for bi, kb in enumerate(blocks):
                if bi + 2 < nb:
                    qk(bi + 2)
                sp = SB[bi % 4]
                tb = bi % 3
                pt = gi % NPT
                gi += 1
                P.op("dve", lambda h, sp=sp, tb=tb: h.scalar_tensor_tensor(
                    out=TMPt[tb][:, :], in0=PS[sp][:, 0:T], scalar=scale, in1=CQ[b], op0=ALU.mult, op1=ALU.add),
                    reads=[rPS[sp], rCQ[b]], writes=[rTMP[tb]])
                if kb >= 3 * KBC:
                    kl = kb - 3 * KBC
                    if kl * 128 + 127 > rq0:
                        def fsel(h, tb=tb, kl=kl):
                            if "neg" not in regcache:
                                regcache["neg"] = h.to_reg(NEG)
                            return h.affine_select(
                                out=TMPt[tb][:, :], in_=TMPt[tb][:, :], pattern=[[1, T]], compare_op=ALU.is_ge,
                                fill=regcache["neg"], base=rq0 - 128 * kl, channel_multiplier=-1)
                        P.op("pool", fsel, reads=[rTMP[tb]], writes=[rTMP[tb]])
                P.op("act", lambda h, tb=tb, pt=pt, kb=kb: h.activation(
                    out=PT[pt], in_=TMPt[tb][:, :], func=AF.Exp, bias=NEGCv[:, kb, hd:hd + 1], scale=1.0),
                    reads=[rTMP[tb], rNEGC], writes=[rPT[pt]])
                P.mm(rPS[po], [(PS[po][:, 0:T], VV[b][:, kb, :], PT[pt])], reads=[rVV[b][kb // KBC], rPT[pt]],
                     start=(bi == 0), stop=(bi == nb - 1))
                P.mm(rPS[pl], [(PS[pl][:, 0:T], ONESB[:, :], PT[pt])], reads=[rPT[pt], rCONST],
                     start=(bi == 0), stop=(bi == nb - 1))
            P.op("dve", lambda h, pl=pl: h.tensor_scalar_max(out=RSTDt[:, :], in0=PS[pl][:, 0:T], scalar1=1e-30),
                 reads=[rPS[pl]], writes=[rRSTD])
            P.op("dve", lambda h: h.reciprocal(out=RSTDt[:, :], in_=RSTDt[:, :]), reads=[rRSTD], writes=[rRSTD])
            P.op("dve", lambda h, po=po, hd=hd: h.tensor_tensor(out=HNv[:, hd, :], in0=PS[po][:, 0:T], in1=RSTDt[:, :],
                                                                op=ALU.mult), reads=[rPS[po], rRSTD], writes=[rHN])
        proj_resid(Wt["wo"], None)

    def final_out(i):
        P.barrier()
        c0, r0, n = real_cols(i)
        OUTB = ar_f32(0, NCH * T).rearrange("p (c t) -> p c t", t=T)
        rOUT = Res("outb")
        rmsnorm(O_FIN, to_out=(OUTB, rOUT))
        P.dma("sp", misc_sem(), lambda h: h.dma_start(
            out=outT.rearrange("(c p) t -> p c t", p=128)[:, :, r0:r0 + n], in_=OUTB[:, :, c0:c0 + n]),
            reads=[rOUT], writes=[Res("outdram")])

    stage_ctr = [0]

    def run(fn, *a):
        stage_ctr[0] += 1
        if STAGE_LIMIT is not None and stage_ctr[0] > STAGE_LIMIT:
            return
        fn(*a)

    stages1 = [(ffn, (0, 1, O_NF1), False), (conv_mixer, (), True), (ffn, (0, 2, O_NF2), False),
               (kv_stage, (), True), (ffn, (1, 1, O_NF1 + NCH), False)]
    for si, (fn, args, per_tile) in enumerate(stages1):
        for i in range(NT):
            run(P.barrier)
            if si == 0:
                run(load_x, i)
            else:
                run(reload_x, i)
            if per_tile:
                run(fn, i)
            else:
                run(fn, *args)
            run(spill_x, i)
        if fn is kv_stage:
            run(exchange_issue)
    run(exchange_and_prep)
    for i in range(NT):
        run(P.barrier)
        run(reload_x, i)
        run(attention, i)
        run(ffn, 1, 2, O_NF2 + NCH)
        run(final_out, i)
    P.barrier()
    P.emit(None)
    print("kernel build: n_inst", P.n_inst, {e: len(v) for e, v in P.q.items()}, flush=True)
    return nc


def _chunk_layout(W, nchp):
    n = W.shape[1] // 128
    t = W.reshape(NCH, 128, n, 128).transpose(2, 1, 0, 3).reshape(n * 128, D)
    if nchp > n:
        t = np.concatenate([t, np.zeros(((nchp - n) * 128, D), np.float32)], axis=0)
    return np.ascontiguousarray(t)


def _col(v):
    return np.ascontiguousarray(np.asarray(v, np.float32).reshape(-1, 128).T)


_NC_CACHE = {}


def make_in_maps(x, norm_ffn1, ffn1_w1, ffn1_w3, ffn1_w2, norm_mix, norm_ffn2, ffn2_w1, ffn2_w3, ffn2_w2,
                 conv_pw1_w, conv_pw1_b, conv_dw_w, conv_dw_b, conv_ln_g, conv_ln_b, conv_pw2_w, conv_pw2_b,
                 kv_norm, w_kvf, b_f, attn_wq, attn_wo, final_norm):
    x = np.asarray(x, np.float32)
    full = {}
    for l in range(2):
        for f, (a, b, c) in ((1, (ffn1_w1, ffn1_w3, ffn1_w2)), (2, (ffn2_w1, ffn2_w3, ffn2_w2))):
            full[f"w1_{l}{f}"] = _chunk_layout(np.asarray(a[l], np.float32), NFP)
            full[f"w3_{l}{f}"] = _chunk_layout(np.asarray(b[l], np.float32), NFP)
            w2 = np.asarray(c[l], np.float32)
            full[f"w2_{l}{f}"] = np.concatenate([w2, np.zeros((NFP * 128 - FF, D), np.float32)], axis=0)
    full["pw1"] = _chunk_layout(np.asarray(conv_pw1_w[0], np.float32), NP1)
    full["pw2"] = _chunk_layout(np.asarray(conv_pw2_w[0], np.float32), NCP)
    wkvf = np.asarray(w_kvf, np.float32)
    full["wk"] = _chunk_layout(np.ascontiguousarray(wkvf[:, 0:D]), NCP)
    wv = wkvf[:, D:2 * D]
    cpg = NCH // NVC
    wvl = wv.reshape(NVC, cpg, 128, NVG, GV).transpose(3, 0, 2, 1, 4).reshape(NVG * NVC * 128, cpg * GV)
    if NVG < 8:
        wvl = np.concatenate([wvl, np.zeros(((8 - NVG) * NVC * 128, cpg * GV), np.float32)], axis=0)
    full["wv"] = np.ascontiguousarray(wvl)
    full["wq"] = _chunk_layout(np.asarray(attn_wq[0], np.float32), NCP)
    full["wo"] = _chunk_layout(np.asarray(attn_wo[0], np.float32), NCP)
    wf = np.ascontiguousarray(wkvf[:, 2 * D:2 * D + H].reshape(NCH, 128, H).transpose(1, 0, 2).reshape(128, NCH * H))

    small = np.zeros((128, NS), np.float32)
    n = NCH
    small[:, O_NF1:O_NF1 + n] = _col(norm_ffn1[0]); small[:, O_NF1 + n:O_NF1 + 2 * n] = _col(norm_ffn1[1])
    small[:, O_NMIX:O_NMIX + n] = _col(norm_mix[0]); small[:, O_NMIX + n:O_NMIX + 2 * n] = _col(norm_mix[1])
    small[:, O_NF2:O_NF2 + n] = _col(norm_ffn2[0]); small[:, O_NF2 + n:O_NF2 + 2 * n] = _col(norm_ffn2[1])
    small[:, O_KVN:O_KVN + n] = _col(kv_norm)
    small[:, O_FIN:O_FIN + n] = _col(final_norm)
    small[:, O_PW1B:O_PW1B + 2 * n] = _col(conv_pw1_b[0])
    small[:, O_DWB:O_DWB + n] = _col(conv_dw_b[0])
    small[:, O_LNG:O_LNG + n] = _col(conv_ln_g[0])
    small[:, O_LNB:O_LNB + n] = _col(conv_ln_b[0])
    small[:, O_PW2B:O_PW2B + n] = _col(conv_pw2_b[0])
    dww = np.asarray(conv_dw_w[0], np.float32)
    small[:, O_DWW:O_DWW + NCH * CW] = dww.T.reshape(NCH, 128, CW).transpose(1, 0, 2).reshape(128, NCH * CW)
    small[0:H, O_BF] = np.asarray(b_f, np.float32)

    in_maps = []
    for r in range(8):
        b, j = r // 4, r % 4
        st = j * TOK
        xs = np.zeros((TOKH, D), np.float32)
        if j > 0:
            xs[:] = x[b, st - HALO:st + TOK]
        else:
            xs[HALO:] = x[b, 0:TOK]
        sm = small.copy()
        sm[:, O_META] = 0.0 if j == 0 else 1.0
        for s in range(4):
            valid = (j + s - 3) >= 0
            sm[:, O_META + 1 + s] = 1.0 if valid else 0.0
            sm[:, O_META + 5 + s] = 0.0 if valid else NEG
        m = {"xin": np.ascontiguousarray(xs.T), "small": sm, "wf": wf}
        for name, arr in full.items():
            rows = arr.shape[0] // 8
            m[name] = arr[r * rows:(r + 1) * rows]
        in_maps.append(m)
    return in_maps


def assemble(results):
    out = np.empty((2, 4 * TOK, D), np.float32)
    for r in range(8):
        b, j = r // 4, r % 4
        out[b, j * TOK:(j + 1) * TOK, :] = np.asarray(results[r]["outT"], np.float32).T
    return out


def kernel(**inputs):
    set_cfg()
    in_maps = make_in_maps(**inputs)
    if "nc" not in _NC_CACHE:
        _NC_CACHE["nc"] = build_nc()
    res = run_bass_kernel_spmd(_NC_CACHE["nc"], in_maps, core_ids=list(range(8)))
    return assemble(res.results)
```
